# Optimizing a Trainium2 kernel written in Bass

```python
import jax
import jax.numpy as jnp
from jax import lax
import numpy as np

D_MODEL = 1024
BATCH = 8
SEQ = 8192
DEPTH = 2
DEC_BATCH = 8
DEC_SEQ = 64
PAST_LEN = 4096

CHUNK = 64
N_AB_LAYERS = (DEPTH + 1) // 2
N_C_LAYERS = DEPTH // 2
D_REC = D_MODEL // 2
REC_HEADS = 8
REC_BLOCK = D_REC // REC_HEADS
REC_CONV = 4
LRU_C = 8.0
D_CONV = D_MODEL // 2
CONV_WIDTH = 31
N_HEADS = 16
N_KV = 4
GROUP = N_HEADS // N_KV
HEAD_DIM = 64
WINDOW = 128
WIN_CHUNKS = WINDOW // CHUNK
D_FF = ((8 * D_MODEL // 3 + 255) // 256) * 256
ALPHA = (2 * DEPTH) ** 0.25
BETA = (8 * DEPTH) ** -0.25
LN_EPS = 1e-5
NEG_INF = -1e30
D_IN_AB = 2 * D_REC + 2 * D_CONV
D_Q = N_HEADS * HEAD_DIM
D_KV = N_KV * HEAD_DIM

kernel_name = 'hybrid_rglru_conformer_swa_step'


def layer_norm(x, g, b):
    xf = x.astype(jnp.float32)
    mu = xf.mean(-1, keepdims=True)
    var = jnp.mean(jnp.square(xf - mu), -1, keepdims=True)
    return ((xf - mu) * lax.rsqrt(var + LN_EPS) * g.astype(jnp.float32) + b.astype(jnp.float32)).astype(x.dtype)


def causal_dwconv(x, buf, w, b):
    k = w.shape[0]
    xp = jnp.concatenate([buf.astype(x.dtype), x], axis=1)
    y = lax.conv_general_dilated(xp, w[:, None, :].astype(x.dtype), window_strides=(1,), padding='VALID',
                                 dimension_numbers=('NWC', 'WIO', 'NWC'), feature_group_count=x.shape[-1])
    return y + b.astype(x.dtype), xp[:, xp.shape[1] - (k - 1):]


def rg_lru(x, h0, ga_w, ga_b, gx_w, gx_b, lam):
    bsz, t, _ = x.shape
    xf = x.astype(jnp.float32)
    xb = xf.reshape(bsz, t, REC_HEADS, REC_BLOCK)
    r = jax.nn.sigmoid(jnp.einsum('bthi,hij->bthj', xb, ga_w.astype(jnp.float32)).reshape(bsz, t, D_REC) + ga_b.astype(jnp.float32))
    i = jax.nn.sigmoid(jnp.einsum('bthi,hij->bthj', xb, gx_w.astype(jnp.float32)).reshape(bsz, t, D_REC) + gx_b.astype(jnp.float32))
    log_a = -LRU_C * r * jax.nn.softplus(-lam.astype(jnp.float32))
    a = jnp.exp(log_a)
    u = jnp.sqrt(-jnp.expm1(2.0 * log_a)) * (i * xf)
    u = u.at[:, 0].add(a[:, 0] * h0.astype(jnp.float32))

    def combine(left, right):
        a1, b1 = left
        a2, b2 = right
        return a1 * a2, a2 * b1 + b2

    _, h = lax.associative_scan(combine, (a, u), axis=1)
    return h.astype(x.dtype), h[:, -1].astype(h0.dtype)


def mixer_ab(x, rc_buf, h0, cf_buf, w_in, rc_w, rc_b, ga_w, ga_b, gx_w, gx_b, lam, cf_w, cf_b, cf_g, cf_beta, w_out):
    u = x @ w_in
    xr, yr, cv, cg = jnp.split(u, [D_REC, 2 * D_REC, 2 * D_REC + D_CONV], axis=-1)
    xc, new_rc = causal_dwconv(xr, rc_buf, rc_w, rc_b)
    h, h_last = rg_lru(xc, h0, ga_w, ga_b, gx_w, gx_b, lam)
    rec_out = h * jax.nn.gelu(yr)
    g = cv * jax.nn.sigmoid(cg)
    c, new_cf = causal_dwconv(g, cf_buf, cf_w, cf_b)
    c = jax.nn.silu(layer_norm(c, cf_g, cf_beta))
    out = jnp.concatenate([rec_out, c], axis=-1) @ w_out
    return out, new_rc, h_last, new_cf


def qkv_proj(x, w_qkv):
    bsz, t, _ = x.shape
    u = x @ w_qkv
    q, k, v = jnp.split(u, [D_Q, D_Q + D_KV], axis=-1)
    q = q.reshape(bsz, t, N_KV, GROUP, HEAD_DIM) * (HEAD_DIM ** -0.5)
    return q, k.reshape(bsz, t, N_KV, HEAD_DIM), v.reshape(bsz, t, N_KV, HEAD_DIM)


def alibi_bias(dist):
    slopes = jnp.exp2(-8.0 * jnp.arange(1, N_HEADS + 1, dtype=jnp.float32) / N_HEADS).reshape(N_KV, GROUP)
    return -slopes[:, :, None, None] * jnp.abs(dist).astype(jnp.float32)[None, None]


def sink_attention(q, k, v, bias, valid, sinks):
    s = jnp.einsum('...qkgd,...skd->...kgqs', q, k, preferred_element_type=jnp.float32)
    s = jnp.where(valid[..., None, None, :, :], s + bias, NEG_INF)
    sink = sinks.astype(jnp.float32).reshape(N_KV, GROUP)[:, :, None]
    m = jnp.maximum(s.max(-1), sink)
    p = jnp.exp(s - m[..., None])
    denom = p.sum(-1) + jnp.exp(sink - m)
    o = jnp.einsum('...kgqs,...skd->...qkgd', p, v.astype(jnp.float32))
    o = o / jnp.moveaxis(denom, -1, -3)[..., None]
    return o.astype(q.dtype)


def mixer_c_prompt(x, w_qkv, sinks, w_out):
    bsz, t, _ = x.shape
    nc = t // CHUNK
    q, k, v = qkv_proj(x, w_qkv)
    qc = q.reshape(bsz, nc, CHUNK, N_KV, GROUP, HEAD_DIM)

    def windows(z):
        zc = z.reshape(bsz, nc, CHUNK, N_KV, HEAD_DIM)
        zp = jnp.pad(zc, ((0, 0), (WIN_CHUNKS, 0), (0, 0), (0, 0), (0, 0)))
        return jnp.concatenate([zp[:, j:j + nc] for j in range(WIN_CHUNKS + 1)], axis=2)

    kw, vw = windows(k), windows(v)
    koff = jnp.arange((WIN_CHUNKS + 1) * CHUNK)
    dist = WIN_CHUNKS * CHUNK + jnp.arange(CHUNK)[:, None] - koff[None, :]
    bias = alibi_bias(dist)
    key_chunk = jnp.arange(nc)[:, None] - WIN_CHUNKS + (koff // CHUNK)[None, :]
    valid = (key_chunk >= 0)[:, None, :]
    o = lax.map(lambda a: sink_attention(a[0], a[1], a[2], bias, valid, sinks), (qc, kw, vw))
    out = o.reshape(bsz, t, D_Q) @ w_out
    return out, k[:, t - WINDOW:], v[:, t - WINDOW:]


def mixer_c_sample(x, ck, cv, w_qkv, sinks, w_out):
    bsz, t, _ = x.shape
    rows = ck.shape[1]
    q, k, v = qkv_proj(x, w_qkv)
    kf = jnp.concatenate([ck.astype(k.dtype), k], axis=1)
    vf = jnp.concatenate([cv.astype(v.dtype), v], axis=1)
    q_pos = PAST_LEN + jnp.arange(t)
    k_pos = jnp.concatenate([PAST_LEN - rows + jnp.arange(rows), q_pos])
    bias = alibi_bias(q_pos[:, None] - k_pos[None, :])
    qch, kch = q_pos[:, None] // CHUNK, k_pos[None, :] // CHUNK
    valid = (kch <= qch) & (kch >= qch - WIN_CHUNKS)
    o = sink_attention(q, kf, vf, bias, valid, sinks)
    out = o.reshape(bsz, t, D_Q) @ w_out
    return out, kf[:, kf.shape[1] - rows:], vf[:, vf.shape[1] - rows:]


def swiglu(x, wg, wu, wd):
    return (jax.nn.silu(x @ wg) * (x @ wu)) @ wd


def setup_inputs(seed: int = 0) -> dict:
    key = jax.random.key(seed)
    ks = iter(jax.random.split(key, 40))

    def nrm(shape, scale):
        return scale * jax.random.normal(next(ks), shape, jnp.float32)

    win_rows = min(WINDOW, PAST_LEN)
    ua = jax.random.uniform(next(ks), (N_AB_LAYERS, D_REC), jnp.float32, 0.9, 0.999)
    sig = ua ** (1.0 / LRU_C)
    rec_lambda = jnp.log(sig) - jnp.log1p(-sig)
    w_qk = nrm((N_C_LAYERS, D_MODEL, D_Q + D_KV), D_MODEL ** -0.5)
    w_v = nrm((N_C_LAYERS, D_MODEL, D_KV), BETA * D_MODEL ** -0.5)
    return {
        'x_prompt': nrm((BATCH, SEQ, D_MODEL), 1.0),
        'x_sample': nrm((DEC_BATCH, DEC_SEQ, D_MODEL), 1.0),
        'state_rec_h': nrm((N_AB_LAYERS, DEC_BATCH, D_REC), 0.5),
        'state_rec_conv': nrm((N_AB_LAYERS, DEC_BATCH, REC_CONV - 1, D_REC), 1.0),
        'state_cf_conv': nrm((N_AB_LAYERS, DEC_BATCH, CONV_WIDTH - 1, D_CONV), 1.0),
        'cache_k': nrm((N_C_LAYERS, DEC_BATCH, win_rows, N_KV, HEAD_DIM), 1.0),
        'cache_v': nrm((N_C_LAYERS, DEC_BATCH, win_rows, N_KV, HEAD_DIM), BETA),
        'w_in_ab': nrm((N_AB_LAYERS, D_MODEL, D_IN_AB), D_MODEL ** -0.5),
        'rec_conv_w': nrm((N_AB_LAYERS, REC_CONV, D_REC), REC_CONV ** -0.5),
        'rec_conv_b': nrm((N_AB_LAYERS, D_REC), 0.02),
        'rec_gate_a_w': nrm((N_AB_LAYERS, REC_HEADS, REC_BLOCK, REC_BLOCK), REC_BLOCK ** -0.5),
        'rec_gate_a_b': nrm((N_AB_LAYERS, D_REC), 0.02),
        'rec_gate_x_w': nrm((N_AB_LAYERS, REC_HEADS, REC_BLOCK, REC_BLOCK), REC_BLOCK ** -0.5),
        'rec_gate_x_b': nrm((N_AB_LAYERS, D_REC), 0.02),
        'rec_lambda': rec_lambda,
        'cf_conv_w': nrm((N_AB_LAYERS, CONV_WIDTH, D_CONV), CONV_WIDTH ** -0.5),
        'cf_conv_b': nrm((N_AB_LAYERS, D_CONV), 0.02),
        'cf_norm_g': 1.0 + nrm((N_AB_LAYERS, D_CONV), 0.02),
        'cf_norm_b': nrm((N_AB_LAYERS, D_CONV), 0.02),
        'w_out_ab': nrm((N_AB_LAYERS, D_REC + D_CONV, D_MODEL), BETA * (D_REC + D_CONV) ** -0.5),
        'w_qkv': jnp.concatenate([w_qk, w_v], axis=-1),
        'attn_sinks': nrm((N_C_LAYERS, N_HEADS), 0.5),
        'w_out_c': nrm((N_C_LAYERS, D_Q, D_MODEL), BETA * D_Q ** -0.5),
        'ln_mix_g': 1.0 + nrm((DEPTH, D_MODEL), 0.02),
        'ln_mix_b': nrm((DEPTH, D_MODEL), 0.02),
        'w_ff_gate': nrm((DEPTH, D_MODEL, D_FF), BETA * D_MODEL ** -0.5),
        'w_ff_up': nrm((DEPTH, D_MODEL, D_FF), BETA * D_MODEL ** -0.5),
        'w_ff_down': nrm((DEPTH, D_FF, D_MODEL), BETA * D_FF ** -0.5),
        'ln_ff_g': 1.0 + nrm((DEPTH, D_MODEL), 0.02),
        'ln_ff_b': nrm((DEPTH, D_MODEL), 0.02),
    }


def reference(x_prompt, x_sample, state_rec_h, state_rec_conv, state_cf_conv, cache_k, cache_v,
              w_in_ab, rec_conv_w, rec_conv_b, rec_gate_a_w, rec_gate_a_b, rec_gate_x_w, rec_gate_x_b,
              rec_lambda, cf_conv_w, cf_conv_b, cf_norm_g, cf_norm_b, w_out_ab,
              w_qkv, attn_sinks, w_out_c,
              ln_mix_g, ln_mix_b, w_ff_gate, w_ff_up, w_ff_down, ln_ff_g, ln_ff_b):
    yp, ys = x_prompt, x_sample
    p_h, s_h, p_rc, s_rc, p_cf, s_cf, p_k, s_k, p_v, s_v = ([] for _ in range(10))
    for layer in range(DEPTH):
        if layer % 2 == 0:
            j = layer // 2
            prm = (w_in_ab[j], rec_conv_w[j], rec_conv_b[j], rec_gate_a_w[j], rec_gate_a_b[j],
                   rec_gate_x_w[j], rec_gate_x_b[j], rec_lambda[j], cf_conv_w[j], cf_conv_b[j],
                   cf_norm_g[j], cf_norm_b[j], w_out_ab[j])
            nb = yp.shape[0]
            mp, rc, hl, cf = mixer_ab(yp, jnp.zeros((nb, REC_CONV - 1, D_REC), yp.dtype),
                                      jnp.zeros((nb, D_REC), yp.dtype),
                                      jnp.zeros((nb, CONV_WIDTH - 1, D_CONV), yp.dtype), *prm)
            p_rc.append(rc)
            p_h.append(hl)
            p_cf.append(cf)
            ms, rc, hl, cf = mixer_ab(ys, state_rec_conv[j], state_rec_h[j], state_cf_conv[j], *prm)
            s_rc.append(rc)
            s_h.append(hl)
            s_cf.append(cf)
        else:
            j = layer // 2
            mp, kk, vv = mixer_c_prompt(yp, w_qkv[j], attn_sinks[j], w_out_c[j])
            p_k.append(kk)
            p_v.append(vv)
            ms, kk, vv = mixer_c_sample(ys, cache_k[j], cache_v[j], w_qkv[j], attn_sinks[j], w_out_c[j])
            s_k.append(kk)
            s_v.append(vv)
        yp = layer_norm(ALPHA * yp + mp, ln_mix_g[layer], ln_mix_b[layer])
        ys = layer_norm(ALPHA * ys + ms, ln_mix_g[layer], ln_mix_b[layer])
        yp = layer_norm(ALPHA * yp + swiglu(yp, w_ff_gate[layer], w_ff_up[layer], w_ff_down[layer]), ln_ff_g[layer], ln_ff_b[layer])
        ys = layer_norm(ALPHA * ys + swiglu(ys, w_ff_gate[layer], w_ff_up[layer], w_ff_down[layer]), ln_ff_g[layer], ln_ff_b[layer])
    return (yp, ys, jnp.stack(p_h), jnp.stack(s_h), jnp.stack(p_rc), jnp.stack(s_rc),
            jnp.stack(p_cf), jnp.stack(s_cf), jnp.stack(p_k), jnp.stack(s_k), jnp.stack(p_v), jnp.stack(s_v))
```

```python
import contextlib
import numpy as np
import concourse.bass as bass
import concourse.mybir as mybir
from concourse.bass_utils import run_bass_kernel_spmd

F32 = mybir.dt.float32
BF16 = mybir.dt.bfloat16
AF = mybir.ActivationFunctionType
ALU = mybir.AluOpType
AX = mybir.AxisListType

D = 1024
SEQ = 8192
NCORE = 8
DEC = 64
D_FF = 2816
NJ = D_FF // 128
ALPHA = 4.0 ** 0.25
LN_EPS = 1e-5
NHEAD = 16
SLOPES = [2.0 ** (-8.0 * (h + 1) / NHEAD) for h in range(NHEAD)]
RING = 8
PIECE = 4096
SB_BASE = 16512

PC_WIN = 0
PC_CONV = 4
PC_WOUT = 8
PC_GU0 = 10
PC_WD0 = 21
PC_Q = 27
PC_KDUP = 29
PC_KV = 30
PC_WOC = 31
PC_GU1 = 33
PC_WD1 = 44
NPIECE = 50

C_RCW, C_RCB, C_GAB, C_GXB, C_LAM, C_CFB, C_CFG, C_CFBE, C_LNG, C_LNB = 0, 16, 20, 24, 28, 32, 36, 40, 44, 76
NCOL = 108


class Op:
    __slots__ = ("eng", "fn", "waits", "semkey", "seq", "needs_inc", "value", "is_dma")


class Prog:
    ENGS = ["pe", "act", "dve", "pool", "sp"]

    def __init__(self, nc):
        self.nc = nc
        self.ops = {e: [] for e in self.ENGS}
        self.recs = {}
        self.waited = {e: {} for e in self.ENGS}
        self.semseq = {}
        self.lastdma = {}
        self.sbuf_addr = {}
        self.pending = {}

    def region(self, ap):
        t = ap.tensor
        name = t.name
        pat = [(int(s), int(n)) for s, n in ap.ap]
        off = int(ap.offset)
        esz = mybir.dt.size(ap.dtype)
        cls = type(t).__name__
        if cls.startswith("DRam"):
            lo = off
            hi = off + sum((n - 1) * abs(s) for s, n in pat) + 1
            return ("d:" + name, 0, 1, lo * esz, hi * esz)
        pstep = pat[0][0]
        p0 = off // pstep if pstep else 0
        fo = off - p0 * pstep
        p1 = p0 + pat[0][1]
        ext = sum((n - 1) * abs(s) for s, n in pat[1:]) + 1
        if cls.startswith("PSum"):
            return ("p:" + name, 0, 128, 0, 2048)
        base = self.sbuf_addr[name]
        return ("sb", p0, p1, base + fo * esz, base + (fo + ext) * esz)

    def _psum_guard(self, op, reads, writes, start):
        for kind, aps in (("r", reads), ("w", writes)):
            for ap in aps:
                if not type(ap.tensor).__name__.startswith("PSum"):
                    continue
                pat = [(int(s_), int(n)) for s_, n in ap.ap]
                off = int(ap.offset)
                pstep = pat[0][0]
                fo = off - (off // pstep) * pstep if pstep else 0
                ext = sum((n - 1) * abs(s_) for s_, n in pat[1:]) + 1
                esz = mybir.dt.size(ap.dtype)
                lo, hi = fo * esz, (fo + ext) * esz
                pend = self.pending.setdefault(ap.tensor.name, [])
                if kind == "r" and op.eng != "pe":
                    pend[:] = [iv for iv in pend if not (iv[0] < hi and lo < iv[1])]
                elif kind == "w" and op.eng == "pe":
                    if start:
                        for iv in pend:
                            assert not (iv[0] < hi and lo < iv[1]), ("PSUM reuse before consumption", ap.tensor.name, lo, hi, iv)
                        pend.append((lo, hi))

    def _deps(self, op, reads, writes):
        deps = []
        for kind, aps in (("w", writes), ("r", reads)):
            for ap in aps:
                key, p0, p1, lo, hi = self.region(ap)
                lst = self.recs.setdefault(key, [])
                keep = []
                for r in lst:
                    rp0, rp1, rlo, rhi, rkind, rop = r
                    if rp0 < p1 and p0 < rp1 and rlo < hi and lo < rhi:
                        if kind == "r":
                            if rkind == "w":
                                deps.append((rop, "raw"))
                            elif key[0] == "p" and rop.eng != op.eng:
                                deps.append((rop, "rar"))
                            keep.append(r)
                        else:
                            deps.append((rop, "waw" if rkind == "w" else "war"))
                            if p0 <= rp0 and rp1 <= p1 and lo <= rlo and rhi <= hi:
                                continue
                            keep.append(r)
                    else:
                        keep.append(r)
                if kind == "r":
                    keep = [r for r in keep if not (r[4] == "r" and r[5].semkey == op.semkey and not r[5].is_dma
                                                    and not op.is_dma and p0 <= r[0] and r[1] <= p1 and lo <= r[2] and r[3] <= hi)]
                keep.append((p0, p1, lo, hi, kind, op))
                self.recs[key] = keep
        return deps

    def _add_waits(self, op, deps):
        w = self.waited[op.eng]
        for rop, kind in deps:
            if rop is op:
                continue
            if not rop.is_dma and not op.is_dma and rop.eng == op.eng:
                if op.eng == "pe":
                    continue
            if w.get(rop.semkey, -1) >= rop.seq:
                continue
            w[rop.semkey] = rop.seq
            rop.needs_inc = True
            op.waits.append(rop)

    def op(self, eng, fn, reads=(), writes=(), start=True):
        o = Op()
        o.eng, o.fn, o.waits, o.semkey, o.is_dma = eng, fn, [], eng, False
        o.seq = len(self.ops[eng])
        o.needs_inc, o.value = False, None
        self._psum_guard(o, reads, writes, start)
        self._add_waits(o, self._deps(o, reads, writes))
        self.ops[eng].append(o)
        return o

    def dma(self, q, sem, fn, reads=(), writes=()):
        o = Op()
        o.eng, o.fn, o.waits, o.semkey, o.is_dma = q, fn, [], "dma:" + sem, True
        o.seq = self.semseq.get(sem, 0)
        self.semseq[sem] = o.seq + 1
        o.needs_inc, o.value = True, 16 * (o.seq + 1)
        deps = self._deps(o, reads, writes)
        prev = self.lastdma.get(sem)
        if prev is not None:
            deps.append((prev, "raw"))
        self.lastdma[sem] = o
        self._add_waits(o, deps)
        self.ops[q].append(o)
        return o

    def wait_all(self, eng, oplist):
        o = Op()
        o.eng, o.fn, o.waits, o.semkey, o.is_dma = eng, None, [], eng, False
        o.seq = len(self.ops[eng])
        o.needs_inc, o.value = False, None
        self._add_waits(o, [(x, "raw") for x in oplist])
        self.ops[eng].append(o)

    def emit(self):
        nc = self.nc
        for e in self.ENGS:
            cnt = 0
            for o in self.ops[e]:
                if o.is_dma:
                    continue
                if o.needs_inc:
                    cnt += 1
                    o.value = cnt
        with contextlib.ExitStack() as es:
            sems = {}
            for e in self.ENGS:
                sems[e] = es.enter_context(nc.semaphore("s_" + e))
            for s in self.semseq:
                sems["dma:" + s] = es.enter_context(nc.semaphore("d_" + s))
            block = es.enter_context(nc.Block())

            def run(ename):
                def body(eng):
                    for o in self.ops[ename]:
                        for w in o.waits:
                            eng.wait_ge(sems[w.semkey], w.value)
                        if o.fn is None:
                            continue
                        ins = o.fn(eng)
                        if o.is_dma:
                            ins.then_inc(sems[o.semkey], 16)
                        elif o.needs_inc:
                            ins.then_inc(sems[o.semkey], 1)
                return body

            block.tensor(run("pe"))
            block.scalar(run("act"))
            block.vector(run("dve"))
            block.gpsimd(run("pool"))
            block.sync(run("sp"))


def _build(n_tiles, with_sample, seq, order):
    nc = bass.Bass("TRN2", target_bir_lowering=False)
    P = Prog(nc)

    def din(name, shape, dt=F32):
        return nc.dram_tensor(name, list(shape), dt, kind="ExternalInput").ap()

    def dout(name, shape, dt=F32):
        return nc.dram_tensor(name, list(shape), dt, kind="ExternalOutput").ap()

    x_d = din("x", [seq, D])
    xs_d = din("xs", [DEC, D])
    tape32 = din("tape32", [NPIECE, 128, PIECE])
    ident_d = din("ident", [128, 128])
    ones_d = din("ones", [128, 128])
    dist_d = din("dist", [128, 256])
    pcol_d = din("pcol", [128, NCOL])
    lntab_d = din("lntab", [4, 128, 2 * D])
    gates_d = din("gates", [128, 8 * 128])
    sinkb_d = din("sinkb", [128, NHEAD])
    st_rc_d = din("st_rc", [128, 12])
    st_cf_d = din("st_cf", [128, 120])
    st_h_d = din("st_h", [128, 4])
    st_kT_d = din("st_kT", [128, 512])
    st_v_d = din("st_v", [128, 256])
    ck_d = din("ck", [128, 256])
    cv_d = din("cv", [128, 256])

    y_d = dout("y", [seq, D])
    ys_d = dout("ys", [DEC, D])
    oh_d = [dout("o_h_p", [4, 128]), dout("o_h_s", [4, 128])]
    orc_d = [dout("o_rc_p", [3, 512]), dout("o_rc_s", [3, 512])]
    ocf_d = [dout("o_cf_p", [30, 512]), dout("o_cf_s", [30, 512])]
    ok_d = [dout("o_k_p", [128, 256]), dout("o_k_s", [128, 256])]
    ov_d = [dout("o_v_p", [128, 256]), dout("o_v_s", [128, 256])]

    tape16 = nc.dram_tensor("tape16", [NPIECE, 128, PIECE], BF16, kind="Internal").ap()

    cur = [SB_BASE]

    def sb(name, shape, dt, at=None):
        esz = mybir.dt.size(dt)
        n = 1
        for s in shape[1:]:
            n *= s
        nbytes = n * esz
        if at is None:
            off = (cur[0] + 63) // 64 * 64
            cur[0] = off + nbytes
        else:
            off = at
        t = nc.alloc_sbuf_tensor_at(name, list(shape), dt, offset=off)
        P.sbuf_addr[t.name] = off
        return t, off

    ring_off = []
    rv_flat, rv_8x512, rv_conv, rv_gu, rv_wd = [], [], [], [], []
    for s in range(RING):
        t, off = sb(f"ring{s}", [128, PIECE], BF16)
        ring_off.append(off)
        rv_flat.append(t)
        rv_8x512.append(t[:, :].rearrange("p (k n) -> p k n", n=512))
        rv_conv.append(t[:, :].rearrange("p (k n) -> p k n", n=128))
        rv_gu.append(t[:, :].rearrange("p (t k n) -> p t k n", t=2, k=8))
        rv_wd.append(t[:, :].rearrange("p (j n) -> p j n", n=1024))
    xt = sb("xt", [128, 4, D], F32)[0]
    xT = sb("xT", [128, 8, 512], BF16)[0]
    xin = sb("xin", [128, 4, D], F32)[0]
    lnt = sb("lnt", [128, 2 * D], F32)[0]
    ident = sb("identt", [128, 128], F32)[0]
    ones = sb("onest", [128, 128], F32)[0]
    dist = sb("distt", [128, 256], F32)[0]
    pcol = sb("pcolt", [128, NCOL], F32)[0]
    gates32 = None
    gatesb = sb("gatesb", [128, 8, 128], BF16)[0]
    sinkb = sb("sinkbt", [128, NHEAD], F32)[0]
    identb = sb("identb", [128, 128], BF16)[0]
    smA = sb("smA", [128, 2, 32], F32)[0]
    xnb1 = sb("xnb", [128, D], BF16)[0]
    xnb = [xnb1, xnb1]
    negsink = sb("negsink", [128, NHEAD], F32)[0]
    cpv = sb("cpv", [128, 8], F32)[0]
    sm = sb("sm", [128, 128], F32)[0]
    xr_buf = sb("xr_buf", [128, 4, 3 + 512], F32)[0]
    g_buf = sb("g_buf", [128, 4, 30 + 512], BF16)[0]
    g32 = sb("g32", [128, 4, 30], F32)[0]
    hstate = sb("hstate", [128, 4], F32)[0]
    kT = sb("kT", [128, 4, 128 + 512], BF16)[0]
    vbuf = sb("vbuf", [128, 5, 256], BF16)[0]
    XB = (cur[0] + 63) // 64 * 64
    cur[0] = XB
    gy = sb("gy", [128, 4, 512], F32)[0]
    sg = sb("sg", [128, 512], F32)[0]
    xc2 = [sb(f"xc{i}", [128, 512], F32)[0] for i in range(2)]
    rr2 = [sb(f"rr{i}", [128, 512], F32)[0] for i in range(2)]
    ii2 = [sb(f"ii{i}", [128, 512], F32)[0] for i in range(2)]
    a22 = [sb(f"a2{i}", [128, 512], F32)[0] for i in range(2)]
    xcb2 = [sb(f"xcb{i}", [128, 512], BF16)[0] for i in range(2)]
    ro = sb("ro", [128, 4, 512], BF16)[0]
    cc = sb("cc", [128, 4, 512], F32)[0]
    sq = sb("sq", [128, 512], F32)[0]
    sq2 = [sq, sg]
    mean, var, tt = xc2[0], rr2[0], ii2[0]
    cn = gy[:, :, :].bitcast(BF16).rearrange("p c n -> p (c n)")[:, 0:2048].rearrange("p (c n) -> p c n", n=512)
    XE = cur[0]
    cur[0] = XB
    qT = sb("qT", [128, 8, 512], BF16)[0]
    oT = sb("oT", [128, 8, 512], BF16)[0]
    sbb = sb("sbb", [128, 8, 256], F32)[0]
    pbf = [sb(f"pbf{i}", [128, 8, 256], BF16)[0] for i in range(2)]
    pT = [sb(f"pT{i}", [128, 2, 128], BF16)[0] for i in range(2)]
    otok1 = sb("otok", [128, D], BF16)[0]
    otok = [otok1, otok1]
    stg = sb("stg", [128, 512], F32)[0]
    sm_rc = sb("sm_rc", [128, 512], F32)[0]
    sm_cf = sm_rc
    kvo = sb("kvo", [128, 512], F32)[0]
    XE = max(cur[0], XE)
    cur[0] = XE
    hT = sb("hT", [128, NJ, 512], BF16)[0]
    xTa = hT[:, 14:22, :]
    sgt1 = sb("sgt", [128, 512], F32)[0]
    sgt = [sgt1, sgt1]
    assert cur[0] <= 229344, cur[0]

    ps = [nc.alloc_psum_tensor(f"ps{i}", [128, 512], F32) for i in range(8)]
    bank = [0]

    def nbank():
        b = bank[0]
        bank[0] = (b + 1) % 6
        return ps[b]

    def mm(out, lhsT, rhs, start, stop):
        P.op("pe", lambda e: e.matmul(out, lhsT, rhs, start=start, stop=stop), [lhsT, rhs], [out], start=start)

    def tr(out, in_, n):
        idn = ident[0:n, 0:n]
        P.op("pe", lambda e: e.transpose(out, in_, idn), [in_, idn], [out])

    def trb(out, in_, n):
        idn = identb[0:n, 0:n]
        P.op("pe", lambda e: e.transpose(out, in_, idn), [in_, idn], [out])

    def act(out, in_, func, bias=None, scale=None, accum=None):
        rd = [in_]
        kw = {}
        if bias is not None:
            kw["bias"] = bias
            if not isinstance(bias, float):
                rd.append(bias)
        if scale is not None:
            kw["scale"] = scale
            if not isinstance(scale, float):
                rd.append(scale)
        wr = [out]
        if accum is not None:
            kw["accum_out"] = accum
            wr.append(accum)
        P.op("act", lambda e: e.activation(out, in_, func, **kw), rd, wr)

    def tt_op(eng, out, a, b, op):
        P.op(eng, lambda e: e.tensor_tensor(out, a, b, op), [a, b], [out])

    def ts_op(eng, out, a, s1, s2, op0, op1=None):
        rd = [a] + [s for s in (s1, s2) if s is not None and not isinstance(s, float)]
        if op1 is None:
            P.op(eng, lambda e: e.tensor_scalar(out, a, s1, None, op0), rd, [out])
        else:
            P.op(eng, lambda e: e.tensor_scalar(out, a, s1, s2, op0, op1), rd, [out])

    def stt(out, a, s, b, op0, op1):
        rd = [a, b] + ([] if isinstance(s, float) else [s])
        P.op("dve", lambda e: e.scalar_tensor_tensor(out, a, s, b, op0, op1), rd, [out])

    def cp(eng, out, in_):
        if eng == "act":
            act(out, in_, AF.Copy)
        else:
            P.op(eng, lambda e: e.tensor_copy(out, in_), [in_], [out])

    def dma(q, sem, out, in_, **kw):
        return P.dma(q, sem, lambda e: e.dma_start(out=out, in_=in_, **kw), [in_], [out])

    for i in range(NPIECE):
        src = tape32[i].rearrange("p (a b) -> p a b", b=2048)
        dst = tape16[i].rearrange("p (a b) -> p a b", b=2048)
        dma("pool", f"cv{i % 4}", dst, src)

    seq_rec = []
    nload = [0]
    held = set()
    curpos = {}

    def use_pieces(tile, first, last):
        ids = list(range(first, last + 1))
        p0 = len(seq_rec)
        for i, pid in enumerate(ids):
            curpos[pid] = p0 + i
            seq_rec.append(pid)
        p1 = p0 + len(ids) - 1
        if order is not None:
            assert order[p0:p1 + 1] == ids, (order[p0:p1 + 1], ids)
            base = min([p0] + list(held))
            lim = min(max(p1, base + RING - 1), len(order) - 1)
        else:
            base = min([p0] + list(held))
            lim = p1
        assert p1 - base < RING, (p1, base)
        while nload[0] <= lim:
            g = nload[0]
            pid = order[g] if order is not None else seq_rec[g]
            dma("sp", f"ring{g % RING}", rv_flat[g % RING][:, :], tape16[pid])
            nload[0] += 1

    def slot(tile, piece):
        return curpos[piece] % RING

    dma("act", "c0", ident[:, :], ident_d)
    dma("act", "c1", ones[:, :], ones_d)
    dma("act", "c2", dist[:, :], dist_d)
    dma("act", "c3", pcol[:, :], pcol_d)
    dma("act", "c0", sinkb[:, :], sinkb_d)
    gst = cc
    dma("act", "c1", gst[:, 0:2, :].rearrange("p a b -> p (a b)"), gates_d)
    P.op("dve", lambda e: e.tensor_copy(gatesb[:, :, :].rearrange("p a b -> p (a b)"),
                                        gst[:, 0:2, :].rearrange("p a b -> p (a b)")),
         [gst[:, 0:2, :]], [gatesb[:, :, :]])
    ts_op("dve", negsink[:, :], sinkb[:, :], -1.0, None, ALU.mult)
    cp("dve", identb[:, :], ident[:, :])
    lam = pcol[:, C_LAM:C_LAM + 4]
    s_abs, s_y, s_z, s_z2, s_p, s_m = (sm[:, 4 * i:4 * i + 4] for i in range(6))
    ts_op("dve", s_m, lam, -1.0, None, ALU.mult)
    tt_op("dve", s_abs, lam, s_m, ALU.max)
    act(s_y, s_abs, AF.Exp, scale=-1.0)
    ts_op("dve", s_z, s_y, 2.0, None, ALU.add)
    P.op("dve", lambda e: e.reciprocal(s_z, s_z), [s_z], [s_z])
    tt_op("dve", s_z, s_z, s_y, ALU.mult)
    tt_op("dve", s_z2, s_z, s_z, ALU.mult)
    ts_op("dve", s_p, s_z2, 1.0 / 9.0, 1.0 / 7.0, ALU.mult, ALU.add)
    for cst in (1.0 / 5.0, 1.0 / 3.0, 1.0):
        tt_op("dve", s_p, s_p, s_z2, ALU.mult)
        ts_op("dve", s_p, s_p, cst, None, ALU.add)
    tt_op("dve", s_p, s_p, s_z, ALU.mult)
    ts_op("dve", s_m, s_m, 0.0, None, ALU.max)
    stt(s_p, s_p, 2.0, s_m, ALU.mult, ALU.add)
    ts_op("dve", cpv[:, 0:4], s_p, -8.0, None, ALU.mult)
    ts_op("dve", cpv[:, 4:8], s_p, -16.0, None, ALU.mult)

    final_ops = []

    def chk(k):
        pass

    def proj_ln(tile, N, nk, src, wrhs, ln_idx, ydst, res=None, tail_jobs=()):
        PT = min(N, 128)
        NB = (N + 127) // 128
        if res is None:
            res = xt
        dma("pool", "lnt", lnt[:, :], lntab_d[ln_idx])
        pbs = {}

        def mmphase(nb):
            pb = [nbank(), nbank()]
            for half in range(2):
                for k in range(nk):
                    mm(pb[half][0:PT, :], src(k)[:, nb * 128:nb * 128 + PT], wrhs(k, half), k == 0, k == nk - 1)
            pbs[nb] = pb

        def post_a(nb):
            pb = pbs[nb]
            for half in range(2):
                xs_ = xt[0:PT, nb, half * 512:(half + 1) * 512]
                rs_ = res[0:PT, nb, half * 512:(half + 1) * 512]
                stt(xs_, rs_, ALPHA, pb[half][0:PT, :], ALU.mult, ALU.add)

        def post(nb, do_a=True):
            if do_a:
                post_a(nb)
            so = 64 + 16 * (nb % 2)
            st = sm[0:PT, so:so + 12]
            mv = sm[0:PT, so + 12:so + 14]
            rs = sm[0:PT, so + 14:so + 15]
            nmr = sm[0:PT, so + 15:so + 16]
            for half in range(2):
                o_ = sm[0:PT, so + 6 * half:so + 6 + 6 * half]
                i_ = xt[0:PT, nb, half * 512:(half + 1) * 512]
                P.op("dve", lambda e, o_=o_, i_=i_: e.bn_stats(o_, i_), [i_], [o_])
            P.op("dve", lambda e: e.bn_aggr(mv, st), [st], [mv])
            act(rs, sm[0:PT, so + 13:so + 14], AF.Sqrt, bias=epsb[0:PT, :])
            P.op("dve", lambda e: e.reciprocal(rs, rs), [rs], [rs])
            stt(nmr, sm[0:PT, so + 12:so + 13], -1.0, rs, ALU.mult, ALU.mult)
            row = xt[0:PT, nb, :]
            if ydst is None:
                xb_ = xnb[nb % 2]
                act(xb_[0:PT, :], row, AF.Identity, bias=nmr, scale=rs)
                gcol = pcol[:, C_LNG + 8 * ln_idx:C_LNG + 8 * ln_idx + 8]
                bcol = pcol[:, C_LNB + 8 * ln_idx:C_LNB + 8 * ln_idx + 8]
                tbb = nbank()[:, :].bitcast(BF16)
                for kc in range(8):
                    trb(tbb[:, kc * 128:kc * 128 + PT], xb_[0:PT, kc * 128:(kc + 1) * 128], PT)
                for kc in range(8):
                    ts_op("dve", xT[:, kc, nb * 128:nb * 128 + PT], tbb[:, kc * 128:kc * 128 + PT],
                          gcol[:, kc:kc + 1], bcol[:, kc:kc + 1], ALU.mult, ALU.add)
            ts_op("pool", row, row, rs, nmr, ALU.mult, ALU.add)
            tt_op("pool", row, row, lnt[0:PT, 0:D], ALU.mult)
            tt_op("pool", row, row, lnt[0:PT, D:2 * D], ALU.add)
            if ydst is not None:
                final_ops.append(dma("act", "yout", ydst[nb * 128:nb * 128 + PT, :], row))

        for nb in range(NB):
            mmphase(nb)
            if nb > 0:
                post(nb - 1)
        post_a(NB - 1)
        for job in tail_jobs:
            job()
        post(NB - 1, do_a=False)

    def ffn_partA(tile, N, layer, holder):
        pg = PC_GU0 if layer == 0 else PC_GU1
        use_pieces(tile, pg, pg)
        held.add(curpos[pg])
        V = rv_gu[slot(tile, pg)]
        pre = {}
        for sub in range(2):
            pre[sub] = (nbank(), nbank())
        for sub in range(2):
            for t_ in range(2):
                for kc in range(8):
                    mm(pre[sub][t_][:, 0:384], V[:, t_, kc, sub * 128:(sub + 1) * 128], xT[:, kc, 0:384], kc == 0, kc == 7)
        holder["pre"] = pre

    def ffn(tile, N, layer, ydst, side=(), holder=None, tail_jobs=()):
        pg = PC_GU0 if layer == 0 else PC_GU1
        pd = PC_WD0 if layer == 0 else PC_WD1
        side = list(side)
        iters = 22
        for jj in range(11):
            pre = {}
            if jj == 0 and holder is not None and "pre" in holder:
                pre = holder["pre"]
                gpos = curpos[pg]
                held.discard(gpos)
                V = rv_gu[slot(tile, pg)]
                for sub in range(2):
                    for t_ in range(2):
                        for kc in range(8):
                            mm(pre[sub][t_][:, 384:512], V[:, t_, kc, sub * 128:(sub + 1) * 128], xT[:, kc, 384:512],
                               kc == 0, kc == 7)
            else:
                use_pieces(tile, pg + jj, pg + jj)
                gpos = curpos[pg + jj]
                V = rv_gu[slot(tile, pg + jj)]
            for sub in range(2):
                j = jj * 2 + sub
                if sub in pre:
                    bg, bu = pre[sub]
                else:
                    bg, bu = nbank(), nbank()
                    for kc in range(8):
                        mm(bg[:, 0:N], V[:, 0, kc, sub * 128:(sub + 1) * 128], xT[:, kc, 0:N], kc == 0, kc == 7)
                    for kc in range(8):
                        mm(bu[:, 0:N], V[:, 1, kc, sub * 128:(sub + 1) * 128], xT[:, kc, 0:N], kc == 0, kc == 7)
                s_ = sgt[j % 2]
                act(s_[:, 0:N], bg[:, 0:N], AF.Silu)
                tt_op("dve", hT[:, j, 0:N], s_[:, 0:N], bu[:, 0:N], ALU.mult)
                if side and not (pre and sub == 0):
                    held.add(gpos)
                    n = -(-len(side) // (iters - j))
                    for _ in range(n):
                        side.pop(0)()
                    held.discard(gpos)
        for job in side:
            job()
        use_pieces(tile, pd, pd + 5)
        proj_ln(tile, N, NJ, lambda j: hT[:, j, :],
                lambda j, half: rv_wd[slot(tile, pd + j // 4)][:, j % 4, half * 512:(half + 1) * 512],
                2 * layer + 1, ydst, tail_jobs=tail_jobs)

    def stageA_jobs(tile, N, is_last, pre=()):
        jobs = list(pre)
        jobs.append(lambda: load_x_tr(N))

        def j_xr(c):
            if c == 0:
                use_pieces(tile, PC_WIN, PC_WIN)
                held.add(curpos[PC_WIN])
            V = rv_8x512[slot(tile, PC_WIN)]
            b = nbank()
            for kc in range(8):
                mm(b[:, 0:N], V[:, kc, c * 128:(c + 1) * 128], xTa[:, kc, 0:N], kc == 0, kc == 7)
            act(xr_buf[:, c, 3:3 + N], b[:, 0:N], AF.Copy)
            if c == 3:
                held.discard(curpos[PC_WIN])

        def j_g(c):
            if c == 0:
                use_pieces(tile, PC_WIN + 1, PC_WIN + 2)
                held.add(curpos[PC_WIN + 1])
                held.add(curpos[PC_WIN + 2])
            Vg = rv_8x512[slot(tile, PC_WIN + 1)]
            Vv = rv_8x512[slot(tile, PC_WIN + 2)]
            b1, b2 = nbank(), nbank()
            for kc in range(8):
                mm(b1[:, 0:N], Vg[:, kc, c * 128:(c + 1) * 128], xTa[:, kc, 0:N], kc == 0, kc == 7)
            for kc in range(8):
                mm(b2[:, 0:N], Vv[:, kc, c * 128:(c + 1) * 128], xTa[:, kc, 0:N], kc == 0, kc == 7)
            act(sg[:, 0:N], b1[:, 0:N], AF.Sigmoid)
            tt_op("dve", g_buf[:, c, 30:30 + N], b2[:, 0:N], sg[:, 0:N], ALU.mult)
            if is_last:
                tt_op("dve", g32[:, c, :], b2[:, N - 30:N], sg[:, N - 30:N], ALU.mult)
            if c == 3:
                held.discard(curpos[PC_WIN + 1])
                held.discard(curpos[PC_WIN + 2])

        def j_yr(c):
            if c == 0:
                use_pieces(tile, PC_WIN + 3, PC_WIN + 3)
                held.add(curpos[PC_WIN + 3])
            V = rv_8x512[slot(tile, PC_WIN + 3)]
            b = nbank()
            for kc in range(8):
                mm(b[:, 0:N], V[:, kc, c * 128:(c + 1) * 128], xTa[:, kc, 0:N], kc == 0, kc == 7)
            act(gy[:, c, 0:N], b[:, 0:N], AF.Gelu_apprx_tanh)
            if c == 3:
                held.discard(curpos[PC_WIN + 3])

        for c in range(4):
            jobs.append(lambda c=c: j_xr(c))
        for c in range(4):
            jobs.append(lambda c=c: j_g(c))
        for c in range(4):
            jobs.append(lambda c=c: j_yr(c))

        def rec1(c):
            xc, rr, ii, a2, xcb = xc2[c % 2], rr2[c % 2], ii2[c % 2], a22[c % 2], xcb2[c % 2]
            w = lambda k: pcol[:, C_RCW + 4 * c + k:C_RCW + 4 * c + k + 1]
            ts_op("dve", xc[:, 0:N], xr_buf[:, c, 0:N], w(0), pcol[:, C_RCB + c:C_RCB + c + 1], ALU.mult, ALU.add)
            for k in range(1, 4):
                stt(xc[:, 0:N], xr_buf[:, c, k:k + N], w(k), xc[:, 0:N], ALU.mult, ALU.add)
            cp("pool", xr_buf[:, c, 0:3], xr_buf[:, c, N:N + 3])
            act(xcb[:, 0:N], xc[:, 0:N], AF.Copy)

        def rec1b(c):
            xc, rr, ii, a2, xcb = xc2[c % 2], rr2[c % 2], ii2[c % 2], a22[c % 2], xcb2[c % 2]
            b1, b2 = nbank(), nbank()
            mm(b1[:, 0:N], gatesb[:, c, :], xcb[:, 0:N], True, True)
            mm(b2[:, 0:N], gatesb[:, 4 + c, :], xcb[:, 0:N], True, True)
            act(rr[:, 0:N], b1[:, 0:N], AF.Sigmoid, bias=pcol[:, C_GAB + c:C_GAB + c + 1])
            act(ii[:, 0:N], b2[:, 0:N], AF.Sigmoid, bias=pcol[:, C_GXB + c:C_GXB + c + 1])
            act(a2[:, 0:N], rr[:, 0:N], AF.Exp, scale=cpv[:, 4 + c:5 + c])
            act(rr[:, 0:N], rr[:, 0:N], AF.Exp, scale=cpv[:, c:c + 1])
            ts_op("pool", a2[:, 0:N], a2[:, 0:N], 1.0, 0.0, ALU.min, ALU.max)
            act(a2[:, 0:N], a2[:, 0:N], AF.Sqrt, bias=oneb[:, :], scale=-1.0)

        def rec2(c):
            xc, rr, ii, a2 = xc2[c % 2], rr2[c % 2], ii2[c % 2], a22[c % 2]
            tt_op("dve", a2[:, 0:N], a2[:, 0:N], ii[:, 0:N], ALU.mult)
            tt_op("dve", a2[:, 0:N], a2[:, 0:N], xc[:, 0:N], ALU.mult)
            hi_, da_, du_, h0_ = ii[:, 0:N], rr[:, 0:N], a2[:, 0:N], hstate[:, c:c + 1]
            P.op("dve", lambda e, hi_=hi_, da_=da_, du_=du_, h0_=h0_: e.tensor_tensor_scan(hi_, da_, du_, h0_, ALU.mult, ALU.add),
                 [da_, du_, h0_], [hi_])
            cp("dve", hstate[:, c:c + 1], ii[:, N - 1:N])
            tt_op("pool", ro[:, c, 0:N], ii[:, 0:N], gy[:, c, 0:N], ALU.mult)

        s1, s2 = ps[6], ps[7]

        def convpe(c):
            use_pieces(tile, PC_CONV + c, PC_CONV + c)
            V = rv_conv[slot(tile, PC_CONV + c)]
            b = nbank()
            for k in range(31):
                mm(b[:, 0:N], V[:, k, :], g_buf[:, c, k:k + N], k == 0, k == 30)
            cp("pool", g_buf[:, c, 0:30], g_buf[:, c, N:N + 30])
            if c > 0:
                stats(c - 1)
            act(cc[:, c, 0:N], b[:, 0:N], AF.Identity, bias=pcol[:, C_CFB + c:C_CFB + c + 1])
            act(sq2[c % 2][:, 0:N], cc[:, c, 0:N], AF.Square)

        def stats(c):
            mm(s1[:, 0:N], ones[:, :], cc[:, c, 0:N], c == 0, c == 3)
            mm(s2[:, 0:N], ones[:, :], sq2[c % 2][:, 0:N], c == 0, c == 3)

        for f, c in ((rec1, 0), (convpe, 0), (rec1b, 0), (rec1, 1), (rec2, 0), (convpe, 1), (rec1b, 1), (rec1, 2),
                     (rec2, 1), (convpe, 2), (rec1b, 2), (rec1, 3), (rec2, 2), (convpe, 3), (rec1b, 3), (rec2, 3)):
            jobs.append(lambda f=f, c=c: f(c))

        def j_cln():
            stats(3)
            ts_op("dve", mean[:, 0:N], s1[:, 0:N], 1.0 / 512.0, None, ALU.mult)
            tt_op("dve", var[:, 0:N], mean[:, 0:N], mean[:, 0:N], ALU.mult)
            stt(var[:, 0:N], s2[:, 0:N], 1.0 / 512.0, var[:, 0:N], ALU.mult, ALU.subtract)
            act(var[:, 0:N], var[:, 0:N], AF.Sqrt, bias=epsb[:, :])
            P.op("dve", lambda e: e.reciprocal(var[:, 0:N], var[:, 0:N]), [var[:, 0:N]], [var[:, 0:N]])
            for c in range(4):
                tt_op("pool", tt[:, 0:N], cc[:, c, 0:N], mean[:, 0:N], ALU.subtract)
                tt_op("dve", tt[:, 0:N], tt[:, 0:N], var[:, 0:N], ALU.mult)
                act(cn[:, c, 0:N], tt[:, 0:N], AF.Silu, bias=pcol[:, C_CFBE + c:C_CFBE + c + 1],
                    scale=pcol[:, C_CFG + c:C_CFG + c + 1])
        jobs.append(j_cln)
        return jobs

    def projA(tile, N, is_last, oi, tail_jobs=()):
        if is_last:
            b = nbank()
            tr(b[0:4, 0:128], hstate[:, 0:4], 128)
            cp("dve", stg[0:4, 0:128], b[0:4, 0:128])
            final_ops.append(dma("act", "so0", oh_d[oi], stg[0:4, 0:128]))
            b = nbank()
            for c in range(4):
                tr(b[0:3, c * 128:(c + 1) * 128], xr_buf[:, c, 0:3], 128)
            cp("dve", sm_rc[0:3, :], b[0:3, :])
            final_ops.append(dma("act", "so1", orc_d[oi], sm_rc[0:3, :]))
            b = nbank()
            for c in range(4):
                tr(b[0:30, c * 128:(c + 1) * 128], g32[:, c, 0:30], 128)
            cp("dve", sm_cf[0:30, :], b[0:30, :])
            final_ops.append(dma("act", "so2", ocf_d[oi], sm_cf[0:30, :]))
        use_pieces(tile, PC_WOUT, PC_WOUT + 1)
        proj_ln(tile, N, 8, lambda k: (ro[:, k, :] if k < 4 else cn[:, k - 4, :]),
                lambda k, half: rv_8x512[slot(tile, PC_WOUT + half)][:, k, :], 0, None, res=xin, tail_jobs=tail_jobs)

    def q_partA(tile, N, holder):
        use_pieces(tile, PC_Q, PC_Q)
        held.add(curpos[PC_Q])
        V = rv_8x512[slot(tile, PC_Q)]
        pre = {}
        for c4 in range(4):
            pre[c4] = nbank()
        for c4 in range(4):
            for kc in range(8):
                mm(pre[c4][:, 0:384], V[:, kc, c4 * 128:(c4 + 1) * 128], xT[:, kc, 0:384], kc == 0, kc == 7)
        holder["pre"] = pre

    def mixer_c(tile, N, is_first, is_last, oi, tail_jobs=(), holder=None):
        PT = min(N, 128)
        NB = (N + 127) // 128
        for hq in range(2):
            pre = {}
            if hq == 0 and holder is not None and "pre" in holder:
                pre = holder["pre"]
                held.discard(curpos[PC_Q])
                V = rv_8x512[slot(tile, PC_Q)]
                for c4 in range(4):
                    for kc in range(8):
                        mm(pre[c4][:, 384:512], V[:, kc, c4 * 128:(c4 + 1) * 128], xT[:, kc, 384:512], kc == 0, kc == 7)
            else:
                use_pieces(tile, PC_Q + hq, PC_Q + hq)
                V = rv_8x512[slot(tile, PC_Q + hq)]
            for c4 in range(4):
                oc = hq * 4 + c4
                if c4 in pre:
                    b = pre[c4]
                else:
                    b = nbank()
                    for kc in range(8):
                        mm(b[:, 0:N], V[:, kc, c4 * 128:(c4 + 1) * 128], xT[:, kc, 0:N], kc == 0, kc == 7)
                act(qT[:, oc, 0:N], b[:, 0:N], AF.Identity, scale=0.125)
        chk(3.01)
        use_pieces(tile, PC_KDUP, PC_KDUP)
        V = rv_8x512[slot(tile, PC_KDUP)]
        for j in range(4):
            b = nbank()
            for kc in range(8):
                mm(b[:, 0:N], V[:, kc, j * 128:(j + 1) * 128], xT[:, kc, 0:N], kc == 0, kc == 7)
            cp("dve", kT[:, j, 128:128 + N], b[:, 0:N])
        chk(3.02)
        use_pieces(tile, PC_KV, PC_KV)
        V = rv_8x512[slot(tile, PC_KV)]
        for nb in range(NB):
            b = nbank()
            for kc in range(8):
                mm(b[0:PT, :], xT[:, kc, nb * 128:nb * 128 + PT], V[:, kc, :], kc == 0, kc == 7)
            act(vbuf[0:PT, 1 + nb, :], b[0:PT, 256:512], AF.Copy)
            if nb == 1:
                chk(3.03)
            if nb == NB - 1:
                chk(3.04)
            if is_last and nb == NB - 1:
                cp("dve", kvo[0:PT, :], b[0:PT, :])
                chk(3.05)
                r0 = 128 - PT
                final_ops.append(dma("act", "so3", ok_d[oi][r0:128, :], kvo[0:PT, 0:256]))
                final_ops.append(dma("act", "so4", ov_d[oi][r0:128, :], kvo[0:PT, 256:512]))
        chk(3.1)
        QN = PT
        units = [(qb, hg) for qb in range(NB) for hg in range(2)]
        info = {}

        def s_phase(ui):
            qb, hg = units[ui]
            par = ui % 2
            blocks = [(qb * 128, 128, qb, 0), (qb * 128 + 128, QN, qb + 1, 128)]
            if is_first and qb == 0:
                blocks = blocks[1:]
            kstart = blocks[0][0]
            d0 = blocks[0][3]
            nk = sum(bk[1] for bk in blocks)
            for hp in range(4):
                bb = [nbank(), nbank()]
                for hh in range(2):
                    h = hg * 8 + hp * 2 + hh
                    oc, half, kv = h // 2, h % 2, h // 4
                    pr = slice(half * 64, half * 64 + 64)
                    mm(bb[hh][0:QN, 0:nk], qT[pr, oc, qb * 128:qb * 128 + QN],
                       kT[pr, kv, kstart:kstart + nk], True, True)
                for hh in range(2):
                    h = hg * 8 + hp * 2 + hh
                    stt(sbb[0:QN, hp * 2 + hh, 0:nk], dist[0:QN, d0:d0 + nk], -SLOPES[h],
                        bb[hh][0:QN, 0:nk], ALU.mult, ALU.add)
            mx = smA[0:QN, par, 0:8]
            negm = smA[0:QN, par, 8:16]
            se = smA[0:QN, par, 16:24]
            rsum = smA[0:QN, par, 24:32]
            sin_ = sbb[0:QN, :, 0:nk]
            P.op("dve", lambda e, sin_=sin_, mx=mx: e.tensor_reduce(mx, sin_, AX.X, ALU.max), [sin_], [mx])
            stt(negm, mx, -1.0, negsink[0:QN, hg * 8:hg * 8 + 8], ALU.mult, ALU.min)
            tt_op("dve", se, negm, sinkb[0:QN, hg * 8:hg * 8 + 8], ALU.add)
            act(se, se, AF.Exp)
            info[ui] = (blocks, kstart, nk)

        def e_phase(ui):
            qb, hg = units[ui]
            par = ui % 2
            blocks, kstart, nk = info[ui]
            for hl in range(8):
                act(pbf[par][0:QN, hl, 0:nk], sbb[0:QN, hl, 0:nk], AF.Exp, bias=smA[0:QN, par, 8 + hl:9 + hl],
                    accum=smA[0:QN, par, 24 + hl:25 + hl])

        def pe_phase(ui):
            qb, hg = units[ui]
            par = ui % 2
            blocks, kstart, nk = info[ui]
            ob = ps[6 + par]
            se = smA[0:QN, par, 16:24]
            tt_op("dve", se, se, smA[0:QN, par, 24:32], ALU.add)
            P.op("dve", lambda e, se=se: e.reciprocal(se, se), [se], [se])
            allfull = all(bk[1] == 128 for bk in blocks) and QN == 128
            for hl in range(8):
                h = hg * 8 + hl
                kv = h // 4
                pT_ = pT[hl % 2]
                tbb = nbank()[:, :].bitcast(BF16)
                for bi, (kcol, kn, vblk, dcol) in enumerate(blocks):
                    off = kcol - kstart
                    trb(tbb[0:kn, bi * 128:bi * 128 + QN], pbf[par][0:QN, hl, off:off + kn], QN)
                ev = "act" if hl % 2 == 0 else "dve"
                if allfull:
                    nb_ = len(blocks)
                    cp(ev, pT_[:, 0:nb_, :], tbb[:, 0:nb_ * 128].rearrange("p (b q) -> p b q", q=128))
                else:
                    for bi, (kcol, kn, vblk, dcol) in enumerate(blocks):
                        cp(ev, pT_[0:kn, bi, 0:QN], tbb[0:kn, bi * 128:bi * 128 + QN])
                for bi, (kcol, kn, vblk, dcol) in enumerate(blocks):
                    mm(ob[0:QN, hl * 64:(hl + 1) * 64], pT_[0:kn, bi, 0:QN], vbuf[0:kn, vblk, kv * 64:kv * 64 + 64],
                       bi == 0, bi == len(blocks) - 1)
            ot = otok[qb % 2]
            rden = smA[0:QN, par, 16:24].unsqueeze(2).broadcast_to([QN, 8, 64])
            tt_op("dve", ot[0:QN, hg * 512:(hg + 1) * 512].rearrange("p (h d) -> p h d", d=64),
                  ob[0:QN, :].rearrange("p (h d) -> p h d", d=64), rden, ALU.mult)
            chk(3.5)
            if hg == 1:
                tbb = nbank()[:, :].bitcast(BF16)
                for oc in range(8):
                    trb(tbb[:, oc * 128:oc * 128 + QN], ot[0:QN, oc * 128:(oc + 1) * 128], QN)
                cp("act", oT[:, :, qb * 128:qb * 128 + QN], tbb[:, :].rearrange("p (c q) -> p c q", q=128)[:, :, 0:QN])

        s_phase(0)
        e_phase(0)
        for ui in range(1, len(units)):
            s_phase(ui)
            pe_phase(ui - 1)
            e_phase(ui)
        pe_phase(len(units) - 1)
        if not is_last:
            cp("pool", kT[:, :, 0:128], kT[:, :, N:N + 128])
            cp("pool", vbuf[:, 0, :], vbuf[:, NB, :])
        use_pieces(tile, PC_WOC, PC_WOC + 1)
        proj_ln(tile, N, 8, lambda k: oT[:, k, :],
                lambda k, half: rv_8x512[slot(tile, PC_WOC + half)][:, k, :], 2, None, tail_jobs=tail_jobs)

    epsb = sb("epsb", [128, 1], F32)[0]
    oneb = sb("oneb", [128, 1], F32)[0]
    sm2 = sb("sm2", [128, 8], F32)[0]
    assert cur[0] <= 229344, cur[0]
    P.op("pool", lambda e: e.memset(epsb[:, :], LN_EPS), [], [epsb[:, :]])
    P.op("pool", lambda e: e.memset(oneb[:, :], 1.0), [], [oneb[:, :]])

    def load_x_dma(src, N):
        PT = min(N, 128)
        NB = (N + 127) // 128
        if NB > 1:
            dma("act", "xin", xin[:, 0:NB, :], src.rearrange("(nb p) d -> p nb d", p=128))
        else:
            dma("act", "xin", xin[0:PT, 0, :], src)

    def load_x_tr(N):
        PT = min(N, 128)
        NB = (N + 127) // 128
        for nb in range(NB):
            xb_ = xnb[nb % 2]
            cp("act" if nb % 2 else "dve", xb_[0:PT, :], xin[0:PT, nb, :])
            tbb = nbank()[:, :].bitcast(BF16)
            for kc in range(8):
                trb(tbb[:, kc * 128:kc * 128 + PT], xb_[0:PT, kc * 128:(kc + 1) * 128], PT)
            cp("dve" if nb % 2 else "act", xTa[:, :, nb * 128:nb * 128 + PT],
               tbb[:, :].rearrange("p (c q) -> p c q", q=128)[:, :, 0:PT])

    P.op("pool", lambda e: e.memset(xr_buf[:, :, 0:3], 0.0), [], [xr_buf[:, :, 0:3]])
    P.op("pool", lambda e: e.memset(g_buf[:, :, 0:30], 0.0), [], [g_buf[:, :, 0:30]])
    P.op("pool", lambda e: e.memset(hstate[:, :], 0.0), [], [hstate[:, :]])
    P.op("pool", lambda e: e.memset(kT[:, :, 0:128], 0.0), [], [kT[:, :, 0:128]])
    P.op("pool", lambda e: e.memset(vbuf[:, 0, :], 0.0), [], [vbuf[:, 0, :]])

    def sample_init():
        dma("act", "c2", xr_buf[:, :, 0:3], st_rc_d.rearrange("p (c k) -> p c k", k=3))
        dma("act", "c3", hstate[:, :], st_h_d)
        dma("act", "c0", cc[:, 0, 0:120], st_cf_d)
        cp("dve", g_buf[:, :, 0:30], cc[:, 0, 0:120].rearrange("p (c k) -> p c k", k=30))
        dma("act", "c1", cc[:, 1, :], st_kT_d)
        cp("dve", kT[:, :, 0:128], cc[:, 1, :].rearrange("p (c k) -> p c k", k=128))
        dma("act", "c2", cc[:, 2, 0:256], st_v_d)
        cp("dve", vbuf[:, 0, :], cc[:, 2, 0:256])
        final_ops.append(dma("act", "so5", ok_d[1][0:64, :], ck_d[64:128, :]))
        final_ops.append(dma("act", "so6", ov_d[1][0:64, :], cv_d[64:128, :]))

    tiles = [(t, 512, 0) for t in range(n_tiles)] + ([(n_tiles, DEC, 1)] if with_sample else [])
    load_x_dma(x_d[0:512, :], 512)
    for job in stageA_jobs(0, 512, n_tiles == 1):
        job()
    for idx, (t, N, oi) in enumerate(tiles):
        is_sample = oi == 1
        last = (t == n_tiles - 1) or is_sample
        h0, hq_, h1 = {}, {}, {}
        split = N == 512
        projA(t, N, last, oi, tail_jobs=[lambda: ffn_partA(t, N, 0, h0)] if split else [])
        nxt = tiles[idx + 1] if idx + 1 < len(tiles) else None
        if nxt is not None:
            nt, nN, noi = nxt
            load_x_dma(xs_d if noi == 1 else x_d[nt * 512:(nt + 1) * 512, :], nN)
        ffn(t, N, 0, None, holder=h0, tail_jobs=[lambda: q_partA(t, N, hq_)] if split else [])
        side = []
        if nxt is not None:
            nt, nN, noi = nxt
            nlast = (nt == n_tiles - 1) or noi == 1
            side = stageA_jobs(nt, nN, nlast, pre=[sample_init] if noi == 1 else [])
        npre = min(len(side), 6 if (nxt is not None and nxt[2] == 1) else 5)
        tj = side[:npre] + ([lambda: ffn_partA(t, N, 1, h1)] if split else [])
        mixer_c(t, N, (t == 0 and not is_sample), last, oi, tail_jobs=tj, holder=hq_)
        ffn(t, N, 1, ys_d if is_sample else y_d[t * 512:(t + 1) * 512, :], side=side[npre:], holder=h1)

    P.wait_all("sp", final_ops + list(P.lastdma.values()))
    P.wait_all("act", final_ops)
    if order is not None:
        P.emit()
    return nc, seq_rec


def build_program(n_tiles=SEQ // 512, with_sample=True, seq=SEQ):
    _, order = _build(n_tiles, with_sample, seq, None)
    nc, order2 = _build(n_tiles, with_sample, seq, list(order))
    assert order2 == order
    return nc


def _pieces(inp):
    f = np.float32
    tape = np.zeros((NPIECE, 128, PIECE), f)

    def kmajor(w, c0, ncol):
        return w[:, c0:c0 + ncol].reshape(8, 128, ncol).transpose(1, 0, 2)

    w_in = inp["w_in_ab"][0]
    for g, c0 in enumerate((0, 1536, 1024, 512)):
        tape[PC_WIN + g] = kmajor(w_in, c0, 512).reshape(128, PIECE)
    cw = inp["cf_conv_w"][0]
    for c in range(4):
        pc = np.zeros((128, 32, 128), f)
        idx = np.arange(128)
        for k in range(31):
            pc[idx, k, idx] = cw[k, c * 128:(c + 1) * 128]
        tape[PC_CONV + c] = pc.reshape(128, PIECE)
    wo = inp["w_out_ab"][0]
    for h in range(2):
        tape[PC_WOUT + h] = kmajor(wo, h * 512, 512).reshape(128, PIECE)
    for layer, (pg, pd) in enumerate(((PC_GU0, PC_WD0), (PC_GU1, PC_WD1))):
        wg, wu, wd = inp["w_ff_gate"][layer], inp["w_ff_up"][layer], inp["w_ff_down"][layer]
        for jj in range(11):
            pc = np.stack([kmajor(wg, jj * 256, 256), kmajor(wu, jj * 256, 256)], axis=1)
            tape[pg + jj] = pc.reshape(128, PIECE)
        wdk = wd.reshape(NJ, 128, D).transpose(1, 0, 2)
        for q in range(6):
            pc = np.zeros((128, 4, D), f)
            n = min(4, NJ - q * 4)
            pc[:, 0:n] = wdk[:, q * 4:q * 4 + n]
            tape[pd + q] = pc.reshape(128, PIECE)
    wqkv = inp["w_qkv"][0]
    for h in range(2):
        tape[PC_Q + h] = kmajor(wqkv, h * 512, 512).reshape(128, PIECE)
    wk = kmajor(wqkv, 1024, 256).reshape(128, 8, 4, 1, 64)
    tape[PC_KDUP] = np.broadcast_to(wk, (128, 8, 4, 2, 64)).reshape(128, PIECE)
    tape[PC_KV] = kmajor(wqkv, 1024, 512).reshape(128, PIECE)
    woc = inp["w_out_c"][0]
    for h in range(2):
        tape[PC_WOC + h] = kmajor(woc, h * 512, 512).reshape(128, PIECE)
    return tape


def _col(v):
    return np.ascontiguousarray(v.reshape(-1, 128).T)


def _shared_inputs(inp):
    f = np.float32
    sh = {}
    sh["tape32"] = _pieces(inp)
    sh["ident"] = np.eye(128, dtype=f)
    sh["ones"] = np.ones((128, 128), f)
    i = np.arange(128)[:, None]
    s = np.arange(256)[None, :]
    dist = np.abs(128 + i - s).astype(f)
    qc = i // 64
    kc = s // 64 - 2
    valid = (kc <= qc) & (kc >= qc - 2)
    dist = np.where(valid, dist, f(1e10)).astype(f)
    sh["dist"] = dist
    pcol = np.zeros((128, NCOL), f)
    rcw = inp["rec_conv_w"][0]
    for c in range(4):
        for k in range(4):
            pcol[:, C_RCW + 4 * c + k] = rcw[k, c * 128:(c + 1) * 128]
    for name, col in (("rec_conv_b", C_RCB), ("rec_gate_a_b", C_GAB), ("rec_gate_x_b", C_GXB), ("rec_lambda", C_LAM),
                      ("cf_conv_b", C_CFB), ("cf_norm_g", C_CFG), ("cf_norm_b", C_CFBE)):
        pcol[:, col:col + 4] = _col(inp[name][0])
    lng = [inp["ln_mix_g"][0], inp["ln_ff_g"][0], inp["ln_mix_g"][1], inp["ln_ff_g"][1]]
    lnb = [inp["ln_mix_b"][0], inp["ln_ff_b"][0], inp["ln_mix_b"][1], inp["ln_ff_b"][1]]
    lntab = np.zeros((4, 128, 2 * D), f)
    for l in range(4):
        pcol[:, C_LNG + 8 * l:C_LNG + 8 * l + 8] = _col(lng[l])
        pcol[:, C_LNB + 8 * l:C_LNB + 8 * l + 8] = _col(lnb[l])
        lntab[l, :, 0:D] = lng[l][None, :]
        lntab[l, :, D:] = lnb[l][None, :]
    sh["pcol"] = pcol
    sh["lntab"] = lntab
    gates = np.zeros((128, 8, 128), f)
    for t_, nm in enumerate(("rec_gate_a_w", "rec_gate_x_w")):
        w = inp[nm][0]
        for c in range(4):
            gates[0:64, 4 * t_ + c, 0:64] = w[2 * c]
            gates[64:128, 4 * t_ + c, 64:128] = w[2 * c + 1]
    sh["gates"] = gates.reshape(128, 1024)
    sh["sinkb"] = np.broadcast_to(inp["attn_sinks"][0][None, :], (128, NHEAD)).astype(f).copy()
    return sh


def _core_inputs(inp, b, seq):
    f = np.float32
    m = {}
    m["x"] = np.ascontiguousarray(inp["x_prompt"][b, :seq])
    m["xs"] = np.ascontiguousarray(inp["x_sample"][b])
    rc = inp["state_rec_conv"][0, b]
    m["st_rc"] = np.ascontiguousarray(rc.reshape(3, 4, 128).transpose(2, 1, 0)).reshape(128, 12)
    cf = inp["state_cf_conv"][0, b]
    m["st_cf"] = np.ascontiguousarray(cf.reshape(30, 4, 128).transpose(2, 1, 0)).reshape(128, 120)
    m["st_h"] = _col(inp["state_rec_h"][0, b])
    ck = inp["cache_k"][0, b]
    kTt = ck.transpose(1, 2, 0)
    kd = np.stack([kTt, kTt], axis=1)
    m["st_kT"] = np.ascontiguousarray(kd.reshape(4, 128, 128).transpose(1, 0, 2)).reshape(128, 512)
    m["st_v"] = np.ascontiguousarray(inp["cache_v"][0, b].reshape(128, 256))
    m["ck"] = np.ascontiguousarray(ck.reshape(128, 256))
    m["cv"] = np.ascontiguousarray(inp["cache_v"][0, b].reshape(128, 256))
    return {k: np.asarray(v, f) for k, v in m.items()}


_PROG_CACHE = {}


def run(inp, ncores=NCORE, seq=SEQ, with_sample=True):
    inp = {k: np.asarray(v) for k, v in inp.items()}
    key = (seq, with_sample)
    if key not in _PROG_CACHE:
        _PROG_CACHE[key] = build_program(seq // 512, with_sample, seq)
    nc = _PROG_CACHE[key]
    sh = _shared_inputs(inp)
    in_maps = []
    for b in range(ncores):
        m = dict(sh)
        m.update(_core_inputs(inp, b, seq))
        in_maps.append(m)
    res = run_bass_kernel_spmd(nc, in_maps, core_ids=list(range(ncores)))
    R = res.results
    f = np.float32

    def st(name, shape):
        return np.stack([np.asarray(R[b][name], f).reshape(shape) for b in range(ncores)])

    y = st("y", (seq, D))
    ys = st("ys", (DEC, D))
    outs = (y, ys,
            st("o_h_p", (512,))[None], st("o_h_s", (512,))[None],
            st("o_rc_p", (3, 512))[None], st("o_rc_s", (3, 512))[None],
            st("o_cf_p", (30, 512))[None], st("o_cf_s", (30, 512))[None],
            st("o_k_p", (128, 4, 64))[None], st("o_k_s", (128, 4, 64))[None],
            st("o_v_p", (128, 4, 64))[None], st("o_v_s", (128, 4, 64))[None])
    return outs


def kernel(**inputs):
    return run(inputs)
```

```python
import contextlib
import numpy as np
import concourse.bass as bass
import concourse.mybir as mybir
from concourse.bass_utils import run_bass_kernel_spmd

F32 = mybir.dt.float32
BF16 = mybir.dt.bfloat16
AF = mybir.ActivationFunctionType
ALU = mybir.AluOpType
AX = mybir.AxisListType

D = 1024
SEQ = 8192
NCORE = 8
DEC = 64
D_FF = 2816
NJ = D_FF // 128
ALPHA = 4.0 ** 0.25
LN_EPS = 1e-5
NHEAD = 16
SLOPES = [2.0 ** (-8.0 * (h + 1) / NHEAD) for h in range(NHEAD)]
RING = 8
PIECE = 4096
SB_BASE = 16512

PC_WIN = 0
PC_CONV = 4
PC_WOUT = 8
PC_GU0 = 10
PC_WD0 = 21
PC_Q = 27
PC_KDUP = 29
PC_KV = 30
PC_WOC = 31
PC_GU1 = 33
PC_WD1 = 44
NPIECE = 50

C_RCW, C_RCB, C_GAB, C_GXB, C_LAM, C_CFB, C_CFG, C_CFBE, C_LNG, C_LNB = 0, 16, 20, 24, 28, 32, 36, 40, 44, 76
NCOL = 108


class Op:
    __slots__ = ("eng", "fn", "waits", "semkey", "seq", "needs_inc", "value", "is_dma")


class Prog:
    ENGS = ["pe", "act", "dve", "pool", "sp"]

    def __init__(self, nc):
        self.nc = nc
        self.ops = {e: [] for e in self.ENGS}
        self.recs = {}
        self.waited = {e: {} for e in self.ENGS}
        self.semseq = {}
        self.lastdma = {}
        self.sbuf_addr = {}
        self.pending = {}

    def region(self, ap):
        t = ap.tensor
        name = t.name
        pat = [(int(s), int(n)) for s, n in ap.ap]
        off = int(ap.offset)
        esz = mybir.dt.size(ap.dtype)
        cls = type(t).__name__
        if cls.startswith("DRam"):
            lo = off
            hi = off + sum((n - 1) * abs(s) for s, n in pat) + 1
            return ("d:" + name, 0, 1, lo * esz, hi * esz)
        pstep = pat[0][0]
        p0 = off // pstep if pstep else 0
        fo = off - p0 * pstep
        p1 = p0 + pat[0][1]
        ext = sum((n - 1) * abs(s) for s, n in pat[1:]) + 1
        if cls.startswith("PSum"):
            return ("p:" + name, 0, 128, 0, 2048)
        base = self.sbuf_addr[name]
        return ("sb", p0, p1, base + fo * esz, base + (fo + ext) * esz)

    def _psum_guard(self, op, reads, writes, start):
        for kind, aps in (("r", reads), ("w", writes)):
            for ap in aps:
                if not type(ap.tensor).__name__.startswith("PSum"):
                    continue
                pat = [(int(s_), int(n)) for s_, n in ap.ap]
                off = int(ap.offset)
                pstep = pat[0][0]
                fo = off - (off // pstep) * pstep if pstep else 0
                ext = sum((n - 1) * abs(s_) for s_, n in pat[1:]) + 1
                esz = mybir.dt.size(ap.dtype)
                lo, hi = fo * esz, (fo + ext) * esz
                pend = self.pending.setdefault(ap.tensor.name, [])
                if kind == "r" and op.eng != "pe":
                    pend[:] = [iv for iv in pend if not (iv[0] < hi and lo < iv[1])]
                elif kind == "w" and op.eng == "pe":
                    if start:
                        for iv in pend:
                            assert not (iv[0] < hi and lo < iv[1]), ("PSUM reuse before consumption", ap.tensor.name, lo, hi, iv)
                        pend.append((lo, hi))

    def _deps(self, op, reads, writes):
        deps = []
        for kind, aps in (("w", writes), ("r", reads)):
            for ap in aps:
                key, p0, p1, lo, hi = self.region(ap)
                lst = self.recs.setdefault(key, [])
                keep = []
                for r in lst:
                    rp0, rp1, rlo, rhi, rkind, rop = r
                    if rp0 < p1 and p0 < rp1 and rlo < hi and lo < rhi:
                        if kind == "r":
                            if rkind == "w":
                                deps.append((rop, "raw"))
                            elif key[0] == "p" and rop.eng != op.eng:
                                deps.append((rop, "rar"))
                            keep.append(r)
                        else:
                            deps.append((rop, "waw" if rkind == "w" else "war"))
                            if p0 <= rp0 and rp1 <= p1 and lo <= rlo and rhi <= hi:
                                continue
                            keep.append(r)
                    else:
                        keep.append(r)
                if kind == "r":
                    keep = [r for r in keep if not (r[4] == "r" and r[5].semkey == op.semkey and not r[5].is_dma
                                                    and not op.is_dma and p0 <= r[0] and r[1] <= p1 and lo <= r[2] and r[3] <= hi)]
                keep.append((p0, p1, lo, hi, kind, op))
                self.recs[key] = keep
        return deps

    def _add_waits(self, op, deps):
        w = self.waited[op.eng]
        for rop, kind in deps:
            if rop is op:
                continue
            if not rop.is_dma and not op.is_dma and rop.eng == op.eng:
                if op.eng == "pe":
                    continue
            if w.get(rop.semkey, -1) >= rop.seq:
                continue
            w[rop.semkey] = rop.seq
            rop.needs_inc = True
            op.waits.append(rop)

    def op(self, eng, fn, reads=(), writes=(), start=True):
        o = Op()
        o.eng, o.fn, o.waits, o.semkey, o.is_dma = eng, fn, [], eng, False
        o.seq = len(self.ops[eng])
        o.needs_inc, o.value = False, None
        self._psum_guard(o, reads, writes, start)
        self._add_waits(o, self._deps(o, reads, writes))
        self.ops[eng].append(o)
        return o

    def dma(self, q, sem, fn, reads=(), writes=()):
        o = Op()
        o.eng, o.fn, o.waits, o.semkey, o.is_dma = q, fn, [], "dma:" + sem, True
        o.seq = self.semseq.get(sem, 0)
        self.semseq[sem] = o.seq + 1
        o.needs_inc, o.value = True, 16 * (o.seq + 1)
        deps = self._deps(o, reads, writes)
        prev = self.lastdma.get(sem)
        if prev is not None:
            deps.append((prev, "raw"))
        self.lastdma[sem] = o
        self._add_waits(o, deps)
        self.ops[q].append(o)
        return o

    def wait_all(self, eng, oplist):
        o = Op()
        o.eng, o.fn, o.waits, o.semkey, o.is_dma = eng, None, [], eng, False
        o.seq = len(self.ops[eng])
        o.needs_inc, o.value = False, None
        self._add_waits(o, [(x, "raw") for x in oplist])
        self.ops[eng].append(o)

    def emit(self):
        nc = self.nc
        for e in self.ENGS:
            cnt = 0
            for o in self.ops[e]:
                if o.is_dma:
                    continue
                if o.needs_inc:
                    cnt += 1
                    o.value = cnt
        with contextlib.ExitStack() as es:
            sems = {}
            for e in self.ENGS:
                sems[e] = es.enter_context(nc.semaphore("s_" + e))
            for s in self.semseq:
                sems["dma:" + s] = es.enter_context(nc.semaphore("d_" + s))
            block = es.enter_context(nc.Block())

            def run(ename):
                def body(eng):
                    for o in self.ops[ename]:
                        for w in o.waits:
                            eng.wait_ge(sems[w.semkey], w.value)
                        if o.fn is None:
                            continue
                        ins = o.fn(eng)
                        if o.is_dma:
                            ins.then_inc(sems[o.semkey], 16)
                        elif o.needs_inc:
                            ins.then_inc(sems[o.semkey], 1)
                return body

            block.tensor(run("pe"))
            block.scalar(run("act"))
            block.vector(run("dve"))
            block.gpsimd(run("pool"))
            block.sync(run("sp"))


def _build(n_tiles, with_sample, seq, order):
    nc = bass.Bass("TRN2", target_bir_lowering=False)
    P = Prog(nc)

    def din(name, shape, dt=F32):
        return nc.dram_tensor(name, list(shape), dt, kind="ExternalInput").ap()

    def dout(name, shape, dt=F32):
        return nc.dram_tensor(name, list(shape), dt, kind="ExternalOutput").ap()

    x_d = din("x", [seq, D])
    xs_d = din("xs", [DEC, D])
    tape32 = din("tape32", [NPIECE, 128, PIECE])
    ident_d = din("ident", [128, 128])
    ones_d = din("ones", [128, 128])
    dist_d = din("dist", [128, 256])
    pcol_d = din("pcol", [128, NCOL])
    lntab_d = din("lntab", [4, 128, 2 * D])
    gates_d = din("gates", [128, 8 * 128])
    sinkb_d = din("sinkb", [128, NHEAD])
    st_rc_d = din("st_rc", [128, 12])
    st_cf_d = din("st_cf", [128, 120])
    st_h_d = din("st_h", [128, 4])
    st_kT_d = din("st_kT", [128, 512])
    st_v_d = din("st_v", [128, 256])
    ck_d = din("ck", [128, 256])
    cv_d = din("cv", [128, 256])

    y_d = dout("y", [seq, D])
    ys_d = dout("ys", [DEC, D])
    oh_d = [dout("o_h_p", [4, 128]), dout("o_h_s", [4, 128])]
    orc_d = [dout("o_rc_p", [3, 512]), dout("o_rc_s", [3, 512])]
    ocf_d = [dout("o_cf_p", [30, 512]), dout("o_cf_s", [30, 512])]
    ok_d = [dout("o_k_p", [128, 256]), dout("o_k_s", [128, 256])]
    ov_d = [dout("o_v_p", [128, 256]), dout("o_v_s", [128, 256])]

    tape16 = nc.dram_tensor("tape16", [NPIECE, 128, PIECE], BF16, kind="Internal").ap()

    cur = [SB_BASE]

    def sb(name, shape, dt, at=None):
        esz = mybir.dt.size(dt)
        n = 1
        for s in shape[1:]:
            n *= s
        nbytes = n * esz
        if at is None:
            off = (cur[0] + 63) // 64 * 64
            cur[0] = off + nbytes
        else:
            off = at
        t = nc.alloc_sbuf_tensor_at(name, list(shape), dt, offset=off)
        P.sbuf_addr[t.name] = off
        return t, off

    ring_off = []
    rv_flat, rv_8x512, rv_conv, rv_gu, rv_wd = [], [], [], [], []
    for s in range(RING):
        t, off = sb(f"ring{s}", [128, PIECE], BF16)
        ring_off.append(off)
        rv_flat.append(t)
        rv_8x512.append(t[:, :].rearrange("p (k n) -> p k n", n=512))
        rv_conv.append(t[:, :].rearrange("p (k n) -> p k n", n=128))
        rv_gu.append(t[:, :].rearrange("p (t k n) -> p t k n", t=2, k=8))
        rv_wd.append(t[:, :].rearrange("p (j n) -> p j n", n=1024))
    xt = sb("xt", [128, 4, D], F32)[0]
    xT = sb("xT", [128, 8, 512], BF16)[0]
    xin = sb("xin", [128, 4, D], F32)[0]
    lnt = sb("lnt", [128, 2 * D], F32)[0]
    ident = sb("identt", [128, 128], F32)[0]
    ones = sb("onest", [128, 128], F32)[0]
    dist = sb("distt", [128, 256], F32)[0]
    pcol = sb("pcolt", [128, NCOL], F32)[0]
    gates32 = None
    gatesb = sb("gatesb", [128, 8, 128], BF16)[0]
    sinkb = sb("sinkbt", [128, NHEAD], F32)[0]
    identb = sb("identb", [128, 128], BF16)[0]
    smA = sb("smA", [128, 2, 32], F32)[0]
    xnb1 = sb("xnb", [128, D], BF16)[0]
    xnb = [xnb1, xnb1]
    negsink = sb("negsink", [128, NHEAD], F32)[0]
    cpv = sb("cpv", [128, 8], F32)[0]
    sm = sb("sm", [128, 128], F32)[0]
    xr_buf = sb("xr_buf", [128, 4, 3 + 512], F32)[0]
    g_buf = sb("g_buf", [128, 4, 30 + 512], BF16)[0]
    g32 = sb("g32", [128, 4, 30], F32)[0]
    hstate = sb("hstate", [128, 4], F32)[0]
    kT = sb("kT", [128, 4, 128 + 512], BF16)[0]
    vbuf = sb("vbuf", [128, 5, 256], BF16)[0]
    XB = (cur[0] + 63) // 64 * 64
    cur[0] = XB
    gy = sb("gy", [128, 4, 512], F32)[0]
    sg = sb("sg", [128, 512], F32)[0]
    xc2 = [sb(f"xc{i}", [128, 512], F32)[0] for i in range(2)]
    rr2 = [sb(f"rr{i}", [128, 512], F32)[0] for i in range(2)]
    ii2 = [sb(f"ii{i}", [128, 512], F32)[0] for i in range(2)]
    a22 = [sb(f"a2{i}", [128, 512], F32)[0] for i in range(2)]
    xcb2 = [sb(f"xcb{i}", [128, 512], BF16)[0] for i in range(2)]
    ro = sb("ro", [128, 4, 512], BF16)[0]
    cc = sb("cc", [128, 4, 512], F32)[0]
    sq = sb("sq", [128, 512], F32)[0]
    sq2 = [sq, sg]
    mean, var, tt = xc2[0], rr2[0], ii2[0]
    cn = gy[:, :, :].bitcast(BF16).rearrange("p c n -> p (c n)")[:, 0:2048].rearrange("p (c n) -> p c n", n=512)
    XE = cur[0]
    cur[0] = XB
    qT = sb("qT", [128, 8, 512], BF16)[0]
    oT = sb("oT", [128, 8, 512], BF16)[0]
    sbb = sb("sbb", [128, 8, 256], F32)[0]
    pbf = [sb(f"pbf{i}", [128, 8, 256], BF16)[0] for i in range(2)]
    pT = [sb(f"pT{i}", [128, 2, 128], BF16)[0] for i in range(2)]
    otok1 = sb("otok", [128, D], BF16)[0]
    otok = [otok1, otok1]
    stg = sb("stg", [128, 512], F32)[0]
    sm_rc = sb("sm_rc", [128, 512], F32)[0]
    sm_cf = sm_rc
    kvo = sb("kvo", [128, 512], F32)[0]
    XE = max(cur[0], XE)
    cur[0] = XE
    hT = sb("hT", [128, NJ, 512], BF16)[0]
    xTa = hT[:, 14:22, :]
    sgt1 = sb("sgt", [128, 512], F32)[0]
    sgt = [sgt1, sgt1]
    assert cur[0] <= 229344, cur[0]

    ps = [nc.alloc_psum_tensor(f"ps{i}", [128, 512], F32) for i in range(8)]
    bank = [0]

    def nbank():
        b = bank[0]
        bank[0] = (b + 1) % 6
        return ps[b]

    def mm(out, lhsT, rhs, start, stop):
        P.op("pe", lambda e: e.matmul(out, lhsT, rhs, start=start, stop=stop), [lhsT, rhs], [out], start=start)

    def tr(out, in_, n):
        idn = ident[0:n, 0:n]
        P.op("pe", lambda e: e.transpose(out, in_, idn), [in_, idn], [out])

    def trb(out, in_, n):
        idn = identb[0:n, 0:n]
        P.op("pe", lambda e: e.transpose(out, in_, idn), [in_, idn], [out])

    def act(out, in_, func, bias=None, scale=None, accum=None):
        rd = [in_]
        kw = {}
        if bias is not None:
            kw["bias"] = bias
            if not isinstance(bias, float):
                rd.append(bias)
        if scale is not None:
            kw["scale"] = scale
            if not isinstance(scale, float):
                rd.append(scale)
        wr = [out]
        if accum is not None:
            kw["accum_out"] = accum
            wr.append(accum)
        P.op("act", lambda e: e.activation(out, in_, func, **kw), rd, wr)

    def tt_op(eng, out, a, b, op):
        P.op(eng, lambda e: e.tensor_tensor(out, a, b, op), [a, b], [out])

    def ts_op(eng, out, a, s1, s2, op0, op1=None):
        rd = [a] + [s for s in (s1, s2) if s is not None and not isinstance(s, float)]
        if op1 is None:
            P.op(eng, lambda e: e.tensor_scalar(out, a, s1, None, op0), rd, [out])
        else:
            P.op(eng, lambda e: e.tensor_scalar(out, a, s1, s2, op0, op1), rd, [out])

    def stt(out, a, s, b, op0, op1):
        rd = [a, b] + ([] if isinstance(s, float) else [s])
        P.op("dve", lambda e: e.scalar_tensor_tensor(out, a, s, b, op0, op1), rd, [out])

    def cp(eng, out, in_):
        if eng == "act":
            act(out, in_, AF.Copy)
        else:
            P.op(eng, lambda e: e.tensor_copy(out, in_), [in_], [out])

    def dma(q, sem, out, in_, **kw):
        return P.dma(q, sem, lambda e: e.dma_start(out=out, in_=in_, **kw), [in_], [out])

    for i in range(NPIECE):
        src = tape32[i].rearrange("p (a b) -> p a b", b=2048)
        dst = tape16[i].rearrange("p (a b) -> p a b", b=2048)
        dma("pool", f"cv{i % 4}", dst, src)

    seq_rec = []
    nload = [0]
    held = set()
    curpos = {}

    def use_pieces(tile, first, last):
        ids = list(range(first, last + 1))
        p0 = len(seq_rec)
        for i, pid in enumerate(ids):
            curpos[pid] = p0 + i
            seq_rec.append(pid)
        p1 = p0 + len(ids) - 1
        if order is not None:
            assert order[p0:p1 + 1] == ids, (order[p0:p1 + 1], ids)
            base = min([p0] + list(held))
            lim = min(max(p1, base + RING - 1), len(order) - 1)
        else:
            base = min([p0] + list(held))
            lim = p1
        assert p1 - base < RING, (p1, base)
        while nload[0] <= lim:
            g = nload[0]
            pid = order[g] if order is not None else seq_rec[g]
            dma("sp", f"ring{g % RING}", rv_flat[g % RING][:, :], tape16[pid])
            nload[0] += 1

    def slot(tile, piece):
        return curpos[piece] % RING

    dma("act", "c0", ident[:, :], ident_d)
    dma("act", "c1", ones[:, :], ones_d)
    dma("act", "c2", dist[:, :], dist_d)
    dma("act", "c3", pcol[:, :], pcol_d)
    dma("act", "c0", sinkb[:, :], sinkb_d)
    gst = cc
    dma("act", "c1", gst[:, 0:2, :].rearrange("p a b -> p (a b)"), gates_d)
    P.op("dve", lambda e: e.tensor_copy(gatesb[:, :, :].rearrange("p a b -> p (a b)"),
                                        gst[:, 0:2, :].rearrange("p a b -> p (a b)")),
         [gst[:, 0:2, :]], [gatesb[:, :, :]])
    ts_op("dve", negsink[:, :], sinkb[:, :], -1.0, None, ALU.mult)
    cp("dve", identb[:, :], ident[:, :])
    lam = pcol[:, C_LAM:C_LAM + 4]
    s_abs, s_y, s_z, s_z2, s_p, s_m = (sm[:, 4 * i:4 * i + 4] for i in range(6))
    ts_op("dve", s_m, lam, -1.0, None, ALU.mult)
    tt_op("dve", s_abs, lam, s_m, ALU.max)
    act(s_y, s_abs, AF.Exp, scale=-1.0)
    ts_op("dve", s_z, s_y, 2.0, None, ALU.add)
    P.op("dve", lambda e: e.reciprocal(s_z, s_z), [s_z], [s_z])
    tt_op("dve", s_z, s_z, s_y, ALU.mult)
    tt_op("dve", s_z2, s_z, s_z, ALU.mult)
    ts_op("dve", s_p, s_z2, 1.0 / 9.0, 1.0 / 7.0, ALU.mult, ALU.add)
    for cst in (1.0 / 5.0, 1.0 / 3.0, 1.0):
        tt_op("dve", s_p, s_p, s_z2, ALU.mult)
        ts_op("dve", s_p, s_p, cst, None, ALU.add)
    tt_op("dve", s_p, s_p, s_z, ALU.mult)
    ts_op("dve", s_m, s_m, 0.0, None, ALU.max)
    stt(s_p, s_p, 2.0, s_m, ALU.mult, ALU.add)
    ts_op("dve", cpv[:, 0:4], s_p, -8.0, None, ALU.mult)
    ts_op("dve", cpv[:, 4:8], s_p, -16.0, None, ALU.mult)

    final_ops = []

    def chk(k):
        pass

    def proj_ln(tile, N, nk, src, wrhs, ln_idx, ydst, res=None, tail_jobs=()):
        PT = min(N, 128)
        NB = (N + 127) // 128
        if res is None:
            res = xt
        dma("pool", "lnt", lnt[:, :], lntab_d[ln_idx])
        pbs = {}

        def mmphase(nb):
            pb = [nbank(), nbank()]
            for half in range(2):
                for k in range(nk):
                    mm(pb[half][0:PT, :], src(k)[:, nb * 128:nb * 128 + PT], wrhs(k, half), k == 0, k == nk - 1)
            pbs[nb] = pb

        def post_a(nb):
            pb = pbs[nb]
            for half in range(2):
                xs_ = xt[0:PT, nb, half * 512:(half + 1) * 512]
                rs_ = res[0:PT, nb, half * 512:(half + 1) * 512]
                stt(xs_, rs_, ALPHA, pb[half][0:PT, :], ALU.mult, ALU.add)

        def post(nb, do_a=True):
            if do_a:
                post_a(nb)
            so = 64 + 16 * (nb % 2)
            st = sm[0:PT, so:so + 12]
            mv = sm[0:PT, so + 12:so + 14]
            rs = sm[0:PT, so + 14:so + 15]
            nmr = sm[0:PT, so + 15:so + 16]
            for half in range(2):
                o_ = sm[0:PT, so + 6 * half:so + 6 + 6 * half]
                i_ = xt[0:PT, nb, half * 512:(half + 1) * 512]
                P.op("dve", lambda e, o_=o_, i_=i_: e.bn_stats(o_, i_), [i_], [o_])
            P.op("dve", lambda e: e.bn_aggr(mv, st), [st], [mv])
            act(rs, sm[0:PT, so + 13:so + 14], AF.Sqrt, bias=epsb[0:PT, :])
            P.op("dve", lambda e: e.reciprocal(rs, rs), [rs], [rs])
            stt(nmr, sm[0:PT, so + 12:so + 13], -1.0, rs, ALU.mult, ALU.mult)
            row = xt[0:PT, nb, :]
            if ydst is None:
                xb_ = xnb[nb % 2]
                act(xb_[0:PT, :], row, AF.Identity, bias=nmr, scale=rs)
                gcol = pcol[:, C_LNG + 8 * ln_idx:C_LNG + 8 * ln_idx + 8]
                bcol = pcol[:, C_LNB + 8 * ln_idx:C_LNB + 8 * ln_idx + 8]
                tbb = nbank()[:, :].bitcast(BF16)
                for kc in range(8):
                    trb(tbb[:, kc * 128:kc * 128 + PT], xb_[0:PT, kc * 128:(kc + 1) * 128], PT)
                for kc in range(8):
                    ts_op("dve", xT[:, kc, nb * 128:nb * 128 + PT], tbb[:, kc * 128:kc * 128 + PT],
                          gcol[:, kc:kc + 1], bcol[:, kc:kc + 1], ALU.mult, ALU.add)
            if ydst is not None:
                act(row, row, AF.Identity, bias=nmr, scale=rs)
                tt_op("dve", row, row, lnt[0:PT, 0:D], ALU.mult)
                tt_op("pool", row, row, lnt[0:PT, D:2 * D], ALU.add)
            else:
                ts_op("pool", row, row, rs, nmr, ALU.mult, ALU.add)
                tt_op("pool", row, row, lnt[0:PT, 0:D], ALU.mult)
                tt_op("pool", row, row, lnt[0:PT, D:2 * D], ALU.add)
            if ydst is not None:
                final_ops.append(dma("act", "yout", ydst[nb * 128:nb * 128 + PT, :], row))

        for nb in range(NB):
            mmphase(nb)
            if nb > 0:
                post(nb - 1)
        post_a(NB - 1)
        for job in tail_jobs:
            job()
        post(NB - 1, do_a=False)

    def ffn_partA(tile, N, layer, holder):
        pg = PC_GU0 if layer == 0 else PC_GU1
        use_pieces(tile, pg, pg)
        held.add(curpos[pg])
        V = rv_gu[slot(tile, pg)]
        pre = {}
        for sub in range(2):
            pre[sub] = (nbank(), nbank())
        for sub in range(2):
            for t_ in range(2):
                for kc in range(8):
                    mm(pre[sub][t_][:, 0:384], V[:, t_, kc, sub * 128:(sub + 1) * 128], xT[:, kc, 0:384], kc == 0, kc == 7)
        holder["pre"] = pre

    def ffn(tile, N, layer, ydst, side=(), holder=None, tail_jobs=()):
        pg = PC_GU0 if layer == 0 else PC_GU1
        pd = PC_WD0 if layer == 0 else PC_WD1
        side = list(side)
        iters = 22
        for jj in range(11):
            pre = {}
            if jj == 0 and holder is not None and "pre" in holder:
                pre = holder["pre"]
                gpos = curpos[pg]
                held.discard(gpos)
                V = rv_gu[slot(tile, pg)]
                for sub in range(2):
                    for t_ in range(2):
                        for kc in range(8):
                            mm(pre[sub][t_][:, 384:512], V[:, t_, kc, sub * 128:(sub + 1) * 128], xT[:, kc, 384:512],
                               kc == 0, kc == 7)
            else:
                use_pieces(tile, pg + jj, pg + jj)
                gpos = curpos[pg + jj]
                V = rv_gu[slot(tile, pg + jj)]
            for sub in range(2):
                j = jj * 2 + sub
                if sub in pre:
                    bg, bu = pre[sub]
                else:
                    bg, bu = nbank(), nbank()
                    for kc in range(8):
                        mm(bg[:, 0:N], V[:, 0, kc, sub * 128:(sub + 1) * 128], xT[:, kc, 0:N], kc == 0, kc == 7)
                    for kc in range(8):
                        mm(bu[:, 0:N], V[:, 1, kc, sub * 128:(sub + 1) * 128], xT[:, kc, 0:N], kc == 0, kc == 7)
                s_ = sgt[j % 2]
                act(s_[:, 0:N], bg[:, 0:N], AF.Silu)
                tt_op("dve", hT[:, j, 0:N], s_[:, 0:N], bu[:, 0:N], ALU.mult)
                if side and not (pre and sub == 0):
                    held.add(gpos)
                    n = -(-len(side) // (iters - j))
                    for _ in range(n):
                        side.pop(0)()
                    held.discard(gpos)
        for job in side:
            job()
        use_pieces(tile, pd, pd + 5)
        proj_ln(tile, N, NJ, lambda j: hT[:, j, :],
                lambda j, half: rv_wd[slot(tile, pd + j // 4)][:, j % 4, half * 512:(half + 1) * 512],
                2 * layer + 1, ydst, tail_jobs=tail_jobs)

    def stageA_jobs(tile, N, is_last, pre=()):
        jobs = list(pre)
        jobs.append(lambda: load_x_tr(N))

        def j_xr(c):
            if c == 0:
                use_pieces(tile, PC_WIN, PC_WIN)
                held.add(curpos[PC_WIN])
            V = rv_8x512[slot(tile, PC_WIN)]
            b = nbank()
            for kc in range(8):
                mm(b[:, 0:N], V[:, kc, c * 128:(c + 1) * 128], xTa[:, kc, 0:N], kc == 0, kc == 7)
            act(xr_buf[:, c, 3:3 + N], b[:, 0:N], AF.Copy)
            if c == 3:
                held.discard(curpos[PC_WIN])

        def j_g(c):
            if c == 0:
                use_pieces(tile, PC_WIN + 1, PC_WIN + 2)
                held.add(curpos[PC_WIN + 1])
                held.add(curpos[PC_WIN + 2])
            Vg = rv_8x512[slot(tile, PC_WIN + 1)]
            Vv = rv_8x512[slot(tile, PC_WIN + 2)]
            b1, b2 = nbank(), nbank()
            for kc in range(8):
                mm(b1[:, 0:N], Vg[:, kc, c * 128:(c + 1) * 128], xTa[:, kc, 0:N], kc == 0, kc == 7)
            for kc in range(8):
                mm(b2[:, 0:N], Vv[:, kc, c * 128:(c + 1) * 128], xTa[:, kc, 0:N], kc == 0, kc == 7)
            act(sg[:, 0:N], b1[:, 0:N], AF.Sigmoid)
            tt_op("dve", g_buf[:, c, 30:30 + N], b2[:, 0:N], sg[:, 0:N], ALU.mult)
            if is_last:
                tt_op("dve", g32[:, c, :], b2[:, N - 30:N], sg[:, N - 30:N], ALU.mult)
            if c == 3:
                held.discard(curpos[PC_WIN + 1])
                held.discard(curpos[PC_WIN + 2])

        def j_yr(c):
            if c == 0:
                use_pieces(tile, PC_WIN + 3, PC_WIN + 3)
                held.add(curpos[PC_WIN + 3])
            V = rv_8x512[slot(tile, PC_WIN + 3)]
            b = nbank()
            for kc in range(8):
                mm(b[:, 0:N], V[:, kc, c * 128:(c + 1) * 128], xTa[:, kc, 0:N], kc == 0, kc == 7)
            act(gy[:, c, 0:N], b[:, 0:N], AF.Gelu_apprx_tanh)
            if c == 3:
                held.discard(curpos[PC_WIN + 3])

        for c in range(4):
            jobs.append(lambda c=c: j_xr(c))
        for c in range(4):
            jobs.append(lambda c=c: j_g(c))
        for c in range(4):
            jobs.append(lambda c=c: j_yr(c))

        def rec1(c):
            xc, rr, ii, a2, xcb = xc2[c % 2], rr2[c % 2], ii2[c % 2], a22[c % 2], xcb2[c % 2]
            w = lambda k: pcol[:, C_RCW + 4 * c + k:C_RCW + 4 * c + k + 1]
            ts_op("dve", xc[:, 0:N], xr_buf[:, c, 0:N], w(0), pcol[:, C_RCB + c:C_RCB + c + 1], ALU.mult, ALU.add)
            for k in range(1, 4):
                stt(xc[:, 0:N], xr_buf[:, c, k:k + N], w(k), xc[:, 0:N], ALU.mult, ALU.add)
            cp("pool", xr_buf[:, c, 0:3], xr_buf[:, c, N:N + 3])
            act(xcb[:, 0:N], xc[:, 0:N], AF.Copy)

        def rec1b(c):
            xc, rr, ii, a2, xcb = xc2[c % 2], rr2[c % 2], ii2[c % 2], a22[c % 2], xcb2[c % 2]
            b1, b2 = nbank(), nbank()
            mm(b1[:, 0:N], gatesb[:, c, :], xcb[:, 0:N], True, True)
            mm(b2[:, 0:N], gatesb[:, 4 + c, :], xcb[:, 0:N], True, True)
            act(rr[:, 0:N], b1[:, 0:N], AF.Sigmoid, bias=pcol[:, C_GAB + c:C_GAB + c + 1])
            act(ii[:, 0:N], b2[:, 0:N], AF.Sigmoid, bias=pcol[:, C_GXB + c:C_GXB + c + 1])
            act(a2[:, 0:N], rr[:, 0:N], AF.Exp, scale=cpv[:, 4 + c:5 + c])
            act(rr[:, 0:N], rr[:, 0:N], AF.Exp, scale=cpv[:, c:c + 1])
            ts_op("pool", a2[:, 0:N], a2[:, 0:N], 1.0, 0.0, ALU.min, ALU.max)
            act(a2[:, 0:N], a2[:, 0:N], AF.Sqrt, bias=oneb[:, :], scale=-1.0)

        def rec2(c):
            xc, rr, ii, a2 = xc2[c % 2], rr2[c % 2], ii2[c % 2], a22[c % 2]
            tt_op("dve", a2[:, 0:N], a2[:, 0:N], ii[:, 0:N], ALU.mult)
            tt_op("dve", a2[:, 0:N], a2[:, 0:N], xc[:, 0:N], ALU.mult)
            hi_, da_, du_, h0_ = ii[:, 0:N], rr[:, 0:N], a2[:, 0:N], hstate[:, c:c + 1]
            P.op("dve", lambda e, hi_=hi_, da_=da_, du_=du_, h0_=h0_: e.tensor_tensor_scan(hi_, da_, du_, h0_, ALU.mult, ALU.add),
                 [da_, du_, h0_], [hi_])
            cp("dve", hstate[:, c:c + 1], ii[:, N - 1:N])
            tt_op("pool", ro[:, c, 0:N], ii[:, 0:N], gy[:, c, 0:N], ALU.mult)

        s1, s2 = ps[6], ps[7]

        def convpe(c):
            use_pieces(tile, PC_CONV + c, PC_CONV + c)
            V = rv_conv[slot(tile, PC_CONV + c)]
            b = nbank()
            for k in range(31):
                mm(b[:, 0:N], V[:, k, :], g_buf[:, c, k:k + N], k == 0, k == 30)
            cp("pool", g_buf[:, c, 0:30], g_buf[:, c, N:N + 30])
            if c > 0:
                stats(c - 1)
            act(cc[:, c, 0:N], b[:, 0:N], AF.Identity, bias=pcol[:, C_CFB + c:C_CFB + c + 1])
            act(sq2[c % 2][:, 0:N], cc[:, c, 0:N], AF.Square)

        def stats(c):
            mm(s1[:, 0:N], ones[:, :], cc[:, c, 0:N], c == 0, c == 3)
            mm(s2[:, 0:N], ones[:, :], sq2[c % 2][:, 0:N], c == 0, c == 3)

        for f, c in ((rec1, 0), (convpe, 0), (rec1b, 0), (rec1, 1), (rec2, 0), (convpe, 1), (rec1b, 1), (rec1, 2),
                     (rec2, 1), (convpe, 2), (rec1b, 2), (rec1, 3), (rec2, 2), (convpe, 3), (rec1b, 3), (rec2, 3)):
            jobs.append(lambda f=f, c=c: f(c))

        def j_cln():
            stats(3)
            ts_op("dve", mean[:, 0:N], s1[:, 0:N], 1.0 / 512.0, None, ALU.mult)
            tt_op("dve", var[:, 0:N], mean[:, 0:N], mean[:, 0:N], ALU.mult)
            stt(var[:, 0:N], s2[:, 0:N], 1.0 / 512.0, var[:, 0:N], ALU.mult, ALU.subtract)
            act(var[:, 0:N], var[:, 0:N], AF.Sqrt, bias=epsb[:, :])
            P.op("dve", lambda e: e.reciprocal(var[:, 0:N], var[:, 0:N]), [var[:, 0:N]], [var[:, 0:N]])
            for c in range(4):
                tt_op("pool", tt[:, 0:N], cc[:, c, 0:N], mean[:, 0:N], ALU.subtract)
                tt_op("dve", tt[:, 0:N], tt[:, 0:N], var[:, 0:N], ALU.mult)
                act(cn[:, c, 0:N], tt[:, 0:N], AF.Silu, bias=pcol[:, C_CFBE + c:C_CFBE + c + 1],
                    scale=pcol[:, C_CFG + c:C_CFG + c + 1])
        jobs.append(j_cln)
        return jobs

    def projA(tile, N, is_last, oi, tail_jobs=()):
        if is_last:
            b = nbank()
            tr(b[0:4, 0:128], hstate[:, 0:4], 128)
            cp("dve", stg[0:4, 0:128], b[0:4, 0:128])
            final_ops.append(dma("act", "so0", oh_d[oi], stg[0:4, 0:128]))
            b = nbank()
            for c in range(4):
                tr(b[0:3, c * 128:(c + 1) * 128], xr_buf[:, c, 0:3], 128)
            cp("dve", sm_rc[0:3, :], b[0:3, :])
            final_ops.append(dma("act", "so1", orc_d[oi], sm_rc[0:3, :]))
            b = nbank()
            for c in range(4):
                tr(b[0:30, c * 128:(c + 1) * 128], g32[:, c, 0:30], 128)
            cp("dve", sm_cf[0:30, :], b[0:30, :])
            final_ops.append(dma("act", "so2", ocf_d[oi], sm_cf[0:30, :]))
        use_pieces(tile, PC_WOUT, PC_WOUT + 1)
        proj_ln(tile, N, 8, lambda k: (ro[:, k, :] if k < 4 else cn[:, k - 4, :]),
                lambda k, half: rv_8x512[slot(tile, PC_WOUT + half)][:, k, :], 0, None, res=xin, tail_jobs=tail_jobs)

    def q_partA(tile, N, holder):
        use_pieces(tile, PC_Q, PC_Q)
        held.add(curpos[PC_Q])
        V = rv_8x512[slot(tile, PC_Q)]
        pre = {}
        for c4 in range(4):
            pre[c4] = nbank()
        for c4 in range(4):
            for kc in range(8):
                mm(pre[c4][:, 0:384], V[:, kc, c4 * 128:(c4 + 1) * 128], xT[:, kc, 0:384], kc == 0, kc == 7)
        holder["pre"] = pre

    def mixer_c(tile, N, is_first, is_last, oi, tail_jobs=(), holder=None):
        PT = min(N, 128)
        NB = (N + 127) // 128
        for hq in range(2):
            pre = {}
            if hq == 0 and holder is not None and "pre" in holder:
                pre = holder["pre"]
                held.discard(curpos[PC_Q])
                V = rv_8x512[slot(tile, PC_Q)]
                for c4 in range(4):
                    for kc in range(8):
                        mm(pre[c4][:, 384:512], V[:, kc, c4 * 128:(c4 + 1) * 128], xT[:, kc, 384:512], kc == 0, kc == 7)
            else:
                use_pieces(tile, PC_Q + hq, PC_Q + hq)
                V = rv_8x512[slot(tile, PC_Q + hq)]
            for c4 in range(4):
                oc = hq * 4 + c4
                if c4 in pre:
                    b = pre[c4]
                else:
                    b = nbank()
                    for kc in range(8):
                        mm(b[:, 0:N], V[:, kc, c4 * 128:(c4 + 1) * 128], xT[:, kc, 0:N], kc == 0, kc == 7)
                act(qT[:, oc, 0:N], b[:, 0:N], AF.Identity, scale=0.125)
        chk(3.01)
        use_pieces(tile, PC_KDUP, PC_KDUP)
        V = rv_8x512[slot(tile, PC_KDUP)]
        for j in range(4):
            b = nbank()
            for kc in range(8):
                mm(b[:, 0:N], V[:, kc, j * 128:(j + 1) * 128], xT[:, kc, 0:N], kc == 0, kc == 7)
            cp("dve", kT[:, j, 128:128 + N], b[:, 0:N])
        chk(3.02)
        use_pieces(tile, PC_KV, PC_KV)
        V = rv_8x512[slot(tile, PC_KV)]
        for nb in range(NB):
            b = nbank()
            for kc in range(8):
                mm(b[0:PT, :], xT[:, kc, nb * 128:nb * 128 + PT], V[:, kc, :], kc == 0, kc == 7)
            act(vbuf[0:PT, 1 + nb, :], b[0:PT, 256:512], AF.Copy)
            if nb == 1:
                chk(3.03)
            if nb == NB - 1:
                chk(3.04)
            if is_last and nb == NB - 1:
                cp("dve", kvo[0:PT, :], b[0:PT, :])
                chk(3.05)
                r0 = 128 - PT
                final_ops.append(dma("act", "so3", ok_d[oi][r0:128, :], kvo[0:PT, 0:256]))
                final_ops.append(dma("act", "so4", ov_d[oi][r0:128, :], kvo[0:PT, 256:512]))
        chk(3.1)
        QN = PT
        units = [(qb, hg) for qb in range(NB) for hg in range(2)]
        info = {}

        def s_phase(ui):
            qb, hg = units[ui]
            par = ui % 2
            blocks = [(qb * 128, 128, qb, 0), (qb * 128 + 128, QN, qb + 1, 128)]
            if is_first and qb == 0:
                blocks = blocks[1:]
            kstart = blocks[0][0]
            d0 = blocks[0][3]
            nk = sum(bk[1] for bk in blocks)
            for hp in range(4):
                bb = [nbank(), nbank()]
                for hh in range(2):
                    h = hg * 8 + hp * 2 + hh
                    oc, half, kv = h // 2, h % 2, h // 4
                    pr = slice(half * 64, half * 64 + 64)
                    mm(bb[hh][0:QN, 0:nk], qT[pr, oc, qb * 128:qb * 128 + QN],
                       kT[pr, kv, kstart:kstart + nk], True, True)
                for hh in range(2):
                    h = hg * 8 + hp * 2 + hh
                    stt(sbb[0:QN, hp * 2 + hh, 0:nk], dist[0:QN, d0:d0 + nk], -SLOPES[h],
                        bb[hh][0:QN, 0:nk], ALU.mult, ALU.add)
            mx = smA[0:QN, par, 0:8]
            negm = smA[0:QN, par, 8:16]
            se = smA[0:QN, par, 16:24]
            rsum = smA[0:QN, par, 24:32]
            sin_ = sbb[0:QN, :, 0:nk]
            P.op("dve", lambda e, sin_=sin_, mx=mx: e.tensor_reduce(mx, sin_, AX.X, ALU.max), [sin_], [mx])
            stt(negm, mx, -1.0, negsink[0:QN, hg * 8:hg * 8 + 8], ALU.mult, ALU.min)
            tt_op("dve", se, negm, sinkb[0:QN, hg * 8:hg * 8 + 8], ALU.add)
            act(se, se, AF.Exp)
            info[ui] = (blocks, kstart, nk)

        def e_phase(ui):
            qb, hg = units[ui]
            par = ui % 2
            blocks, kstart, nk = info[ui]
            for hl in range(8):
                act(pbf[par][0:QN, hl, 0:nk], sbb[0:QN, hl, 0:nk], AF.Exp, bias=smA[0:QN, par, 8 + hl:9 + hl],
                    accum=smA[0:QN, par, 24 + hl:25 + hl])

        def pe_phase(ui):
            qb, hg = units[ui]
            par = ui % 2
            blocks, kstart, nk = info[ui]
            ob = ps[6 + par]
            se = smA[0:QN, par, 16:24]
            tt_op("dve", se, se, smA[0:QN, par, 24:32], ALU.add)
            P.op("dve", lambda e, se=se: e.reciprocal(se, se), [se], [se])
            allfull = all(bk[1] == 128 for bk in blocks) and QN == 128
            for hl in range(8):
                h = hg * 8 + hl
                kv = h // 4
                pT_ = pT[hl % 2]
                tbb = nbank()[:, :].bitcast(BF16)
                for bi, (kcol, kn, vblk, dcol) in enumerate(blocks):
                    off = kcol - kstart
                    trb(tbb[0:kn, bi * 128:bi * 128 + QN], pbf[par][0:QN, hl, off:off + kn], QN)
                ev = "act"
                if allfull:
                    nb_ = len(blocks)
                    cp(ev, pT_[:, 0:nb_, :], tbb[:, 0:nb_ * 128].rearrange("p (b q) -> p b q", q=128))
                else:
                    for bi, (kcol, kn, vblk, dcol) in enumerate(blocks):
                        cp(ev, pT_[0:kn, bi, 0:QN], tbb[0:kn, bi * 128:bi * 128 + QN])
                for bi, (kcol, kn, vblk, dcol) in enumerate(blocks):
                    mm(ob[0:QN, hl * 64:(hl + 1) * 64], pT_[0:kn, bi, 0:QN], vbuf[0:kn, vblk, kv * 64:kv * 64 + 64],
                       bi == 0, bi == len(blocks) - 1)
            ot = otok[qb % 2]
            rden = smA[0:QN, par, 16:24].unsqueeze(2).broadcast_to([QN, 8, 64])
            tt_op("dve", ot[0:QN, hg * 512:(hg + 1) * 512].rearrange("p (h d) -> p h d", d=64),
                  ob[0:QN, :].rearrange("p (h d) -> p h d", d=64), rden, ALU.mult)
            chk(3.5)
            if hg == 1:
                tbb = nbank()[:, :].bitcast(BF16)
                for oc in range(8):
                    trb(tbb[:, oc * 128:oc * 128 + QN], ot[0:QN, oc * 128:(oc + 1) * 128], QN)
                cp("act", oT[:, :, qb * 128:qb * 128 + QN], tbb[:, :].rearrange("p (c q) -> p c q", q=128)[:, :, 0:QN])

        s_phase(0)
        e_phase(0)
        for ui in range(1, len(units)):
            s_phase(ui)
            pe_phase(ui - 1)
            e_phase(ui)
        pe_phase(len(units) - 1)
        if not is_last:
            cp("pool", kT[:, :, 0:128], kT[:, :, N:N + 128])
            cp("pool", vbuf[:, 0, :], vbuf[:, NB, :])
        use_pieces(tile, PC_WOC, PC_WOC + 1)
        proj_ln(tile, N, 8, lambda k: oT[:, k, :],
                lambda k, half: rv_8x512[slot(tile, PC_WOC + half)][:, k, :], 2, None, tail_jobs=tail_jobs)

    epsb = sb("epsb", [128, 1], F32)[0]
    oneb = sb("oneb", [128, 1], F32)[0]
    sm2 = sb("sm2", [128, 8], F32)[0]
    assert cur[0] <= 229344, cur[0]
    P.op("pool", lambda e: e.memset(epsb[:, :], LN_EPS), [], [epsb[:, :]])
    P.op("pool", lambda e: e.memset(oneb[:, :], 1.0), [], [oneb[:, :]])

    def load_x_dma(src, N):
        PT = min(N, 128)
        NB = (N + 127) // 128
        if NB > 1:
            dma("act", "xin", xin[:, 0:NB, :], src.rearrange("(nb p) d -> p nb d", p=128))
        else:
            dma("act", "xin", xin[0:PT, 0, :], src)

    def load_x_tr(N):
        PT = min(N, 128)
        NB = (N + 127) // 128
        for nb in range(NB):
            xb_ = xnb[nb % 2]
            cp("act" if nb % 2 else "dve", xb_[0:PT, :], xin[0:PT, nb, :])
            tbb = nbank()[:, :].bitcast(BF16)
            for kc in range(8):
                trb(tbb[:, kc * 128:kc * 128 + PT], xb_[0:PT, kc * 128:(kc + 1) * 128], PT)
            cp("dve" if nb % 2 else "act", xTa[:, :, nb * 128:nb * 128 + PT],
               tbb[:, :].rearrange("p (c q) -> p c q", q=128)[:, :, 0:PT])

    P.op("pool", lambda e: e.memset(xr_buf[:, :, 0:3], 0.0), [], [xr_buf[:, :, 0:3]])
    P.op("pool", lambda e: e.memset(g_buf[:, :, 0:30], 0.0), [], [g_buf[:, :, 0:30]])
    P.op("pool", lambda e: e.memset(hstate[:, :], 0.0), [], [hstate[:, :]])
    P.op("pool", lambda e: e.memset(kT[:, :, 0:128], 0.0), [], [kT[:, :, 0:128]])
    P.op("pool", lambda e: e.memset(vbuf[:, 0, :], 0.0), [], [vbuf[:, 0, :]])

    def sample_init():
        dma("act", "c2", xr_buf[:, :, 0:3], st_rc_d.rearrange("p (c k) -> p c k", k=3))
        dma("act", "c3", hstate[:, :], st_h_d)
        dma("act", "c0", cc[:, 0, 0:120], st_cf_d)
        cp("dve", g_buf[:, :, 0:30], cc[:, 0, 0:120].rearrange("p (c k) -> p c k", k=30))
        dma("act", "c1", cc[:, 1, :], st_kT_d)
        cp("dve", kT[:, :, 0:128], cc[:, 1, :].rearrange("p (c k) -> p c k", k=128))
        dma("act", "c2", cc[:, 2, 0:256], st_v_d)
        cp("dve", vbuf[:, 0, :], cc[:, 2, 0:256])
        final_ops.append(dma("act", "so5", ok_d[1][0:64, :], ck_d[64:128, :]))
        final_ops.append(dma("act", "so6", ov_d[1][0:64, :], cv_d[64:128, :]))

    tiles = [(t, 512, 0) for t in range(n_tiles)] + ([(n_tiles, DEC, 1)] if with_sample else [])
    load_x_dma(x_d[0:512, :], 512)
    for job in stageA_jobs(0, 512, n_tiles == 1):
        job()
    for idx, (t, N, oi) in enumerate(tiles):
        is_sample = oi == 1
        last = (t == n_tiles - 1) or is_sample
        h0, hq_, h1 = {}, {}, {}
        split = N == 512
        projA(t, N, last, oi, tail_jobs=[lambda: ffn_partA(t, N, 0, h0)] if split else [])
        nxt = tiles[idx + 1] if idx + 1 < len(tiles) else None
        if nxt is not None:
            nt, nN, noi = nxt
            load_x_dma(xs_d if noi == 1 else x_d[nt * 512:(nt + 1) * 512, :], nN)
        ffn(t, N, 0, None, holder=h0, tail_jobs=[lambda: q_partA(t, N, hq_)] if split else [])
        side = []
        if nxt is not None:
            nt, nN, noi = nxt
            nlast = (nt == n_tiles - 1) or noi == 1
            side = stageA_jobs(nt, nN, nlast, pre=[sample_init] if noi == 1 else [])
        npre = min(len(side), 6 if (nxt is not None and nxt[2] == 1) else 5)
        tj = side[:npre] + ([lambda: ffn_partA(t, N, 1, h1)] if split else [])
        mixer_c(t, N, (t == 0 and not is_sample), last, oi, tail_jobs=tj, holder=hq_)
        ffn(t, N, 1, ys_d if is_sample else y_d[t * 512:(t + 1) * 512, :], side=side[npre:], holder=h1)

    P.wait_all("sp", final_ops + list(P.lastdma.values()))
    P.wait_all("act", final_ops)
    if order is not None:
        P.emit()
    return nc, seq_rec


def build_program(n_tiles=SEQ // 512, with_sample=True, seq=SEQ):
    _, order = _build(n_tiles, with_sample, seq, None)
    nc, order2 = _build(n_tiles, with_sample, seq, list(order))
    assert order2 == order
    return nc


def _pieces(inp):
    f = np.float32
    tape = np.zeros((NPIECE, 128, PIECE), f)

    def kmajor(w, c0, ncol):
        return w[:, c0:c0 + ncol].reshape(8, 128, ncol).transpose(1, 0, 2)

    w_in = inp["w_in_ab"][0]
    for g, c0 in enumerate((0, 1536, 1024, 512)):
        tape[PC_WIN + g] = kmajor(w_in, c0, 512).reshape(128, PIECE)
    cw = inp["cf_conv_w"][0]
    for c in range(4):
        pc = np.zeros((128, 32, 128), f)
        idx = np.arange(128)
        for k in range(31):
            pc[idx, k, idx] = cw[k, c * 128:(c + 1) * 128]
        tape[PC_CONV + c] = pc.reshape(128, PIECE)
    wo = inp["w_out_ab"][0]
    for h in range(2):
        tape[PC_WOUT + h] = kmajor(wo, h * 512, 512).reshape(128, PIECE)
    for layer, (pg, pd) in enumerate(((PC_GU0, PC_WD0), (PC_GU1, PC_WD1))):
        wg, wu, wd = inp["w_ff_gate"][layer], inp["w_ff_up"][layer], inp["w_ff_down"][layer]
        for jj in range(11):
            pc = np.stack([kmajor(wg, jj * 256, 256), kmajor(wu, jj * 256, 256)], axis=1)
            tape[pg + jj] = pc.reshape(128, PIECE)
        wdk = wd.reshape(NJ, 128, D).transpose(1, 0, 2)
        for q in range(6):
            pc = np.zeros((128, 4, D), f)
            n = min(4, NJ - q * 4)
            pc[:, 0:n] = wdk[:, q * 4:q * 4 + n]
            tape[pd + q] = pc.reshape(128, PIECE)
    wqkv = inp["w_qkv"][0]
    for h in range(2):
        tape[PC_Q + h] = kmajor(wqkv, h * 512, 512).reshape(128, PIECE)
    wk = kmajor(wqkv, 1024, 256).reshape(128, 8, 4, 1, 64)
    tape[PC_KDUP] = np.broadcast_to(wk, (128, 8, 4, 2, 64)).reshape(128, PIECE)
    tape[PC_KV] = kmajor(wqkv, 1024, 512).reshape(128, PIECE)
    woc = inp["w_out_c"][0]
    for h in range(2):
        tape[PC_WOC + h] = kmajor(woc, h * 512, 512).reshape(128, PIECE)
    return tape


def _col(v):
    return np.ascontiguousarray(v.reshape(-1, 128).T)


def _shared_inputs(inp):
    f = np.float32
    sh = {}
    sh["tape32"] = _pieces(inp)
    sh["ident"] = np.eye(128, dtype=f)
    sh["ones"] = np.ones((128, 128), f)
    i = np.arange(128)[:, None]
    s = np.arange(256)[None, :]
    dist = np.abs(128 + i - s).astype(f)
    qc = i // 64
    kc = s // 64 - 2
    valid = (kc <= qc) & (kc >= qc - 2)
    dist = np.where(valid, dist, f(1e10)).astype(f)
    sh["dist"] = dist
    pcol = np.zeros((128, NCOL), f)
    rcw = inp["rec_conv_w"][0]
    for c in range(4):
        for k in range(4):
            pcol[:, C_RCW + 4 * c + k] = rcw[k, c * 128:(c + 1) * 128]
    for name, col in (("rec_conv_b", C_RCB), ("rec_gate_a_b", C_GAB), ("rec_gate_x_b", C_GXB), ("rec_lambda", C_LAM),
                      ("cf_conv_b", C_CFB), ("cf_norm_g", C_CFG), ("cf_norm_b", C_CFBE)):
        pcol[:, col:col + 4] = _col(inp[name][0])
    lng = [inp["ln_mix_g"][0], inp["ln_ff_g"][0], inp["ln_mix_g"][1], inp["ln_ff_g"][1]]
    lnb = [inp["ln_mix_b"][0], inp["ln_ff_b"][0], inp["ln_mix_b"][1], inp["ln_ff_b"][1]]
    lntab = np.zeros((4, 128, 2 * D), f)
    for l in range(4):
        pcol[:, C_LNG + 8 * l:C_LNG + 8 * l + 8] = _col(lng[l])
        pcol[:, C_LNB + 8 * l:C_LNB + 8 * l + 8] = _col(lnb[l])
        lntab[l, :, 0:D] = lng[l][None, :]
        lntab[l, :, D:] = lnb[l][None, :]
    sh["pcol"] = pcol
    sh["lntab"] = lntab
    gates = np.zeros((128, 8, 128), f)
    for t_, nm in enumerate(("rec_gate_a_w", "rec_gate_x_w")):
        w = inp[nm][0]
        for c in range(4):
            gates[0:64, 4 * t_ + c, 0:64] = w[2 * c]
            gates[64:128, 4 * t_ + c, 64:128] = w[2 * c + 1]
    sh["gates"] = gates.reshape(128, 1024)
    sh["sinkb"] = np.broadcast_to(inp["attn_sinks"][0][None, :], (128, NHEAD)).astype(f).copy()
    return sh


def _core_inputs(inp, b, seq):
    f = np.float32
    m = {}
    m["x"] = np.ascontiguousarray(inp["x_prompt"][b, :seq])
    m["xs"] = np.ascontiguousarray(inp["x_sample"][b])
    rc = inp["state_rec_conv"][0, b]
    m["st_rc"] = np.ascontiguousarray(rc.reshape(3, 4, 128).transpose(2, 1, 0)).reshape(128, 12)
    cf = inp["state_cf_conv"][0, b]
    m["st_cf"] = np.ascontiguousarray(cf.reshape(30, 4, 128).transpose(2, 1, 0)).reshape(128, 120)
    m["st_h"] = _col(inp["state_rec_h"][0, b])
    ck = inp["cache_k"][0, b]
    kTt = ck.transpose(1, 2, 0)
    kd = np.stack([kTt, kTt], axis=1)
    m["st_kT"] = np.ascontiguousarray(kd.reshape(4, 128, 128).transpose(1, 0, 2)).reshape(128, 512)
    m["st_v"] = np.ascontiguousarray(inp["cache_v"][0, b].reshape(128, 256))
    m["ck"] = np.ascontiguousarray(ck.reshape(128, 256))
    m["cv"] = np.ascontiguousarray(inp["cache_v"][0, b].reshape(128, 256))
    return {k: np.asarray(v, f) for k, v in m.items()}


_PROG_CACHE = {}


def run(inp, ncores=NCORE, seq=SEQ, with_sample=True):
    inp = {k: np.asarray(v) for k, v in inp.items()}
    key = (seq, with_sample)
    if key not in _PROG_CACHE:
        _PROG_CACHE[key] = build_program(seq // 512, with_sample, seq)
    nc = _PROG_CACHE[key]
    sh = _shared_inputs(inp)
    in_maps = []
    for b in range(ncores):
        m = dict(sh)
        m.update(_core_inputs(inp, b, seq))
        in_maps.append(m)
    res = run_bass_kernel_spmd(nc, in_maps, core_ids=list(range(ncores)))
    R = res.results
    f = np.float32

    def st(name, shape):
        return np.stack([np.asarray(R[b][name], f).reshape(shape) for b in range(ncores)])

    y = st("y", (seq, D))
    ys = st("ys", (DEC, D))
    outs = (y, ys,
            st("o_h_p", (512,))[None], st("o_h_s", (512,))[None],
            st("o_rc_p", (3, 512))[None], st("o_rc_s", (3, 512))[None],
            st("o_cf_p", (30, 512))[None], st("o_cf_s", (30, 512))[None],
            st("o_k_p", (128, 4, 64))[None], st("o_k_s", (128, 4, 64))[None],
            st("o_v_p", (128, 4, 64))[None], st("o_v_s", (128, 4, 64))[None])
    return outs


def kernel(**inputs):
    return run(inputs)
```

```python
import contextlib
import numpy as np
import concourse.bass as bass
import concourse.mybir as mybir
from concourse.bass_utils import run_bass_kernel_spmd

F32 = mybir.dt.float32
BF16 = mybir.dt.bfloat16
AF = mybir.ActivationFunctionType
ALU = mybir.AluOpType
AX = mybir.AxisListType

D = 1024
SEQ = 8192
NCORE = 8
DEC = 64
D_FF = 2816
NJ = D_FF // 128
ALPHA = 4.0 ** 0.25
LN_EPS = 1e-5
NHEAD = 16
SLOPES = [2.0 ** (-8.0 * (h + 1) / NHEAD) for h in range(NHEAD)]
RING = 8
PIECE = 4096
SB_BASE = 16512

PC_WIN = 0
PC_CONV = 4
PC_WOUT = 8
PC_GU0 = 10
PC_WD0 = 21
PC_Q = 27
PC_KDUP = 29
PC_KV = 30
PC_WOC = 31
PC_GU1 = 33
PC_WD1 = 44
NPIECE = 50

C_RCW, C_RCB, C_GAB, C_GXB, C_LAM, C_CFB, C_CFG, C_CFBE, C_LNG, C_LNB = 0, 16, 20, 24, 28, 32, 36, 40, 44, 76
NCOL = 108


class Op:
    __slots__ = ("eng", "fn", "waits", "semkey", "seq", "needs_inc", "value", "is_dma")


class Prog:
    ENGS = ["pe", "act", "dve", "pool", "sp"]

    def __init__(self, nc):
        self.nc = nc
        self.ops = {e: [] for e in self.ENGS}
        self.recs = {}
        self.waited = {e: {} for e in self.ENGS}
        self.semseq = {}
        self.lastdma = {}
        self.sbuf_addr = {}
        self.pending = {}

    def region(self, ap):
        t = ap.tensor
        name = t.name
        pat = [(int(s), int(n)) for s, n in ap.ap]
        off = int(ap.offset)
        esz = mybir.dt.size(ap.dtype)
        cls = type(t).__name__
        if cls.startswith("DRam"):
            lo = off
            hi = off + sum((n - 1) * abs(s) for s, n in pat) + 1
            return ("d:" + name, 0, 1, lo * esz, hi * esz)
        pstep = pat[0][0]
        p0 = off // pstep if pstep else 0
        fo = off - p0 * pstep
        p1 = p0 + pat[0][1]
        ext = sum((n - 1) * abs(s) for s, n in pat[1:]) + 1
        if cls.startswith("PSum"):
            return ("p:" + name, 0, 128, 0, 2048)
        base = self.sbuf_addr[name]
        return ("sb", p0, p1, base + fo * esz, base + (fo + ext) * esz)

    def _psum_guard(self, op, reads, writes, start):
        for kind, aps in (("r", reads), ("w", writes)):
            for ap in aps:
                if not type(ap.tensor).__name__.startswith("PSum"):
                    continue
                pat = [(int(s_), int(n)) for s_, n in ap.ap]
                off = int(ap.offset)
                pstep = pat[0][0]
                fo = off - (off // pstep) * pstep if pstep else 0
                ext = sum((n - 1) * abs(s_) for s_, n in pat[1:]) + 1
                esz = mybir.dt.size(ap.dtype)
                lo, hi = fo * esz, (fo + ext) * esz
                pend = self.pending.setdefault(ap.tensor.name, [])
                if kind == "r" and op.eng != "pe":
                    pend[:] = [iv for iv in pend if not (iv[0] < hi and lo < iv[1])]
                elif kind == "w" and op.eng == "pe":
                    if start:
                        for iv in pend:
                            assert not (iv[0] < hi and lo < iv[1]), ("PSUM reuse before consumption", ap.tensor.name, lo, hi, iv)
                        pend.append((lo, hi))

    def _deps(self, op, reads, writes):
        deps = []
        for kind, aps in (("w", writes), ("r", reads)):
            for ap in aps:
                key, p0, p1, lo, hi = self.region(ap)
                lst = self.recs.setdefault(key, [])
                keep = []
                for r in lst:
                    rp0, rp1, rlo, rhi, rkind, rop = r
                    if rp0 < p1 and p0 < rp1 and rlo < hi and lo < rhi:
                        if kind == "r":
                            if rkind == "w":
                                deps.append((rop, "raw"))
                            elif key[0] == "p" and rop.eng != op.eng:
                                deps.append((rop, "rar"))
                            keep.append(r)
                        else:
                            deps.append((rop, "waw" if rkind == "w" else "war"))
                            if p0 <= rp0 and rp1 <= p1 and lo <= rlo and rhi <= hi:
                                continue
                            keep.append(r)
                    else:
                        keep.append(r)
                if kind == "r":
                    keep = [r for r in keep if not (r[4] == "r" and r[5].semkey == op.semkey and not r[5].is_dma
                                                    and not op.is_dma and p0 <= r[0] and r[1] <= p1 and lo <= r[2] and r[3] <= hi)]
                keep.append((p0, p1, lo, hi, kind, op))
                self.recs[key] = keep
        return deps

    def _add_waits(self, op, deps):
        w = self.waited[op.eng]
        for rop, kind in deps:
            if rop is op:
                continue
            if not rop.is_dma and not op.is_dma and rop.eng == op.eng:
                if op.eng == "pe":
                    continue
            if w.get(rop.semkey, -1) >= rop.seq:
                continue
            w[rop.semkey] = rop.seq
            rop.needs_inc = True
            op.waits.append(rop)

    def op(self, eng, fn, reads=(), writes=(), start=True):
        o = Op()
        o.eng, o.fn, o.waits, o.semkey, o.is_dma = eng, fn, [], eng, False
        o.seq = len(self.ops[eng])
        o.needs_inc, o.value = False, None
        self._psum_guard(o, reads, writes, start)
        self._add_waits(o, self._deps(o, reads, writes))
        self.ops[eng].append(o)
        return o

    def dma(self, q, sem, fn, reads=(), writes=()):
        o = Op()
        o.eng, o.fn, o.waits, o.semkey, o.is_dma = q, fn, [], "dma:" + sem, True
        o.seq = self.semseq.get(sem, 0)
        self.semseq[sem] = o.seq + 1
        o.needs_inc, o.value = True, 16 * (o.seq + 1)
        deps = self._deps(o, reads, writes)
        prev = self.lastdma.get(sem)
        if prev is not None:
            deps.append((prev, "raw"))
        self.lastdma[sem] = o
        self._add_waits(o, deps)
        self.ops[q].append(o)
        return o

    def wait_all(self, eng, oplist):
        o = Op()
        o.eng, o.fn, o.waits, o.semkey, o.is_dma = eng, None, [], eng, False
        o.seq = len(self.ops[eng])
        o.needs_inc, o.value = False, None
        self._add_waits(o, [(x, "raw") for x in oplist])
        self.ops[eng].append(o)

    def emit(self):
        nc = self.nc
        for e in self.ENGS:
            cnt = 0
            for o in self.ops[e]:
                if o.is_dma:
                    continue
                if o.needs_inc:
                    cnt += 1
                    o.value = cnt
        with contextlib.ExitStack() as es:
            sems = {}
            for e in self.ENGS:
                sems[e] = es.enter_context(nc.semaphore("s_" + e))
            for s in self.semseq:
                sems["dma:" + s] = es.enter_context(nc.semaphore("d_" + s))
            block = es.enter_context(nc.Block())

            def run(ename):
                def body(eng):
                    for o in self.ops[ename]:
                        for w in o.waits:
                            eng.wait_ge(sems[w.semkey], w.value)
                        if o.fn is None:
                            continue
                        ins = o.fn(eng)
                        if o.is_dma:
                            ins.then_inc(sems[o.semkey], 16)
                        elif o.needs_inc:
                            ins.then_inc(sems[o.semkey], 1)
                return body

            block.tensor(run("pe"))
            block.scalar(run("act"))
            block.vector(run("dve"))
            block.gpsimd(run("pool"))
            block.sync(run("sp"))


def _build(n_tiles, with_sample, seq, order):
    nc = bass.Bass("TRN2", target_bir_lowering=False)
    P = Prog(nc)

    def din(name, shape, dt=F32):
        return nc.dram_tensor(name, list(shape), dt, kind="ExternalInput").ap()

    def dout(name, shape, dt=F32):
        return nc.dram_tensor(name, list(shape), dt, kind="ExternalOutput").ap()

    x_d = din("x", [seq, D])
    xs_d = din("xs", [DEC, D])
    tape32 = din("tape32", [NPIECE, 128, PIECE])
    ident_d = din("ident", [128, 128])
    ones_d = din("ones", [128, 128])
    dist_d = din("dist", [128, 256])
    pcol_d = din("pcol", [128, NCOL])
    lntab_d = din("lntab", [4, 128, 2 * D])
    gates_d = din("gates", [128, 8 * 128])
    sinkb_d = din("sinkb", [128, NHEAD])
    st_rc_d = din("st_rc", [128, 12])
    st_cf_d = din("st_cf", [128, 120])
    st_h_d = din("st_h", [128, 4])
    st_kT_d = din("st_kT", [128, 512])
    st_v_d = din("st_v", [128, 256])
    ck_d = din("ck", [128, 256])
    cv_d = din("cv", [128, 256])

    y_d = dout("y", [seq, D])
    ys_d = dout("ys", [DEC, D])
    oh_d = [dout("o_h_p", [4, 128]), dout("o_h_s", [4, 128])]
    orc_d = [dout("o_rc_p", [3, 512]), dout("o_rc_s", [3, 512])]
    ocf_d = [dout("o_cf_p", [30, 512]), dout("o_cf_s", [30, 512])]
    ok_d = [dout("o_k_p", [128, 256]), dout("o_k_s", [128, 256])]
    ov_d = [dout("o_v_p", [128, 256]), dout("o_v_s", [128, 256])]

    tape16 = nc.dram_tensor("tape16", [NPIECE, 128, PIECE], BF16, kind="Internal").ap()

    cur = [SB_BASE]

    def sb(name, shape, dt, at=None):
        esz = mybir.dt.size(dt)
        n = 1
        for s in shape[1:]:
            n *= s
        nbytes = n * esz
        if at is None:
            off = (cur[0] + 63) // 64 * 64
            cur[0] = off + nbytes
        else:
            off = at
        t = nc.alloc_sbuf_tensor_at(name, list(shape), dt, offset=off)
        P.sbuf_addr[t.name] = off
        return t, off

    ring_off = []
    rv_flat, rv_8x512, rv_conv, rv_gu, rv_wd = [], [], [], [], []
    for s in range(RING):
        t, off = sb(f"ring{s}", [128, PIECE], BF16)
        ring_off.append(off)
        rv_flat.append(t)
        rv_8x512.append(t[:, :].rearrange("p (k n) -> p k n", n=512))
        rv_conv.append(t[:, :].rearrange("p (k n) -> p k n", n=128))
        rv_gu.append(t[:, :].rearrange("p (t k n) -> p t k n", t=2, k=8))
        rv_wd.append(t[:, :].rearrange("p (j n) -> p j n", n=1024))
    xt = sb("xt", [128, 4, D], F32)[0]
    xT = sb("xT", [128, 8, 512], BF16)[0]
    xin = sb("xin", [128, 4, D], F32)[0]
    lnt = sb("lnt", [128, 2 * D], F32)[0]
    ident = sb("identt", [128, 128], F32)[0]
    ones = sb("onest", [128, 128], F32)[0]
    dist = sb("distt", [128, 256], F32)[0]
    pcol = sb("pcolt", [128, NCOL], F32)[0]
    gates32 = None
    gatesb = sb("gatesb", [128, 8, 128], BF16)[0]
    sinkb = sb("sinkbt", [128, NHEAD], F32)[0]
    identb = sb("identb", [128, 128], BF16)[0]
    smA = sb("smA", [128, 2, 32], F32)[0]
    xnb1 = sb("xnb", [128, D], BF16)[0]
    xnb = [xnb1, xnb1]
    negsink = sb("negsink", [128, NHEAD], F32)[0]
    cpv = sb("cpv", [128, 8], F32)[0]
    sm = sb("sm", [128, 128], F32)[0]
    xr_buf = sb("xr_buf", [128, 4, 3 + 512], F32)[0]
    g_buf = sb("g_buf", [128, 4, 30 + 512], BF16)[0]
    g32 = sb("g32", [128, 4, 30], F32)[0]
    hstate = sb("hstate", [128, 4], F32)[0]
    kT = sb("kT", [128, 4, 128 + 512], BF16)[0]
    vbuf = sb("vbuf", [128, 5, 256], BF16)[0]
    XB = (cur[0] + 63) // 64 * 64
    cur[0] = XB
    gy = sb("gy", [128, 4, 512], F32)[0]
    sg = sb("sg", [128, 512], F32)[0]
    xc2 = [sb(f"xc{i}", [128, 512], F32)[0] for i in range(2)]
    rr2 = [sb(f"rr{i}", [128, 512], F32)[0] for i in range(2)]
    ii2 = [sb(f"ii{i}", [128, 512], F32)[0] for i in range(2)]
    a22 = [sb(f"a2{i}", [128, 512], F32)[0] for i in range(2)]
    xcb2 = [sb(f"xcb{i}", [128, 512], BF16)[0] for i in range(2)]
    ro = sb("ro", [128, 4, 512], BF16)[0]
    cc = sb("cc", [128, 4, 512], F32)[0]
    sq = sb("sq", [128, 512], F32)[0]
    sq2 = [sq, sg]
    mean, var, tt = xc2[0], rr2[0], ii2[0]
    cn = gy[:, :, :].bitcast(BF16).rearrange("p c n -> p (c n)")[:, 0:2048].rearrange("p (c n) -> p c n", n=512)
    XE = cur[0]
    cur[0] = XB
    qT = sb("qT", [128, 8, 512], BF16)[0]
    oT = sb("oT", [128, 8, 512], BF16)[0]
    sbb = sb("sbb", [128, 8, 256], F32)[0]
    pbf = [sb(f"pbf{i}", [128, 8, 256], BF16)[0] for i in range(2)]
    pT = [sb(f"pT{i}", [128, 2, 128], BF16)[0] for i in range(2)]
    otok1 = sb("otok", [128, D], BF16)[0]
    otok = [otok1, otok1]
    stg = sb("stg", [128, 512], F32)[0]
    sm_rc = sb("sm_rc", [128, 512], F32)[0]
    sm_cf = sm_rc
    kvo = sb("kvo", [128, 512], F32)[0]
    XE = max(cur[0], XE)
    cur[0] = XE
    hT = sb("hT", [128, NJ, 512], BF16)[0]
    xTa = hT[:, 14:22, :]
    sgt1 = sb("sgt", [128, 512], F32)[0]
    sgt = [sgt1, sgt1]
    assert cur[0] <= 229344, cur[0]

    ps = [nc.alloc_psum_tensor(f"ps{i}", [128, 512], F32) for i in range(8)]
    bank = [0]

    def nbank():
        b = bank[0]
        bank[0] = (b + 1) % 6
        return ps[b]

    def mm(out, lhsT, rhs, start, stop):
        P.op("pe", lambda e: e.matmul(out, lhsT, rhs, start=start, stop=stop), [lhsT, rhs], [out], start=start)

    def tr(out, in_, n):
        idn = ident[0:n, 0:n]
        P.op("pe", lambda e: e.transpose(out, in_, idn), [in_, idn], [out])

    def trb(out, in_, n):
        idn = identb[0:n, 0:n]
        P.op("pe", lambda e: e.transpose(out, in_, idn), [in_, idn], [out])

    def act(out, in_, func, bias=None, scale=None, accum=None):
        rd = [in_]
        kw = {}
        if bias is not None:
            kw["bias"] = bias
            if not isinstance(bias, float):
                rd.append(bias)
        if scale is not None:
            kw["scale"] = scale
            if not isinstance(scale, float):
                rd.append(scale)
        wr = [out]
        if accum is not None:
            kw["accum_out"] = accum
            wr.append(accum)
        P.op("act", lambda e: e.activation(out, in_, func, **kw), rd, wr)

    def tt_op(eng, out, a, b, op):
        P.op(eng, lambda e: e.tensor_tensor(out, a, b, op), [a, b], [out])

    def ts_op(eng, out, a, s1, s2, op0, op1=None):
        rd = [a] + [s for s in (s1, s2) if s is not None and not isinstance(s, float)]
        if op1 is None:
            P.op(eng, lambda e: e.tensor_scalar(out, a, s1, None, op0), rd, [out])
        else:
            P.op(eng, lambda e: e.tensor_scalar(out, a, s1, s2, op0, op1), rd, [out])

    def stt(out, a, s, b, op0, op1):
        rd = [a, b] + ([] if isinstance(s, float) else [s])
        P.op("dve", lambda e: e.scalar_tensor_tensor(out, a, s, b, op0, op1), rd, [out])

    def cp(eng, out, in_):
        if eng == "act":
            act(out, in_, AF.Copy)
        else:
            P.op(eng, lambda e: e.tensor_copy(out, in_), [in_], [out])

    def dma(q, sem, out, in_, **kw):
        return P.dma(q, sem, lambda e: e.dma_start(out=out, in_=in_, **kw), [in_], [out])

    for i in range(NPIECE):
        src = tape32[i].rearrange("p (a b) -> p a b", b=2048)
        dst = tape16[i].rearrange("p (a b) -> p a b", b=2048)
        dma("pool", f"cv{i % 4}", dst, src)

    seq_rec = []
    nload = [0]
    held = set()
    curpos = {}

    def use_pieces(tile, first, last):
        ids = list(range(first, last + 1))
        p0 = len(seq_rec)
        for i, pid in enumerate(ids):
            curpos[pid] = p0 + i
            seq_rec.append(pid)
        p1 = p0 + len(ids) - 1
        if order is not None:
            assert order[p0:p1 + 1] == ids, (order[p0:p1 + 1], ids)
            base = min([p0] + list(held))
            lim = min(max(p1, base + RING - 1), len(order) - 1)
        else:
            base = min([p0] + list(held))
            lim = p1
        assert p1 - base < RING, (p1, base)
        while nload[0] <= lim:
            g = nload[0]
            pid = order[g] if order is not None else seq_rec[g]
            dma("sp", f"ring{g % RING}", rv_flat[g % RING][:, :], tape16[pid])
            nload[0] += 1

    def slot(tile, piece):
        return curpos[piece] % RING

    dma("act", "c0", ident[:, :], ident_d)
    dma("act", "c1", ones[:, :], ones_d)
    dma("act", "c2", dist[:, :], dist_d)
    dma("act", "c3", pcol[:, :], pcol_d)
    dma("act", "c0", sinkb[:, :], sinkb_d)
    gst = cc
    dma("act", "c1", gst[:, 0:2, :].rearrange("p a b -> p (a b)"), gates_d)
    P.op("dve", lambda e: e.tensor_copy(gatesb[:, :, :].rearrange("p a b -> p (a b)"),
                                        gst[:, 0:2, :].rearrange("p a b -> p (a b)")),
         [gst[:, 0:2, :]], [gatesb[:, :, :]])
    ts_op("dve", negsink[:, :], sinkb[:, :], -1.0, None, ALU.mult)
    cp("dve", identb[:, :], ident[:, :])
    lam = pcol[:, C_LAM:C_LAM + 4]
    s_abs, s_y, s_z, s_z2, s_p, s_m = (sm[:, 4 * i:4 * i + 4] for i in range(6))
    ts_op("dve", s_m, lam, -1.0, None, ALU.mult)
    tt_op("dve", s_abs, lam, s_m, ALU.max)
    act(s_y, s_abs, AF.Exp, scale=-1.0)
    ts_op("dve", s_z, s_y, 2.0, None, ALU.add)
    P.op("dve", lambda e: e.reciprocal(s_z, s_z), [s_z], [s_z])
    tt_op("dve", s_z, s_z, s_y, ALU.mult)
    tt_op("dve", s_z2, s_z, s_z, ALU.mult)
    ts_op("dve", s_p, s_z2, 1.0 / 9.0, 1.0 / 7.0, ALU.mult, ALU.add)
    for cst in (1.0 / 5.0, 1.0 / 3.0, 1.0):
        tt_op("dve", s_p, s_p, s_z2, ALU.mult)
        ts_op("dve", s_p, s_p, cst, None, ALU.add)
    tt_op("dve", s_p, s_p, s_z, ALU.mult)
    ts_op("dve", s_m, s_m, 0.0, None, ALU.max)
    stt(s_p, s_p, 2.0, s_m, ALU.mult, ALU.add)
    ts_op("dve", cpv[:, 0:4], s_p, -8.0, None, ALU.mult)
    ts_op("dve", cpv[:, 4:8], s_p, -16.0, None, ALU.mult)

    final_ops = []

    def chk(k):
        pass

    def proj_ln(tile, N, nk, src, wrhs, ln_idx, ydst, res=None, tail_jobs=()):
        PT = min(N, 128)
        NB = (N + 127) // 128
        if res is None:
            res = xt
        dma("pool", "lnt", lnt[:, :], lntab_d[ln_idx])
        pbs = {}

        def mmphase(nb):
            pb = [nbank(), nbank()]
            for half in range(2):
                for k in range(nk):
                    mm(pb[half][0:PT, :], src(k)[:, nb * 128:nb * 128 + PT], wrhs(k, half), k == 0, k == nk - 1)
            pbs[nb] = pb

        def post_a(nb):
            pb = pbs[nb]
            for half in range(2):
                xs_ = xt[0:PT, nb, half * 512:(half + 1) * 512]
                rs_ = res[0:PT, nb, half * 512:(half + 1) * 512]
                stt(xs_, rs_, ALPHA, pb[half][0:PT, :], ALU.mult, ALU.add)

        def post(nb, do_a=True):
            if do_a:
                post_a(nb)
            so = 64 + 16 * (nb % 2)
            st = sm[0:PT, so:so + 12]
            mv = sm[0:PT, so + 12:so + 14]
            rs = sm[0:PT, so + 14:so + 15]
            nmr = sm[0:PT, so + 15:so + 16]
            for half in range(2):
                o_ = sm[0:PT, so + 6 * half:so + 6 + 6 * half]
                i_ = xt[0:PT, nb, half * 512:(half + 1) * 512]
                P.op("dve", lambda e, o_=o_, i_=i_: e.bn_stats(o_, i_), [i_], [o_])
            P.op("dve", lambda e: e.bn_aggr(mv, st), [st], [mv])
            act(rs, sm[0:PT, so + 13:so + 14], AF.Sqrt, bias=epsb[0:PT, :])
            P.op("dve", lambda e: e.reciprocal(rs, rs), [rs], [rs])
            stt(nmr, sm[0:PT, so + 12:so + 13], -1.0, rs, ALU.mult, ALU.mult)
            row = xt[0:PT, nb, :]
            if ydst is None:
                xb_ = xnb[nb % 2]
                act(xb_[0:PT, :], row, AF.Identity, bias=nmr, scale=rs)
                gcol = pcol[:, C_LNG + 8 * ln_idx:C_LNG + 8 * ln_idx + 8]
                bcol = pcol[:, C_LNB + 8 * ln_idx:C_LNB + 8 * ln_idx + 8]
                tbb = nbank()[:, :].bitcast(BF16)
                for kc in range(8):
                    trb(tbb[:, kc * 128:kc * 128 + PT], xb_[0:PT, kc * 128:(kc + 1) * 128], PT)
                for kc in range(8):
                    ts_op("dve", xT[:, kc, nb * 128:nb * 128 + PT], tbb[:, kc * 128:kc * 128 + PT],
                          gcol[:, kc:kc + 1], bcol[:, kc:kc + 1], ALU.mult, ALU.add)
            if ydst is not None:
                act(row, row, AF.Identity, bias=nmr, scale=rs)
                tt_op("dve", row, row, lnt[0:PT, 0:D], ALU.mult)
                tt_op("pool", row, row, lnt[0:PT, D:2 * D], ALU.add)
            else:
                ts_op("pool", row, row, rs, nmr, ALU.mult, ALU.add)
                tt_op("pool", row, row, lnt[0:PT, 0:D], ALU.mult)
                tt_op("pool", row, row, lnt[0:PT, D:2 * D], ALU.add)
            if ydst is not None:
                final_ops.append(dma("act", "yout", ydst[nb * 128:nb * 128 + PT, :], row))

        for nb in range(NB):
            mmphase(nb)
            if nb > 0:
                post(nb - 1)
        post_a(NB - 1)
        for job in tail_jobs:
            job()
        post(NB - 1, do_a=False)

    def ffn_partA(tile, N, layer, holder):
        pg = PC_GU0 if layer == 0 else PC_GU1
        use_pieces(tile, pg, pg)
        held.add(curpos[pg])
        V = rv_gu[slot(tile, pg)]
        pre = {}
        for sub in range(2):
            pre[sub] = (nbank(), nbank())
        for sub in range(2):
            for t_ in range(2):
                for kc in range(8):
                    mm(pre[sub][t_][:, 0:384], V[:, t_, kc, sub * 128:(sub + 1) * 128], xT[:, kc, 0:384], kc == 0, kc == 7)
        holder["pre"] = pre

    def ffn(tile, N, layer, ydst, side=(), holder=None, tail_jobs=()):
        pg = PC_GU0 if layer == 0 else PC_GU1
        pd = PC_WD0 if layer == 0 else PC_WD1
        side = list(side)
        iters = 22
        for jj in range(11):
            pre = {}
            if jj == 0 and holder is not None and "pre" in holder:
                pre = holder["pre"]
                gpos = curpos[pg]
                held.discard(gpos)
                V = rv_gu[slot(tile, pg)]
                for sub in range(2):
                    for t_ in range(2):
                        for kc in range(8):
                            mm(pre[sub][t_][:, 384:512], V[:, t_, kc, sub * 128:(sub + 1) * 128], xT[:, kc, 384:512],
                               kc == 0, kc == 7)
            else:
                use_pieces(tile, pg + jj, pg + jj)
                gpos = curpos[pg + jj]
                V = rv_gu[slot(tile, pg + jj)]
            for sub in range(2):
                j = jj * 2 + sub
                if sub in pre:
                    bg, bu = pre[sub]
                else:
                    bg, bu = nbank(), nbank()
                    for kc in range(8):
                        mm(bg[:, 0:N], V[:, 0, kc, sub * 128:(sub + 1) * 128], xT[:, kc, 0:N], kc == 0, kc == 7)
                    for kc in range(8):
                        mm(bu[:, 0:N], V[:, 1, kc, sub * 128:(sub + 1) * 128], xT[:, kc, 0:N], kc == 0, kc == 7)
                s_ = sgt[j % 2]
                act(s_[:, 0:N], bg[:, 0:N], AF.Silu)
                tt_op("dve", hT[:, j, 0:N], s_[:, 0:N], bu[:, 0:N], ALU.mult)
                if side and not (pre and sub == 0):
                    held.add(gpos)
                    n = -(-len(side) // (iters - j))
                    for _ in range(n):
                        side.pop(0)()
                    held.discard(gpos)
        for job in side:
            job()
        use_pieces(tile, pd, pd + 5)
        proj_ln(tile, N, NJ, lambda j: hT[:, j, :],
                lambda j, half: rv_wd[slot(tile, pd + j // 4)][:, j % 4, half * 512:(half + 1) * 512],
                2 * layer + 1, ydst, tail_jobs=tail_jobs)

    def stageA_jobs(tile, N, is_last, pre=()):
        jobs = list(pre)
        jobs.append(lambda: load_x_tr(N))

        def j_xr(c):
            if c == 0:
                use_pieces(tile, PC_WIN, PC_WIN)
                held.add(curpos[PC_WIN])
            V = rv_8x512[slot(tile, PC_WIN)]
            b = nbank()
            for kc in range(8):
                mm(b[:, 0:N], V[:, kc, c * 128:(c + 1) * 128], xTa[:, kc, 0:N], kc == 0, kc == 7)
            act(xr_buf[:, c, 3:3 + N], b[:, 0:N], AF.Copy)
            if c == 3:
                held.discard(curpos[PC_WIN])

        def j_g(c):
            if c == 0:
                use_pieces(tile, PC_WIN + 1, PC_WIN + 2)
                held.add(curpos[PC_WIN + 1])
                held.add(curpos[PC_WIN + 2])
            Vg = rv_8x512[slot(tile, PC_WIN + 1)]
            Vv = rv_8x512[slot(tile, PC_WIN + 2)]
            b1, b2 = nbank(), nbank()
            for kc in range(8):
                mm(b1[:, 0:N], Vg[:, kc, c * 128:(c + 1) * 128], xTa[:, kc, 0:N], kc == 0, kc == 7)
            for kc in range(8):
                mm(b2[:, 0:N], Vv[:, kc, c * 128:(c + 1) * 128], xTa[:, kc, 0:N], kc == 0, kc == 7)
            act(sg[:, 0:N], b1[:, 0:N], AF.Sigmoid)
            tt_op("dve", g_buf[:, c, 30:30 + N], b2[:, 0:N], sg[:, 0:N], ALU.mult)
            if is_last:
                tt_op("dve", g32[:, c, :], b2[:, N - 30:N], sg[:, N - 30:N], ALU.mult)
            if c == 3:
                held.discard(curpos[PC_WIN + 1])
                held.discard(curpos[PC_WIN + 2])

        def j_yr(c):
            if c == 0:
                use_pieces(tile, PC_WIN + 3, PC_WIN + 3)
                held.add(curpos[PC_WIN + 3])
            V = rv_8x512[slot(tile, PC_WIN + 3)]
            b = nbank()
            for kc in range(8):
                mm(b[:, 0:N], V[:, kc, c * 128:(c + 1) * 128], xTa[:, kc, 0:N], kc == 0, kc == 7)
            act(gy[:, c, 0:N], b[:, 0:N], AF.Gelu_apprx_tanh)
            if c == 3:
                held.discard(curpos[PC_WIN + 3])

        for c in range(4):
            jobs.append(lambda c=c: j_xr(c))
        for c in range(4):
            jobs.append(lambda c=c: j_g(c))
        for c in range(4):
            jobs.append(lambda c=c: j_yr(c))

        def rec1(c):
            xc, rr, ii, a2, xcb = xc2[c % 2], rr2[c % 2], ii2[c % 2], a22[c % 2], xcb2[c % 2]
            w = lambda k: pcol[:, C_RCW + 4 * c + k:C_RCW + 4 * c + k + 1]
            ts_op("dve", xc[:, 0:N], xr_buf[:, c, 0:N], w(0), pcol[:, C_RCB + c:C_RCB + c + 1], ALU.mult, ALU.add)
            for k in range(1, 4):
                stt(xc[:, 0:N], xr_buf[:, c, k:k + N], w(k), xc[:, 0:N], ALU.mult, ALU.add)
            cp("pool", xr_buf[:, c, 0:3], xr_buf[:, c, N:N + 3])
            act(xcb[:, 0:N], xc[:, 0:N], AF.Copy)

        def rec1b(c):
            xc, rr, ii, a2, xcb = xc2[c % 2], rr2[c % 2], ii2[c % 2], a22[c % 2], xcb2[c % 2]
            b1, b2 = nbank(), nbank()
            mm(b1[:, 0:N], gatesb[:, c, :], xcb[:, 0:N], True, True)
            mm(b2[:, 0:N], gatesb[:, 4 + c, :], xcb[:, 0:N], True, True)
            act(rr[:, 0:N], b1[:, 0:N], AF.Sigmoid, bias=pcol[:, C_GAB + c:C_GAB + c + 1])
            act(ii[:, 0:N], b2[:, 0:N], AF.Sigmoid, bias=pcol[:, C_GXB + c:C_GXB + c + 1])
            act(a2[:, 0:N], rr[:, 0:N], AF.Exp, scale=cpv[:, 4 + c:5 + c])
            act(rr[:, 0:N], rr[:, 0:N], AF.Exp, scale=cpv[:, c:c + 1])
            ts_op("pool", a2[:, 0:N], a2[:, 0:N], 1.0, 0.0, ALU.min, ALU.max)
            act(a2[:, 0:N], a2[:, 0:N], AF.Sqrt, bias=oneb[:, :], scale=-1.0)

        def rec2(c):
            xc, rr, ii, a2 = xc2[c % 2], rr2[c % 2], ii2[c % 2], a22[c % 2]
            tt_op("dve", a2[:, 0:N], a2[:, 0:N], ii[:, 0:N], ALU.mult)
            tt_op("dve", a2[:, 0:N], a2[:, 0:N], xc[:, 0:N], ALU.mult)
            hi_, da_, du_, h0_ = ii[:, 0:N], rr[:, 0:N], a2[:, 0:N], hstate[:, c:c + 1]
            P.op("dve", lambda e, hi_=hi_, da_=da_, du_=du_, h0_=h0_: e.tensor_tensor_scan(hi_, da_, du_, h0_, ALU.mult, ALU.add),
                 [da_, du_, h0_], [hi_])
            cp("dve", hstate[:, c:c + 1], ii[:, N - 1:N])
            tt_op("pool", ro[:, c, 0:N], ii[:, 0:N], gy[:, c, 0:N], ALU.mult)

        s1, s2 = ps[6], ps[7]

        def convpe(c):
            use_pieces(tile, PC_CONV + c, PC_CONV + c)
            V = rv_conv[slot(tile, PC_CONV + c)]
            b = nbank()
            for k in range(31):
                mm(b[:, 0:N], V[:, k, :], g_buf[:, c, k:k + N], k == 0, k == 30)
            cp("pool", g_buf[:, c, 0:30], g_buf[:, c, N:N + 30])
            if c > 0:
                stats(c - 1)
            act(cc[:, c, 0:N], b[:, 0:N], AF.Identity, bias=pcol[:, C_CFB + c:C_CFB + c + 1])
            act(sq2[c % 2][:, 0:N], cc[:, c, 0:N], AF.Square)

        def stats(c):
            mm(s1[:, 0:N], ones[:, :], cc[:, c, 0:N], c == 0, c == 3)
            mm(s2[:, 0:N], ones[:, :], sq2[c % 2][:, 0:N], c == 0, c == 3)

        for f, c in ((rec1, 0), (convpe, 0), (rec1b, 0), (rec1, 1), (rec2, 0), (convpe, 1), (rec1b, 1), (rec1, 2),
                     (rec2, 1), (convpe, 2), (rec1b, 2), (rec1, 3), (rec2, 2), (convpe, 3), (rec1b, 3), (rec2, 3)):
            jobs.append(lambda f=f, c=c: f(c))

        def j_cln():
            stats(3)
            ts_op("dve", mean[:, 0:N], s1[:, 0:N], 1.0 / 512.0, None, ALU.mult)
            tt_op("dve", var[:, 0:N], mean[:, 0:N], mean[:, 0:N], ALU.mult)
            stt(var[:, 0:N], s2[:, 0:N], 1.0 / 512.0, var[:, 0:N], ALU.mult, ALU.subtract)
            act(var[:, 0:N], var[:, 0:N], AF.Sqrt, bias=epsb[:, :])
            P.op("dve", lambda e: e.reciprocal(var[:, 0:N], var[:, 0:N]), [var[:, 0:N]], [var[:, 0:N]])
            for c in range(4):
                tt_op("pool", tt[:, 0:N], cc[:, c, 0:N], mean[:, 0:N], ALU.subtract)
                tt_op("dve", tt[:, 0:N], tt[:, 0:N], var[:, 0:N], ALU.mult)
                act(cn[:, c, 0:N], tt[:, 0:N], AF.Silu, bias=pcol[:, C_CFBE + c:C_CFBE + c + 1],
                    scale=pcol[:, C_CFG + c:C_CFG + c + 1])
        jobs.append(j_cln)
        return jobs

    def projA(tile, N, is_last, oi, tail_jobs=()):
        if is_last:
            b = nbank()
            tr(b[0:4, 0:128], hstate[:, 0:4], 128)
            cp("dve", stg[0:4, 0:128], b[0:4, 0:128])
            final_ops.append(dma("act", "so0", oh_d[oi], stg[0:4, 0:128]))
            b = nbank()
            for c in range(4):
                tr(b[0:3, c * 128:(c + 1) * 128], xr_buf[:, c, 0:3], 128)
            cp("dve", sm_rc[0:3, :], b[0:3, :])
            final_ops.append(dma("act", "so1", orc_d[oi], sm_rc[0:3, :]))
            b = nbank()
            for c in range(4):
                tr(b[0:30, c * 128:(c + 1) * 128], g32[:, c, 0:30], 128)
            cp("dve", sm_cf[0:30, :], b[0:30, :])
            final_ops.append(dma("act", "so2", ocf_d[oi], sm_cf[0:30, :]))
        use_pieces(tile, PC_WOUT, PC_WOUT + 1)
        proj_ln(tile, N, 8, lambda k: (ro[:, k, :] if k < 4 else cn[:, k - 4, :]),
                lambda k, half: rv_8x512[slot(tile, PC_WOUT + half)][:, k, :], 0, None, res=xin, tail_jobs=tail_jobs)

    def q_partA(tile, N, holder):
        use_pieces(tile, PC_Q, PC_Q)
        held.add(curpos[PC_Q])
        V = rv_8x512[slot(tile, PC_Q)]
        pre = {}
        for c4 in range(4):
            pre[c4] = nbank()
        for c4 in range(4):
            for kc in range(8):
                mm(pre[c4][:, 0:384], V[:, kc, c4 * 128:(c4 + 1) * 128], xT[:, kc, 0:384], kc == 0, kc == 7)
        holder["pre"] = pre

    def mixer_c(tile, N, is_first, is_last, oi, tail_jobs=(), holder=None):
        PT = min(N, 128)
        NB = (N + 127) // 128
        for hq in range(2):
            pre = {}
            if hq == 0 and holder is not None and "pre" in holder:
                pre = holder["pre"]
                held.discard(curpos[PC_Q])
                V = rv_8x512[slot(tile, PC_Q)]
                for c4 in range(4):
                    for kc in range(8):
                        mm(pre[c4][:, 384:512], V[:, kc, c4 * 128:(c4 + 1) * 128], xT[:, kc, 384:512], kc == 0, kc == 7)
            else:
                use_pieces(tile, PC_Q + hq, PC_Q + hq)
                V = rv_8x512[slot(tile, PC_Q + hq)]
            for c4 in range(4):
                oc = hq * 4 + c4
                if c4 in pre:
                    b = pre[c4]
                else:
                    b = nbank()
                    for kc in range(8):
                        mm(b[:, 0:N], V[:, kc, c4 * 128:(c4 + 1) * 128], xT[:, kc, 0:N], kc == 0, kc == 7)
                act(qT[:, oc, 0:N], b[:, 0:N], AF.Identity, scale=0.125)
        chk(3.01)
        use_pieces(tile, PC_KDUP, PC_KDUP)
        V = rv_8x512[slot(tile, PC_KDUP)]
        for j in range(4):
            b = nbank()
            for kc in range(8):
                mm(b[:, 0:N], V[:, kc, j * 128:(j + 1) * 128], xT[:, kc, 0:N], kc == 0, kc == 7)
            cp("dve", kT[:, j, 128:128 + N], b[:, 0:N])
        chk(3.02)
        use_pieces(tile, PC_KV, PC_KV)
        V = rv_8x512[slot(tile, PC_KV)]
        for nb in range(NB):
            b = nbank()
            for kc in range(8):
                mm(b[0:PT, :], xT[:, kc, nb * 128:nb * 128 + PT], V[:, kc, :], kc == 0, kc == 7)
            act(vbuf[0:PT, 1 + nb, :], b[0:PT, 256:512], AF.Copy)
            if nb == 1:
                chk(3.03)
            if nb == NB - 1:
                chk(3.04)
            if is_last and nb == NB - 1:
                cp("dve", kvo[0:PT, :], b[0:PT, :])
                chk(3.05)
                r0 = 128 - PT
                final_ops.append(dma("act", "so3", ok_d[oi][r0:128, :], kvo[0:PT, 0:256]))
                final_ops.append(dma("act", "so4", ov_d[oi][r0:128, :], kvo[0:PT, 256:512]))
        chk(3.1)
        QN = PT
        units = [(qb, hg) for qb in range(NB) for hg in range(2)]
        info = {}

        def s_phase(ui):
            qb, hg = units[ui]
            par = ui % 2
            blocks = [(qb * 128, 128, qb, 0), (qb * 128 + 128, QN, qb + 1, 128)]
            if is_first and qb == 0:
                blocks = blocks[1:]
            kstart = blocks[0][0]
            d0 = blocks[0][3]
            nk = sum(bk[1] for bk in blocks)
            for hp in range(4):
                bb = [nbank(), nbank()]
                for hh in range(2):
                    h = hg * 8 + hp * 2 + hh
                    oc, half, kv = h // 2, h % 2, h // 4
                    pr = slice(half * 64, half * 64 + 64)
                    mm(bb[hh][0:QN, 0:nk], qT[pr, oc, qb * 128:qb * 128 + QN],
                       kT[pr, kv, kstart:kstart + nk], True, True)
                for hh in range(2):
                    h = hg * 8 + hp * 2 + hh
                    stt(sbb[0:QN, hp * 2 + hh, 0:nk], dist[0:QN, d0:d0 + nk], -SLOPES[h],
                        bb[hh][0:QN, 0:nk], ALU.mult, ALU.add)
            mx = smA[0:QN, par, 0:8]
            negm = smA[0:QN, par, 8:16]
            se = smA[0:QN, par, 16:24]
            rsum = smA[0:QN, par, 24:32]
            sin_ = sbb[0:QN, :, 0:nk]
            P.op("dve", lambda e, sin_=sin_, mx=mx: e.tensor_reduce(mx, sin_, AX.X, ALU.max), [sin_], [mx])
            stt(negm, mx, -1.0, negsink[0:QN, hg * 8:hg * 8 + 8], ALU.mult, ALU.min)
            tt_op("dve", se, negm, sinkb[0:QN, hg * 8:hg * 8 + 8], ALU.add)
            act(se, se, AF.Exp)
            info[ui] = (blocks, kstart, nk)

        def e_phase(ui):
            qb, hg = units[ui]
            par = ui % 2
            blocks, kstart, nk = info[ui]
            for hl in range(8):
                act(pbf[par][0:QN, hl, 0:nk], sbb[0:QN, hl, 0:nk], AF.Exp, bias=smA[0:QN, par, 8 + hl:9 + hl],
                    accum=smA[0:QN, par, 24 + hl:25 + hl])

        def pe_phase(ui):
            qb, hg = units[ui]
            par = ui % 2
            blocks, kstart, nk = info[ui]
            ob = ps[6 + par]
            se = smA[0:QN, par, 16:24]
            tt_op("dve", se, se, smA[0:QN, par, 24:32], ALU.add)
            P.op("dve", lambda e, se=se: e.reciprocal(se, se), [se], [se])
            allfull = all(bk[1] == 128 for bk in blocks) and QN == 128
            def tphase(hl):
                pT_ = pT[hl % 2]
                tbb = nbank()[:, :].bitcast(BF16)
                for bi, (kcol, kn, vblk, dcol) in enumerate(blocks):
                    off = kcol - kstart
                    trb(tbb[0:kn, bi * 128:bi * 128 + QN], pbf[par][0:QN, hl, off:off + kn], QN)
                ev = "act"
                if allfull:
                    nb_ = len(blocks)
                    cp(ev, pT_[:, 0:nb_, :], tbb[:, 0:nb_ * 128].rearrange("p (b q) -> p b q", q=128))
                else:
                    for bi, (kcol, kn, vblk, dcol) in enumerate(blocks):
                        cp(ev, pT_[0:kn, bi, 0:QN], tbb[0:kn, bi * 128:bi * 128 + QN])

            def pvphase(hl):
                h = hg * 8 + hl
                kv = h // 4
                pT_ = pT[hl % 2]
                for bi, (kcol, kn, vblk, dcol) in enumerate(blocks):
                    mm(ob[0:QN, hl * 64:(hl + 1) * 64], pT_[0:kn, bi, 0:QN], vbuf[0:kn, vblk, kv * 64:kv * 64 + 64],
                       bi == 0, bi == len(blocks) - 1)

            tphase(0)
            for hl in range(1, 8):
                tphase(hl)
                pvphase(hl - 1)
            pvphase(7)
            ot = otok[qb % 2]
            rden = smA[0:QN, par, 16:24].unsqueeze(2).broadcast_to([QN, 8, 64])
            tt_op("dve", ot[0:QN, hg * 512:(hg + 1) * 512].rearrange("p (h d) -> p h d", d=64),
                  ob[0:QN, :].rearrange("p (h d) -> p h d", d=64), rden, ALU.mult)
            chk(3.5)
            if hg == 1:
                tbb = nbank()[:, :].bitcast(BF16)
                for oc in range(8):
                    trb(tbb[:, oc * 128:oc * 128 + QN], ot[0:QN, oc * 128:(oc + 1) * 128], QN)
                cp("act", oT[:, :, qb * 128:qb * 128 + QN], tbb[:, :].rearrange("p (c q) -> p c q", q=128)[:, :, 0:QN])

        s_phase(0)
        e_phase(0)
        for ui in range(1, len(units)):
            s_phase(ui)
            pe_phase(ui - 1)
            e_phase(ui)
        pe_phase(len(units) - 1)
        if not is_last:
            cp("pool", kT[:, :, 0:128], kT[:, :, N:N + 128])
            cp("pool", vbuf[:, 0, :], vbuf[:, NB, :])
        use_pieces(tile, PC_WOC, PC_WOC + 1)
        proj_ln(tile, N, 8, lambda k: oT[:, k, :],
                lambda k, half: rv_8x512[slot(tile, PC_WOC + half)][:, k, :], 2, None, tail_jobs=tail_jobs)

    epsb = sb("epsb", [128, 1], F32)[0]
    oneb = sb("oneb", [128, 1], F32)[0]
    sm2 = sb("sm2", [128, 8], F32)[0]
    assert cur[0] <= 229344, cur[0]
    P.op("pool", lambda e: e.memset(epsb[:, :], LN_EPS), [], [epsb[:, :]])
    P.op("pool", lambda e: e.memset(oneb[:, :], 1.0), [], [oneb[:, :]])

    def load_x_dma(src, N):
        PT = min(N, 128)
        NB = (N + 127) // 128
        if NB > 1:
            dma("act", "xin", xin[:, 0:NB, :], src.rearrange("(nb p) d -> p nb d", p=128))
        else:
            dma("act", "xin", xin[0:PT, 0, :], src)

    def load_x_tr(N):
        PT = min(N, 128)
        NB = (N + 127) // 128
        for nb in range(NB):
            xb_ = xnb[nb % 2]
            cp("act" if nb % 2 else "dve", xb_[0:PT, :], xin[0:PT, nb, :])
            tbb = nbank()[:, :].bitcast(BF16)
            for kc in range(8):
                trb(tbb[:, kc * 128:kc * 128 + PT], xb_[0:PT, kc * 128:(kc + 1) * 128], PT)
            cp("dve" if nb % 2 else "act", xTa[:, :, nb * 128:nb * 128 + PT],
               tbb[:, :].rearrange("p (c q) -> p c q", q=128)[:, :, 0:PT])

    P.op("pool", lambda e: e.memset(xr_buf[:, :, 0:3], 0.0), [], [xr_buf[:, :, 0:3]])
    P.op("pool", lambda e: e.memset(g_buf[:, :, 0:30], 0.0), [], [g_buf[:, :, 0:30]])
    P.op("pool", lambda e: e.memset(hstate[:, :], 0.0), [], [hstate[:, :]])
    P.op("pool", lambda e: e.memset(kT[:, :, 0:128], 0.0), [], [kT[:, :, 0:128]])
    P.op("pool", lambda e: e.memset(vbuf[:, 0, :], 0.0), [], [vbuf[:, 0, :]])

    def sample_init():
        dma("act", "c2", xr_buf[:, :, 0:3], st_rc_d.rearrange("p (c k) -> p c k", k=3))
        dma("act", "c3", hstate[:, :], st_h_d)
        dma("act", "c0", cc[:, 0, 0:120], st_cf_d)
        cp("dve", g_buf[:, :, 0:30], cc[:, 0, 0:120].rearrange("p (c k) -> p c k", k=30))
        dma("act", "c1", cc[:, 1, :], st_kT_d)
        cp("dve", kT[:, :, 0:128], cc[:, 1, :].rearrange("p (c k) -> p c k", k=128))
        dma("act", "c2", cc[:, 2, 0:256], st_v_d)
        cp("dve", vbuf[:, 0, :], cc[:, 2, 0:256])
        final_ops.append(dma("act", "so5", ok_d[1][0:64, :], ck_d[64:128, :]))
        final_ops.append(dma("act", "so6", ov_d[1][0:64, :], cv_d[64:128, :]))

    tiles = [(t, 512, 0) for t in range(n_tiles)] + ([(n_tiles, DEC, 1)] if with_sample else [])
    load_x_dma(x_d[0:512, :], 512)
    for job in stageA_jobs(0, 512, n_tiles == 1):
        job()
    for idx, (t, N, oi) in enumerate(tiles):
        is_sample = oi == 1
        last = (t == n_tiles - 1) or is_sample
        h0, hq_, h1 = {}, {}, {}
        split = N == 512
        projA(t, N, last, oi, tail_jobs=[lambda: ffn_partA(t, N, 0, h0)] if split else [])
        nxt = tiles[idx + 1] if idx + 1 < len(tiles) else None
        if nxt is not None:
            nt, nN, noi = nxt
            load_x_dma(xs_d if noi == 1 else x_d[nt * 512:(nt + 1) * 512, :], nN)
        ffn(t, N, 0, None, holder=h0, tail_jobs=[lambda: q_partA(t, N, hq_)] if split else [])
        side = []
        if nxt is not None:
            nt, nN, noi = nxt
            nlast = (nt == n_tiles - 1) or noi == 1
            side = stageA_jobs(nt, nN, nlast, pre=[sample_init] if noi == 1 else [])
        npre = min(len(side), 6 if (nxt is not None and nxt[2] == 1) else 5)
        tj = side[:npre] + ([lambda: ffn_partA(t, N, 1, h1)] if split else [])
        mixer_c(t, N, (t == 0 and not is_sample), last, oi, tail_jobs=tj, holder=hq_)
        ffn(t, N, 1, ys_d if is_sample else y_d[t * 512:(t + 1) * 512, :], side=side[npre:], holder=h1)

    P.wait_all("sp", final_ops + list(P.lastdma.values()))
    P.wait_all("act", final_ops)
    if order is not None:
        P.emit()
    return nc, seq_rec


def build_program(n_tiles=SEQ // 512, with_sample=True, seq=SEQ):
    _, order = _build(n_tiles, with_sample, seq, None)
    nc, order2 = _build(n_tiles, with_sample, seq, list(order))
    assert order2 == order
    return nc


def _pieces(inp):
    f = np.float32
    tape = np.zeros((NPIECE, 128, PIECE), f)

    def kmajor(w, c0, ncol):
        return w[:, c0:c0 + ncol].reshape(8, 128, ncol).transpose(1, 0, 2)

    w_in = inp["w_in_ab"][0]
    for g, c0 in enumerate((0, 1536, 1024, 512)):
        tape[PC_WIN + g] = kmajor(w_in, c0, 512).reshape(128, PIECE)
    cw = inp["cf_conv_w"][0]
    for c in range(4):
        pc = np.zeros((128, 32, 128), f)
        idx = np.arange(128)
        for k in range(31):
            pc[idx, k, idx] = cw[k, c * 128:(c + 1) * 128]
        tape[PC_CONV + c] = pc.reshape(128, PIECE)
    wo = inp["w_out_ab"][0]
    for h in range(2):
        tape[PC_WOUT + h] = kmajor(wo, h * 512, 512).reshape(128, PIECE)
    for layer, (pg, pd) in enumerate(((PC_GU0, PC_WD0), (PC_GU1, PC_WD1))):
        wg, wu, wd = inp["w_ff_gate"][layer], inp["w_ff_up"][layer], inp["w_ff_down"][layer]
        for jj in range(11):
            pc = np.stack([kmajor(wg, jj * 256, 256), kmajor(wu, jj * 256, 256)], axis=1)
            tape[pg + jj] = pc.reshape(128, PIECE)
        wdk = wd.reshape(NJ, 128, D).transpose(1, 0, 2)
        for q in range(6):
            pc = np.zeros((128, 4, D), f)
            n = min(4, NJ - q * 4)
            pc[:, 0:n] = wdk[:, q * 4:q * 4 + n]
            tape[pd + q] = pc.reshape(128, PIECE)
    wqkv = inp["w_qkv"][0]
    for h in range(2):
        tape[PC_Q + h] = kmajor(wqkv, h * 512, 512).reshape(128, PIECE)
    wk = kmajor(wqkv, 1024, 256).reshape(128, 8, 4, 1, 64)
    tape[PC_KDUP] = np.broadcast_to(wk, (128, 8, 4, 2, 64)).reshape(128, PIECE)
    tape[PC_KV] = kmajor(wqkv, 1024, 512).reshape(128, PIECE)
    woc = inp["w_out_c"][0]
    for h in range(2):
        tape[PC_WOC + h] = kmajor(woc, h * 512, 512).reshape(128, PIECE)
    return tape


def _col(v):
    return np.ascontiguousarray(v.reshape(-1, 128).T)


def _shared_inputs(inp):
    f = np.float32
    sh = {}
    sh["tape32"] = _pieces(inp)
    sh["ident"] = np.eye(128, dtype=f)
    sh["ones"] = np.ones((128, 128), f)
    i = np.arange(128)[:, None]
    s = np.arange(256)[None, :]
    dist = np.abs(128 + i - s).astype(f)
    qc = i // 64
    kc = s // 64 - 2
    valid = (kc <= qc) & (kc >= qc - 2)
    dist = np.where(valid, dist, f(1e10)).astype(f)
    sh["dist"] = dist
    pcol = np.zeros((128, NCOL), f)
    rcw = inp["rec_conv_w"][0]
    for c in range(4):
        for k in range(4):
            pcol[:, C_RCW + 4 * c + k] = rcw[k, c * 128:(c + 1) * 128]
    for name, col in (("rec_conv_b", C_RCB), ("rec_gate_a_b", C_GAB), ("rec_gate_x_b", C_GXB), ("rec_lambda", C_LAM),
                      ("cf_conv_b", C_CFB), ("cf_norm_g", C_CFG), ("cf_norm_b", C_CFBE)):
        pcol[:, col:col + 4] = _col(inp[name][0])
    lng = [inp["ln_mix_g"][0], inp["ln_ff_g"][0], inp["ln_mix_g"][1], inp["ln_ff_g"][1]]
    lnb = [inp["ln_mix_b"][0], inp["ln_ff_b"][0], inp["ln_mix_b"][1], inp["ln_ff_b"][1]]
    lntab = np.zeros((4, 128, 2 * D), f)
    for l in range(4):
        pcol[:, C_LNG + 8 * l:C_LNG + 8 * l + 8] = _col(lng[l])
        pcol[:, C_LNB + 8 * l:C_LNB + 8 * l + 8] = _col(lnb[l])
        lntab[l, :, 0:D] = lng[l][None, :]
        lntab[l, :, D:] = lnb[l][None, :]
    sh["pcol"] = pcol
    sh["lntab"] = lntab
    gates = np.zeros((128, 8, 128), f)
    for t_, nm in enumerate(("rec_gate_a_w", "rec_gate_x_w")):
        w = inp[nm][0]
        for c in range(4):
            gates[0:64, 4 * t_ + c, 0:64] = w[2 * c]
            gates[64:128, 4 * t_ + c, 64:128] = w[2 * c + 1]
    sh["gates"] = gates.reshape(128, 1024)
    sh["sinkb"] = np.broadcast_to(inp["attn_sinks"][0][None, :], (128, NHEAD)).astype(f).copy()
    return sh


def _core_inputs(inp, b, seq):
    f = np.float32
    m = {}
    m["x"] = np.ascontiguousarray(inp["x_prompt"][b, :seq])
    m["xs"] = np.ascontiguousarray(inp["x_sample"][b])
    rc = inp["state_rec_conv"][0, b]
    m["st_rc"] = np.ascontiguousarray(rc.reshape(3, 4, 128).transpose(2, 1, 0)).reshape(128, 12)
    cf = inp["state_cf_conv"][0, b]
    m["st_cf"] = np.ascontiguousarray(cf.reshape(30, 4, 128).transpose(2, 1, 0)).reshape(128, 120)
    m["st_h"] = _col(inp["state_rec_h"][0, b])
    ck = inp["cache_k"][0, b]
    kTt = ck.transpose(1, 2, 0)
    kd = np.stack([kTt, kTt], axis=1)
    m["st_kT"] = np.ascontiguousarray(kd.reshape(4, 128, 128).transpose(1, 0, 2)).reshape(128, 512)
    m["st_v"] = np.ascontiguousarray(inp["cache_v"][0, b].reshape(128, 256))
    m["ck"] = np.ascontiguousarray(ck.reshape(128, 256))
    m["cv"] = np.ascontiguousarray(inp["cache_v"][0, b].reshape(128, 256))
    return {k: np.asarray(v, f) for k, v in m.items()}


_PROG_CACHE = {}


def run(inp, ncores=NCORE, seq=SEQ, with_sample=True):
    inp = {k: np.asarray(v) for k, v in inp.items()}
    key = (seq, with_sample)
    if key not in _PROG_CACHE:
        _PROG_CACHE[key] = build_program(seq // 512, with_sample, seq)
    nc = _PROG_CACHE[key]
    sh = _shared_inputs(inp)
    in_maps = []
    for b in range(ncores):
        m = dict(sh)
        m.update(_core_inputs(inp, b, seq))
        in_maps.append(m)
    res = run_bass_kernel_spmd(nc, in_maps, core_ids=list(range(ncores)))
    R = res.results
    f = np.float32

    def st(name, shape):
        return np.stack([np.asarray(R[b][name], f).reshape(shape) for b in range(ncores)])

    y = st("y", (seq, D))
    ys = st("ys", (DEC, D))
    outs = (y, ys,
            st("o_h_p", (512,))[None], st("o_h_s", (512,))[None],
            st("o_rc_p", (3, 512))[None], st("o_rc_s", (3, 512))[None],
            st("o_cf_p", (30, 512))[None], st("o_cf_s", (30, 512))[None],
            st("o_k_p", (128, 4, 64))[None], st("o_k_s", (128, 4, 64))[None],
            st("o_v_p", (128, 4, 64))[None], st("o_v_s", (128, 4, 64))[None])
    return outs


def kernel(**inputs):
    return run(inputs)
```

```python
import contextlib
import numpy as np
import concourse.bass as bass
import concourse.mybir as mybir
from concourse.bass_utils import run_bass_kernel_spmd

F32 = mybir.dt.float32
BF16 = mybir.dt.bfloat16
AF = mybir.ActivationFunctionType
ALU = mybir.AluOpType
AX = mybir.AxisListType

D = 1024
SEQ = 8192
NCORE = 8
DEC = 64
D_FF = 2816
NJ = D_FF // 128
ALPHA = 4.0 ** 0.25
LN_EPS = 1e-5
NHEAD = 16
SLOPES = [2.0 ** (-8.0 * (h + 1) / NHEAD) for h in range(NHEAD)]
RING = 8
PIECE = 4096
SB_BASE = 16512

PC_WIN = 0
PC_CONV = 4
PC_WOUT = 8
PC_GU0 = 10
PC_WD0 = 21
PC_Q = 27
PC_KDUP = 29
PC_KV = 30
PC_WOC = 31
PC_GU1 = 33
PC_WD1 = 44
NPIECE = 50

C_RCW, C_RCB, C_GAB, C_GXB, C_LAM, C_CFB, C_CFG, C_CFBE, C_LNG, C_LNB = 0, 16, 20, 24, 28, 32, 36, 40, 44, 76
NCOL = 108


class Op:
    __slots__ = ("eng", "fn", "waits", "semkey", "seq", "needs_inc", "value", "is_dma")


class Prog:
    ENGS = ["pe", "act", "dve", "pool", "sp"]

    def __init__(self, nc):
        self.nc = nc
        self.ops = {e: [] for e in self.ENGS}
        self.recs = {}
        self.waited = {e: {} for e in self.ENGS}
        self.semseq = {}
        self.lastdma = {}
        self.sbuf_addr = {}
        self.pending = {}

    def region(self, ap):
        t = ap.tensor
        name = t.name
        pat = [(int(s), int(n)) for s, n in ap.ap]
        off = int(ap.offset)
        esz = mybir.dt.size(ap.dtype)
        cls = type(t).__name__
        if cls.startswith("DRam"):
            lo = off
            hi = off + sum((n - 1) * abs(s) for s, n in pat) + 1
            return ("d:" + name, 0, 1, lo * esz, hi * esz)
        pstep = pat[0][0]
        p0 = off // pstep if pstep else 0
        fo = off - p0 * pstep
        p1 = p0 + pat[0][1]
        ext = sum((n - 1) * abs(s) for s, n in pat[1:]) + 1
        if cls.startswith("PSum"):
            return ("p:" + name, 0, 128, 0, 2048)
        base = self.sbuf_addr[name]
        return ("sb", p0, p1, base + fo * esz, base + (fo + ext) * esz)

    def _psum_guard(self, op, reads, writes, start):
        for kind, aps in (("r", reads), ("w", writes)):
            for ap in aps:
                if not type(ap.tensor).__name__.startswith("PSum"):
                    continue
                pat = [(int(s_), int(n)) for s_, n in ap.ap]
                off = int(ap.offset)
                pstep = pat[0][0]
                fo = off - (off // pstep) * pstep if pstep else 0
                ext = sum((n - 1) * abs(s_) for s_, n in pat[1:]) + 1
                esz = mybir.dt.size(ap.dtype)
                lo, hi = fo * esz, (fo + ext) * esz
                pend = self.pending.setdefault(ap.tensor.name, [])
                if kind == "r" and op.eng != "pe":
                    pend[:] = [iv for iv in pend if not (iv[0] < hi and lo < iv[1])]
                elif kind == "w" and op.eng == "pe":
                    if start:
                        for iv in pend:
                            assert not (iv[0] < hi and lo < iv[1]), ("PSUM reuse before consumption", ap.tensor.name, lo, hi, iv)
                        pend.append((lo, hi))

    def _deps(self, op, reads, writes):
        deps = []
        for kind, aps in (("w", writes), ("r", reads)):
            for ap in aps:
                key, p0, p1, lo, hi = self.region(ap)
                lst = self.recs.setdefault(key, [])
                keep = []
                for r in lst:
                    rp0, rp1, rlo, rhi, rkind, rop = r
                    if rp0 < p1 and p0 < rp1 and rlo < hi and lo < rhi:
                        if kind == "r":
                            if rkind == "w":
                                deps.append((rop, "raw"))
                            elif key[0] == "p" and rop.eng != op.eng:
                                deps.append((rop, "rar"))
                            keep.append(r)
                        else:
                            deps.append((rop, "waw" if rkind == "w" else "war"))
                            if p0 <= rp0 and rp1 <= p1 and lo <= rlo and rhi <= hi:
                                continue
                            keep.append(r)
                    else:
                        keep.append(r)
                if kind == "r":
                    keep = [r for r in keep if not (r[4] == "r" and r[5].semkey == op.semkey and not r[5].is_dma
                                                    and not op.is_dma and p0 <= r[0] and r[1] <= p1 and lo <= r[2] and r[3] <= hi)]
                keep.append((p0, p1, lo, hi, kind, op))
                self.recs[key] = keep
        return deps

    def _add_waits(self, op, deps):
        w = self.waited[op.eng]
        for rop, kind in deps:
            if rop is op:
                continue
            if not rop.is_dma and not op.is_dma and rop.eng == op.eng:
                if op.eng == "pe":
                    continue
            if w.get(rop.semkey, -1) >= rop.seq:
                continue
            w[rop.semkey] = rop.seq
            rop.needs_inc = True
            op.waits.append(rop)

    def op(self, eng, fn, reads=(), writes=(), start=True):
        o = Op()
        o.eng, o.fn, o.waits, o.semkey, o.is_dma = eng, fn, [], eng, False
        o.seq = len(self.ops[eng])
        o.needs_inc, o.value = False, None
        self._psum_guard(o, reads, writes, start)
        self._add_waits(o, self._deps(o, reads, writes))
        self.ops[eng].append(o)
        return o

    def dma(self, q, sem, fn, reads=(), writes=()):
        o = Op()
        o.eng, o.fn, o.waits, o.semkey, o.is_dma = q, fn, [], "dma:" + sem, True
        o.seq = self.semseq.get(sem, 0)
        self.semseq[sem] = o.seq + 1
        o.needs_inc, o.value = True, 16 * (o.seq + 1)
        deps = self._deps(o, reads, writes)
        prev = self.lastdma.get(sem)
        if prev is not None:
            deps.append((prev, "raw"))
        self.lastdma[sem] = o
        self._add_waits(o, deps)
        self.ops[q].append(o)
        return o

    def wait_all(self, eng, oplist):
        o = Op()
        o.eng, o.fn, o.waits, o.semkey, o.is_dma = eng, None, [], eng, False
        o.seq = len(self.ops[eng])
        o.needs_inc, o.value = False, None
        self._add_waits(o, [(x, "raw") for x in oplist])
        self.ops[eng].append(o)

    def emit(self):
        nc = self.nc
        for e in self.ENGS:
            cnt = 0
            for o in self.ops[e]:
                if o.is_dma:
                    continue
                if o.needs_inc:
                    cnt += 1
                    o.value = cnt
        with contextlib.ExitStack() as es:
            sems = {}
            for e in self.ENGS:
                sems[e] = es.enter_context(nc.semaphore("s_" + e))
            for s in self.semseq:
                sems["dma:" + s] = es.enter_context(nc.semaphore("d_" + s))
            block = es.enter_context(nc.Block())

            def run(ename):
                def body(eng):
                    for o in self.ops[ename]:
                        for w in o.waits:
                            eng.wait_ge(sems[w.semkey], w.value)
                        if o.fn is None:
                            continue
                        ins = o.fn(eng)
                        if o.is_dma:
                            ins.then_inc(sems[o.semkey], 16)
                        elif o.needs_inc:
                            ins.then_inc(sems[o.semkey], 1)
                return body

            block.tensor(run("pe"))
            block.scalar(run("act"))
            block.vector(run("dve"))
            block.gpsimd(run("pool"))
            block.sync(run("sp"))


def _build(n_tiles, with_sample, seq, order):
    nc = bass.Bass("TRN2", target_bir_lowering=False)
    P = Prog(nc)

    def din(name, shape, dt=F32):
        return nc.dram_tensor(name, list(shape), dt, kind="ExternalInput").ap()

    def dout(name, shape, dt=F32):
        return nc.dram_tensor(name, list(shape), dt, kind="ExternalOutput").ap()

    x_d = din("x", [seq, D])
    xs_d = din("xs", [DEC, D])
    tape32 = din("tape32", [NPIECE, 128, PIECE])
    ident_d = din("ident", [128, 128])
    ones_d = din("ones", [128, 128])
    dist_d = din("dist", [128, 256])
    pcol_d = din("pcol", [128, NCOL])
    lntab_d = din("lntab", [4, 128, 2 * D])
    gates_d = din("gates", [128, 8 * 128])
    sinkb_d = din("sinkb", [128, NHEAD])
    st_rc_d = din("st_rc", [128, 12])
    st_cf_d = din("st_cf", [128, 120])
    st_h_d = din("st_h", [128, 4])
    st_kT_d = din("st_kT", [128, 512])
    st_v_d = din("st_v", [128, 256])
    ck_d = din("ck", [128, 256])
    cv_d = din("cv", [128, 256])

    y_d = dout("y", [seq, D])
    ys_d = dout("ys", [DEC, D])
    oh_d = [dout("o_h_p", [4, 128]), dout("o_h_s", [4, 128])]
    orc_d = [dout("o_rc_p", [3, 512]), dout("o_rc_s", [3, 512])]
    ocf_d = [dout("o_cf_p", [30, 512]), dout("o_cf_s", [30, 512])]
    ok_d = [dout("o_k_p", [128, 256]), dout("o_k_s", [128, 256])]
    ov_d = [dout("o_v_p", [128, 256]), dout("o_v_s", [128, 256])]

    tape16 = nc.dram_tensor("tape16", [NPIECE, 128, PIECE], BF16, kind="Internal").ap()

    cur = [SB_BASE]

    def sb(name, shape, dt, at=None):
        esz = mybir.dt.size(dt)
        n = 1
        for s in shape[1:]:
            n *= s
        nbytes = n * esz
        if at is None:
            off = (cur[0] + 63) // 64 * 64
            cur[0] = off + nbytes
        else:
            off = at
        t = nc.alloc_sbuf_tensor_at(name, list(shape), dt, offset=off)
        P.sbuf_addr[t.name] = off
        return t, off

    ring_off = []
    rv_flat, rv_8x512, rv_conv, rv_gu, rv_wd = [], [], [], [], []
    for s in range(RING):
        t, off = sb(f"ring{s}", [128, PIECE], BF16)
        ring_off.append(off)
        rv_flat.append(t)
        rv_8x512.append(t[:, :].rearrange("p (k n) -> p k n", n=512))
        rv_conv.append(t[:, :].rearrange("p (k n) -> p k n", n=128))
        rv_gu.append(t[:, :].rearrange("p (t k n) -> p t k n", t=2, k=8))
        rv_wd.append(t[:, :].rearrange("p (j n) -> p j n", n=1024))
    xt = sb("xt", [128, 4, D], F32)[0]
    xT = sb("xT", [128, 8, 512], BF16)[0]
    xin = sb("xin", [128, 4, D], F32)[0]
    lnt = sb("lnt", [128, 2 * D], F32)[0]
    ident = sb("identt", [128, 128], F32)[0]
    ones = sb("onest", [128, 128], F32)[0]
    dist = sb("distt", [128, 256], F32)[0]
    pcol = sb("pcolt", [128, NCOL], F32)[0]
    gates32 = None
    gatesb = sb("gatesb", [128, 8, 128], BF16)[0]
    sinkb = sb("sinkbt", [128, NHEAD], F32)[0]
    identb = sb("identb", [128, 128], BF16)[0]
    smA = sb("smA", [128, 2, 32], F32)[0]
    xnb1 = sb("xnb", [128, D], BF16)[0]
    xnb = [xnb1, xnb1]
    negsink = sb("negsink", [128, NHEAD], F32)[0]
    cpv = sb("cpv", [128, 8], F32)[0]
    sm = sb("sm", [128, 128], F32)[0]
    xr_buf = sb("xr_buf", [128, 4, 3 + 512], F32)[0]
    g_buf = sb("g_buf", [128, 4, 30 + 512], BF16)[0]
    g32 = sb("g32", [128, 4, 30], F32)[0]
    hstate = sb("hstate", [128, 4], F32)[0]
    kT = sb("kT", [128, 4, 128 + 512], BF16)[0]
    vbuf = sb("vbuf", [128, 5, 256], BF16)[0]
    XB = (cur[0] + 63) // 64 * 64
    cur[0] = XB
    gy = sb("gy", [128, 4, 512], F32)[0]
    sg = sb("sg", [128, 512], F32)[0]
    xc2 = [sb(f"xc{i}", [128, 512], F32)[0] for i in range(2)]
    rr2 = [sb(f"rr{i}", [128, 512], F32)[0] for i in range(2)]
    ii2 = [sb(f"ii{i}", [128, 512], F32)[0] for i in range(2)]
    a22 = [sb(f"a2{i}", [128, 512], F32)[0] for i in range(2)]
    xcb2 = [sb(f"xcb{i}", [128, 512], BF16)[0] for i in range(2)]
    ro = sb("ro", [128, 4, 512], BF16)[0]
    cc = sb("cc", [128, 4, 512], F32)[0]
    sq = sb("sq", [128, 512], F32)[0]
    sq2 = [sq, sg]
    mean, var, tt = xc2[0], rr2[0], ii2[0]
    cn = gy[:, :, :].bitcast(BF16).rearrange("p c n -> p (c n)")[:, 0:2048].rearrange("p (c n) -> p c n", n=512)
    XE = cur[0]
    cur[0] = XB
    qT = sb("qT", [128, 8, 512], BF16)[0]
    oT = sb("oT", [128, 8, 512], BF16)[0]
    sbb = sb("sbb", [128, 8, 256], F32)[0]
    pbf = [sb(f"pbf{i}", [128, 8, 256], BF16)[0] for i in range(2)]
    pT = [sb(f"pT{i}", [128, 2, 128], BF16)[0] for i in range(2)]
    otok1 = sb("otok", [128, D], BF16)[0]
    otok = [otok1, otok1]
    stg = sb("stg", [128, 512], F32)[0]
    sm_rc = sb("sm_rc", [128, 512], F32)[0]
    sm_cf = sm_rc
    kvo = sb("kvo", [128, 512], F32)[0]
    XE = max(cur[0], XE)
    cur[0] = XE
    hT = sb("hT", [128, NJ, 512], BF16)[0]
    xTa = hT[:, 14:22, :]
    sgt1 = sb("sgt", [128, 512], F32)[0]
    sgt = [sgt1, sgt1]
    assert cur[0] <= 229344, cur[0]

    ps = [nc.alloc_psum_tensor(f"ps{i}", [128, 512], F32) for i in range(8)]
    bank = [0]

    def nbank():
        b = bank[0]
        bank[0] = (b + 1) % 6
        return ps[b]

    def mm(out, lhsT, rhs, start, stop):
        P.op("pe", lambda e: e.matmul(out, lhsT, rhs, start=start, stop=stop), [lhsT, rhs], [out], start=start)

    def tr(out, in_, n):
        idn = ident[0:n, 0:n]
        P.op("pe", lambda e: e.transpose(out, in_, idn), [in_, idn], [out])

    def trb(out, in_, n):
        idn = identb[0:n, 0:n]
        P.op("pe", lambda e: e.transpose(out, in_, idn), [in_, idn], [out])

    def act(out, in_, func, bias=None, scale=None, accum=None):
        rd = [in_]
        kw = {}
        if bias is not None:
            kw["bias"] = bias
            if not isinstance(bias, float):
                rd.append(bias)
        if scale is not None:
            kw["scale"] = scale
            if not isinstance(scale, float):
                rd.append(scale)
        wr = [out]
        if accum is not None:
            kw["accum_out"] = accum
            wr.append(accum)
        P.op("act", lambda e: e.activation(out, in_, func, **kw), rd, wr)

    def tt_op(eng, out, a, b, op):
        P.op(eng, lambda e: e.tensor_tensor(out, a, b, op), [a, b], [out])

    def ts_op(eng, out, a, s1, s2, op0, op1=None):
        rd = [a] + [s for s in (s1, s2) if s is not None and not isinstance(s, float)]
        if op1 is None:
            P.op(eng, lambda e: e.tensor_scalar(out, a, s1, None, op0), rd, [out])
        else:
            P.op(eng, lambda e: e.tensor_scalar(out, a, s1, s2, op0, op1), rd, [out])

    def stt(out, a, s, b, op0, op1):
        rd = [a, b] + ([] if isinstance(s, float) else [s])
        P.op("dve", lambda e: e.scalar_tensor_tensor(out, a, s, b, op0, op1), rd, [out])

    def cp(eng, out, in_):
        if eng == "act":
            act(out, in_, AF.Copy)
        else:
            P.op(eng, lambda e: e.tensor_copy(out, in_), [in_], [out])

    def dma(q, sem, out, in_, **kw):
        return P.dma(q, sem, lambda e: e.dma_start(out=out, in_=in_, **kw), [in_], [out])

    for i in range(NPIECE):
        src = tape32[i].rearrange("p (a b) -> p a b", b=2048)
        dst = tape16[i].rearrange("p (a b) -> p a b", b=2048)
        dma("pool", f"cv{i % 4}", dst, src)

    seq_rec = []
    nload = [0]
    held = set()
    curpos = {}

    def use_pieces(tile, first, last):
        ids = list(range(first, last + 1))
        p0 = len(seq_rec)
        for i, pid in enumerate(ids):
            curpos[pid] = p0 + i
            seq_rec.append(pid)
        p1 = p0 + len(ids) - 1
        if order is not None:
            assert order[p0:p1 + 1] == ids, (order[p0:p1 + 1], ids)
            base = min([p0] + list(held))
            lim = min(max(p1, base + RING - 1), len(order) - 1)
        else:
            base = min([p0] + list(held))
            lim = p1
        assert p1 - base < RING, (p1, base)
        while nload[0] <= lim:
            g = nload[0]
            pid = order[g] if order is not None else seq_rec[g]
            dma("sp", f"ring{g % RING}", rv_flat[g % RING][:, :], tape16[pid])
            nload[0] += 1

    def slot(tile, piece):
        return curpos[piece] % RING

    dma("act", "c0", ident[:, :], ident_d)
    dma("act", "c1", ones[:, :], ones_d)
    dma("act", "c2", dist[:, :], dist_d)
    dma("act", "c3", pcol[:, :], pcol_d)
    dma("act", "c0", sinkb[:, :], sinkb_d)
    gst = cc
    dma("act", "c1", gst[:, 0:2, :].rearrange("p a b -> p (a b)"), gates_d)
    P.op("dve", lambda e: e.tensor_copy(gatesb[:, :, :].rearrange("p a b -> p (a b)"),
                                        gst[:, 0:2, :].rearrange("p a b -> p (a b)")),
         [gst[:, 0:2, :]], [gatesb[:, :, :]])
    ts_op("dve", negsink[:, :], sinkb[:, :], -1.0, None, ALU.mult)
    cp("dve", identb[:, :], ident[:, :])
    lam = pcol[:, C_LAM:C_LAM + 4]
    s_abs, s_y, s_z, s_z2, s_p, s_m = (sm[:, 4 * i:4 * i + 4] for i in range(6))
    ts_op("dve", s_m, lam, -1.0, None, ALU.mult)
    tt_op("dve", s_abs, lam, s_m, ALU.max)
    act(s_y, s_abs, AF.Exp, scale=-1.0)
    ts_op("dve", s_z, s_y, 2.0, None, ALU.add)
    P.op("dve", lambda e: e.reciprocal(s_z, s_z), [s_z], [s_z])
    tt_op("dve", s_z, s_z, s_y, ALU.mult)
    tt_op("dve", s_z2, s_z, s_z, ALU.mult)
    ts_op("dve", s_p, s_z2, 1.0 / 9.0, 1.0 / 7.0, ALU.mult, ALU.add)
    for cst in (1.0 / 5.0, 1.0 / 3.0, 1.0):
        tt_op("dve", s_p, s_p, s_z2, ALU.mult)
        ts_op("dve", s_p, s_p, cst, None, ALU.add)
    tt_op("dve", s_p, s_p, s_z, ALU.mult)
    ts_op("dve", s_m, s_m, 0.0, None, ALU.max)
    stt(s_p, s_p, 2.0, s_m, ALU.mult, ALU.add)
    ts_op("dve", cpv[:, 0:4], s_p, -8.0, None, ALU.mult)
    ts_op("dve", cpv[:, 4:8], s_p, -16.0, None, ALU.mult)

    final_ops = []

    def chk(k):
        pass

    def proj_ln(tile, N, nk, src, wrhs, ln_idx, ydst, res=None, tail_jobs=()):
        PT = min(N, 128)
        NB = (N + 127) // 128
        if res is None:
            res = xt
        dma("pool", "lnt", lnt[:, :], lntab_d[ln_idx])
        pbs = {}

        def mmphase(nb):
            pb = [nbank(), nbank()]
            for half in range(2):
                for k in range(nk):
                    mm(pb[half][0:PT, :], src(k)[:, nb * 128:nb * 128 + PT], wrhs(k, half), k == 0, k == nk - 1)
            pbs[nb] = pb

        def post_a(nb):
            pb = pbs[nb]
            for half in range(2):
                xs_ = xt[0:PT, nb, half * 512:(half + 1) * 512]
                rs_ = res[0:PT, nb, half * 512:(half + 1) * 512]
                stt(xs_, rs_, ALPHA, pb[half][0:PT, :], ALU.mult, ALU.add)

        def post(nb, do_a=True):
            if do_a:
                post_a(nb)
            so = 64 + 16 * (nb % 2)
            st = sm[0:PT, so:so + 12]
            mv = sm[0:PT, so + 12:so + 14]
            rs = sm[0:PT, so + 14:so + 15]
            nmr = sm[0:PT, so + 15:so + 16]
            for half in range(2):
                o_ = sm[0:PT, so + 6 * half:so + 6 + 6 * half]
                i_ = xt[0:PT, nb, half * 512:(half + 1) * 512]
                P.op("dve", lambda e, o_=o_, i_=i_: e.bn_stats(o_, i_), [i_], [o_])
            P.op("dve", lambda e: e.bn_aggr(mv, st), [st], [mv])
            act(rs, sm[0:PT, so + 13:so + 14], AF.Sqrt, bias=epsb[0:PT, :])
            P.op("dve", lambda e: e.reciprocal(rs, rs), [rs], [rs])
            stt(nmr, sm[0:PT, so + 12:so + 13], -1.0, rs, ALU.mult, ALU.mult)
            row = xt[0:PT, nb, :]
            if ydst is None:
                xb_ = xnb[nb % 2]
                act(xb_[0:PT, :], row, AF.Identity, bias=nmr, scale=rs)
                gcol = pcol[:, C_LNG + 8 * ln_idx:C_LNG + 8 * ln_idx + 8]
                bcol = pcol[:, C_LNB + 8 * ln_idx:C_LNB + 8 * ln_idx + 8]
                tbb = nbank()[:, :].bitcast(BF16)
                for kc in range(8):
                    trb(tbb[:, kc * 128:kc * 128 + PT], xb_[0:PT, kc * 128:(kc + 1) * 128], PT)
                for kc in range(8):
                    ts_op("dve", xT[:, kc, nb * 128:nb * 128 + PT], tbb[:, kc * 128:kc * 128 + PT],
                          gcol[:, kc:kc + 1], bcol[:, kc:kc + 1], ALU.mult, ALU.add)
            if ydst is not None:
                act(row, row, AF.Identity, bias=nmr, scale=rs)
                tt_op("dve", row, row, lnt[0:PT, 0:D], ALU.mult)
                tt_op("pool", row, row, lnt[0:PT, D:2 * D], ALU.add)
            else:
                ts_op("pool", row, row, rs, nmr, ALU.mult, ALU.add)
                tt_op("pool", row, row, lnt[0:PT, 0:D], ALU.mult)
                tt_op("pool", row, row, lnt[0:PT, D:2 * D], ALU.add)
            if ydst is not None:
                final_ops.append(dma("act", "yout", ydst[nb * 128:nb * 128 + PT, :], row))

        for nb in range(NB):
            mmphase(nb)
            if nb > 0:
                post(nb - 1)
        post_a(NB - 1)
        for job in tail_jobs:
            job()
        post(NB - 1, do_a=False)

    def ffn_partA(tile, N, layer, holder):
        pg = PC_GU0 if layer == 0 else PC_GU1
        use_pieces(tile, pg, pg)
        held.add(curpos[pg])
        V = rv_gu[slot(tile, pg)]
        pre = {}
        for sub in range(2):
            pre[sub] = (nbank(), nbank())
        for sub in range(2):
            for t_ in range(2):
                for kc in range(8):
                    mm(pre[sub][t_][:, 0:384], V[:, t_, kc, sub * 128:(sub + 1) * 128], xT[:, kc, 0:384], kc == 0, kc == 7)
        holder["pre"] = pre

    def ffn(tile, N, layer, ydst, side=(), holder=None, tail_jobs=()):
        pg = PC_GU0 if layer == 0 else PC_GU1
        pd = PC_WD0 if layer == 0 else PC_WD1
        side = list(side)
        iters = 22
        for jj in range(11):
            pre = {}
            if jj == 0 and holder is not None and "pre" in holder:
                pre = holder["pre"]
                gpos = curpos[pg]
                held.discard(gpos)
                V = rv_gu[slot(tile, pg)]
                for sub in range(2):
                    for t_ in range(2):
                        for kc in range(8):
                            mm(pre[sub][t_][:, 384:512], V[:, t_, kc, sub * 128:(sub + 1) * 128], xT[:, kc, 384:512],
                               kc == 0, kc == 7)
            else:
                use_pieces(tile, pg + jj, pg + jj)
                gpos = curpos[pg + jj]
                V = rv_gu[slot(tile, pg + jj)]
            for sub in range(2):
                j = jj * 2 + sub
                if sub in pre:
                    bg, bu = pre[sub]
                else:
                    bg, bu = nbank(), nbank()
                    for kc in range(8):
                        mm(bg[:, 0:N], V[:, 0, kc, sub * 128:(sub + 1) * 128], xT[:, kc, 0:N], kc == 0, kc == 7)
                    for kc in range(8):
                        mm(bu[:, 0:N], V[:, 1, kc, sub * 128:(sub + 1) * 128], xT[:, kc, 0:N], kc == 0, kc == 7)
                s_ = sgt[j % 2]
                act(s_[:, 0:N], bg[:, 0:N], AF.Silu)
                tt_op("dve", hT[:, j, 0:N], s_[:, 0:N], bu[:, 0:N], ALU.mult)
                if side and not (pre and sub == 0):
                    held.add(gpos)
                    n = -(-len(side) // (iters - j))
                    for _ in range(n):
                        side.pop(0)()
                    held.discard(gpos)
        for job in side:
            job()
        use_pieces(tile, pd, pd + 5)
        proj_ln(tile, N, NJ, lambda j: hT[:, j, :],
                lambda j, half: rv_wd[slot(tile, pd + j // 4)][:, j % 4, half * 512:(half + 1) * 512],
                2 * layer + 1, ydst, tail_jobs=tail_jobs)

    def stageA_jobs(tile, N, is_last, pre=()):
        jobs = list(pre)
        jobs.append(lambda: load_x_tr(N))

        def j_xr(c):
            if c == 0:
                use_pieces(tile, PC_WIN, PC_WIN)
                held.add(curpos[PC_WIN])
            V = rv_8x512[slot(tile, PC_WIN)]
            b = nbank()
            for kc in range(8):
                mm(b[:, 0:N], V[:, kc, c * 128:(c + 1) * 128], xTa[:, kc, 0:N], kc == 0, kc == 7)
            act(xr_buf[:, c, 3:3 + N], b[:, 0:N], AF.Copy)
            if c == 3:
                held.discard(curpos[PC_WIN])

        def j_g(c):
            if c == 0:
                use_pieces(tile, PC_WIN + 1, PC_WIN + 2)
                held.add(curpos[PC_WIN + 1])
                held.add(curpos[PC_WIN + 2])
            Vg = rv_8x512[slot(tile, PC_WIN + 1)]
            Vv = rv_8x512[slot(tile, PC_WIN + 2)]
            b1, b2 = nbank(), nbank()
            for kc in range(8):
                mm(b1[:, 0:N], Vg[:, kc, c * 128:(c + 1) * 128], xTa[:, kc, 0:N], kc == 0, kc == 7)
            for kc in range(8):
                mm(b2[:, 0:N], Vv[:, kc, c * 128:(c + 1) * 128], xTa[:, kc, 0:N], kc == 0, kc == 7)
            act(sg[:, 0:N], b1[:, 0:N], AF.Sigmoid)
            tt_op("dve", g_buf[:, c, 30:30 + N], b2[:, 0:N], sg[:, 0:N], ALU.mult)
            if is_last:
                tt_op("dve", g32[:, c, :], b2[:, N - 30:N], sg[:, N - 30:N], ALU.mult)
            if c == 3:
                held.discard(curpos[PC_WIN + 1])
                held.discard(curpos[PC_WIN + 2])

        def j_yr(c):
            if c == 0:
                use_pieces(tile, PC_WIN + 3, PC_WIN + 3)
                held.add(curpos[PC_WIN + 3])
            V = rv_8x512[slot(tile, PC_WIN + 3)]
            b = nbank()
            for kc in range(8):
                mm(b[:, 0:N], V[:, kc, c * 128:(c + 1) * 128], xTa[:, kc, 0:N], kc == 0, kc == 7)
            act(gy[:, c, 0:N], b[:, 0:N], AF.Gelu_apprx_tanh)
            if c == 3:
                held.discard(curpos[PC_WIN + 3])

        for c in range(4):
            jobs.append(lambda c=c: j_xr(c))
        for c in range(4):
            jobs.append(lambda c=c: j_g(c))
        for c in range(4):
            jobs.append(lambda c=c: j_yr(c))

        def rec1(c):
            xc, rr, ii, a2, xcb = xc2[c % 2], rr2[c % 2], ii2[c % 2], a22[c % 2], xcb2[c % 2]
            w = lambda k: pcol[:, C_RCW + 4 * c + k:C_RCW + 4 * c + k + 1]
            ts_op("dve", xc[:, 0:N], xr_buf[:, c, 0:N], w(0), pcol[:, C_RCB + c:C_RCB + c + 1], ALU.mult, ALU.add)
            for k in range(1, 4):
                stt(xc[:, 0:N], xr_buf[:, c, k:k + N], w(k), xc[:, 0:N], ALU.mult, ALU.add)
            cp("pool", xr_buf[:, c, 0:3], xr_buf[:, c, N:N + 3])
            act(xcb[:, 0:N], xc[:, 0:N], AF.Copy)

        def rec1b(c):
            xc, rr, ii, a2, xcb = xc2[c % 2], rr2[c % 2], ii2[c % 2], a22[c % 2], xcb2[c % 2]
            b1, b2 = nbank(), nbank()
            mm(b1[:, 0:N], gatesb[:, c, :], xcb[:, 0:N], True, True)
            mm(b2[:, 0:N], gatesb[:, 4 + c, :], xcb[:, 0:N], True, True)
            act(rr[:, 0:N], b1[:, 0:N], AF.Sigmoid, bias=pcol[:, C_GAB + c:C_GAB + c + 1])
            act(ii[:, 0:N], b2[:, 0:N], AF.Sigmoid, bias=pcol[:, C_GXB + c:C_GXB + c + 1])
            act(a2[:, 0:N], rr[:, 0:N], AF.Exp, scale=cpv[:, 4 + c:5 + c])
            act(rr[:, 0:N], rr[:, 0:N], AF.Exp, scale=cpv[:, c:c + 1])
            ts_op("pool", a2[:, 0:N], a2[:, 0:N], 1.0, 0.0, ALU.min, ALU.max)
            act(a2[:, 0:N], a2[:, 0:N], AF.Sqrt, bias=oneb[:, :], scale=-1.0)

        def rec2(c):
            xc, rr, ii, a2 = xc2[c % 2], rr2[c % 2], ii2[c % 2], a22[c % 2]
            tt_op("dve", a2[:, 0:N], a2[:, 0:N], ii[:, 0:N], ALU.mult)
            tt_op("dve", a2[:, 0:N], a2[:, 0:N], xc[:, 0:N], ALU.mult)
            hi_, da_, du_, h0_ = ii[:, 0:N], rr[:, 0:N], a2[:, 0:N], hstate[:, c:c + 1]
            P.op("dve", lambda e, hi_=hi_, da_=da_, du_=du_, h0_=h0_: e.tensor_tensor_scan(hi_, da_, du_, h0_, ALU.mult, ALU.add),
                 [da_, du_, h0_], [hi_])
            cp("dve", hstate[:, c:c + 1], ii[:, N - 1:N])
            tt_op("pool", ro[:, c, 0:N], ii[:, 0:N], gy[:, c, 0:N], ALU.mult)

        s1, s2 = ps[6], ps[7]

        def convpe(c):
            use_pieces(tile, PC_CONV + c, PC_CONV + c)
            V = rv_conv[slot(tile, PC_CONV + c)]
            b = nbank()
            for k in range(31):
                mm(b[:, 0:N], V[:, k, :], g_buf[:, c, k:k + N], k == 0, k == 30)
            cp("pool", g_buf[:, c, 0:30], g_buf[:, c, N:N + 30])
            if c > 0:
                stats(c - 1)
            act(cc[:, c, 0:N], b[:, 0:N], AF.Identity, bias=pcol[:, C_CFB + c:C_CFB + c + 1])
            act(sq2[c % 2][:, 0:N], cc[:, c, 0:N], AF.Square)

        def stats(c):
            mm(s1[:, 0:N], ones[:, :], cc[:, c, 0:N], c == 0, c == 3)
            mm(s2[:, 0:N], ones[:, :], sq2[c % 2][:, 0:N], c == 0, c == 3)

        for f, c in ((rec1, 0), (convpe, 0), (rec1b, 0), (rec1, 1), (rec2, 0), (convpe, 1), (rec1b, 1), (rec1, 2),
                     (rec2, 1), (convpe, 2), (rec1b, 2), (rec1, 3), (rec2, 2), (convpe, 3), (rec1b, 3), (rec2, 3)):
            jobs.append(lambda f=f, c=c: f(c))

        def j_cln():
            stats(3)
            ts_op("dve", mean[:, 0:N], s1[:, 0:N], 1.0 / 512.0, None, ALU.mult)
            tt_op("dve", var[:, 0:N], mean[:, 0:N], mean[:, 0:N], ALU.mult)
            stt(var[:, 0:N], s2[:, 0:N], 1.0 / 512.0, var[:, 0:N], ALU.mult, ALU.subtract)
            act(var[:, 0:N], var[:, 0:N], AF.Sqrt, bias=epsb[:, :])
            P.op("dve", lambda e: e.reciprocal(var[:, 0:N], var[:, 0:N]), [var[:, 0:N]], [var[:, 0:N]])
            for c in range(4):
                tt_op("pool", tt[:, 0:N], cc[:, c, 0:N], mean[:, 0:N], ALU.subtract)
                tt_op("dve", tt[:, 0:N], tt[:, 0:N], var[:, 0:N], ALU.mult)
                act(cn[:, c, 0:N], tt[:, 0:N], AF.Silu, bias=pcol[:, C_CFBE + c:C_CFBE + c + 1],
                    scale=pcol[:, C_CFG + c:C_CFG + c + 1])
        jobs.append(j_cln)
        return jobs

    def projA(tile, N, is_last, oi, tail_jobs=()):
        if is_last:
            b = nbank()
            tr(b[0:4, 0:128], hstate[:, 0:4], 128)
            cp("dve", stg[0:4, 0:128], b[0:4, 0:128])
            final_ops.append(dma("act", "so0", oh_d[oi], stg[0:4, 0:128]))
            b = nbank()
            for c in range(4):
                tr(b[0:3, c * 128:(c + 1) * 128], xr_buf[:, c, 0:3], 128)
            cp("dve", sm_rc[0:3, :], b[0:3, :])
            final_ops.append(dma("act", "so1", orc_d[oi], sm_rc[0:3, :]))
            b = nbank()
            for c in range(4):
                tr(b[0:30, c * 128:(c + 1) * 128], g32[:, c, 0:30], 128)
            cp("dve", sm_cf[0:30, :], b[0:30, :])
            final_ops.append(dma("act", "so2", ocf_d[oi], sm_cf[0:30, :]))
        use_pieces(tile, PC_WOUT, PC_WOUT + 1)
        proj_ln(tile, N, 8, lambda k: (ro[:, k, :] if k < 4 else cn[:, k - 4, :]),
                lambda k, half: rv_8x512[slot(tile, PC_WOUT + half)][:, k, :], 0, None, res=xin, tail_jobs=tail_jobs)

    def q_partA(tile, N, holder):
        use_pieces(tile, PC_Q, PC_Q)
        held.add(curpos[PC_Q])
        V = rv_8x512[slot(tile, PC_Q)]
        pre = {}
        for c4 in range(4):
            pre[c4] = nbank()
        for c4 in range(4):
            for kc in range(8):
                mm(pre[c4][:, 0:384], V[:, kc, c4 * 128:(c4 + 1) * 128], xT[:, kc, 0:384], kc == 0, kc == 7)
        holder["pre"] = pre

    def mixer_c(tile, N, is_first, is_last, oi, tail_jobs=(), holder=None):
        PT = min(N, 128)
        NB = (N + 127) // 128
        for hq in range(2):
            pre = {}
            if hq == 0 and holder is not None and "pre" in holder:
                pre = holder["pre"]
                held.discard(curpos[PC_Q])
                V = rv_8x512[slot(tile, PC_Q)]
                for c4 in range(4):
                    for kc in range(8):
                        mm(pre[c4][:, 384:512], V[:, kc, c4 * 128:(c4 + 1) * 128], xT[:, kc, 384:512], kc == 0, kc == 7)
            else:
                use_pieces(tile, PC_Q + hq, PC_Q + hq)
                V = rv_8x512[slot(tile, PC_Q + hq)]
            for c4 in range(4):
                oc = hq * 4 + c4
                if c4 in pre:
                    b = pre[c4]
                else:
                    b = nbank()
                    for kc in range(8):
                        mm(b[:, 0:N], V[:, kc, c4 * 128:(c4 + 1) * 128], xT[:, kc, 0:N], kc == 0, kc == 7)
                act(qT[:, oc, 0:N], b[:, 0:N], AF.Identity, scale=0.125)
        chk(3.01)
        use_pieces(tile, PC_KDUP, PC_KDUP)
        V = rv_8x512[slot(tile, PC_KDUP)]
        for j in range(4):
            b = nbank()
            for kc in range(8):
                mm(b[:, 0:N], V[:, kc, j * 128:(j + 1) * 128], xT[:, kc, 0:N], kc == 0, kc == 7)
            cp("dve", kT[:, j, 128:128 + N], b[:, 0:N])
        chk(3.02)
        use_pieces(tile, PC_KV, PC_KV)
        V = rv_8x512[slot(tile, PC_KV)]
        for nb in range(NB):
            b = nbank()
            for kc in range(8):
                mm(b[0:PT, :], xT[:, kc, nb * 128:nb * 128 + PT], V[:, kc, :], kc == 0, kc == 7)
            act(vbuf[0:PT, 1 + nb, :], b[0:PT, 256:512], AF.Copy)
            if nb == 1:
                chk(3.03)
            if nb == NB - 1:
                chk(3.04)
            if is_last and nb == NB - 1:
                cp("dve", kvo[0:PT, :], b[0:PT, :])
                chk(3.05)
                r0 = 128 - PT
                final_ops.append(dma("act", "so3", ok_d[oi][r0:128, :], kvo[0:PT, 0:256]))
                final_ops.append(dma("act", "so4", ov_d[oi][r0:128, :], kvo[0:PT, 256:512]))
        chk(3.1)
        QN = PT
        units = [(qb, hg) for qb in range(NB) for hg in range(2)]
        info = {}

        def s_phase(ui):
            qb, hg = units[ui]
            par = ui % 2
            blocks = [(qb * 128, 128, qb, 0), (qb * 128 + 128, QN, qb + 1, 128)]
            if is_first and qb == 0:
                blocks = blocks[1:]
            kstart = blocks[0][0]
            d0 = blocks[0][3]
            nk = sum(bk[1] for bk in blocks)
            for hp in range(4):
                bb = [nbank(), nbank()]
                for hh in range(2):
                    h = hg * 8 + hp * 2 + hh
                    oc, half, kv = h // 2, h % 2, h // 4
                    pr = slice(half * 64, half * 64 + 64)
                    mm(bb[hh][0:QN, 0:nk], qT[pr, oc, qb * 128:qb * 128 + QN],
                       kT[pr, kv, kstart:kstart + nk], True, True)
                for hh in range(2):
                    h = hg * 8 + hp * 2 + hh
                    stt(sbb[0:QN, hp * 2 + hh, 0:nk], dist[0:QN, d0:d0 + nk], -SLOPES[h],
                        bb[hh][0:QN, 0:nk], ALU.mult, ALU.add)
            mx = smA[0:QN, par, 0:8]
            negm = smA[0:QN, par, 8:16]
            se = smA[0:QN, par, 16:24]
            rsum = smA[0:QN, par, 24:32]
            sin_ = sbb[0:QN, :, 0:nk]
            P.op("dve", lambda e, sin_=sin_, mx=mx: e.tensor_reduce(mx, sin_, AX.X, ALU.max), [sin_], [mx])
            stt(negm, mx, -1.0, negsink[0:QN, hg * 8:hg * 8 + 8], ALU.mult, ALU.min)
            tt_op("dve", se, negm, sinkb[0:QN, hg * 8:hg * 8 + 8], ALU.add)
            info[ui] = (blocks, kstart, nk)

        def e_phase(ui):
            qb, hg = units[ui]
            par = ui % 2
            blocks, kstart, nk = info[ui]
            se = smA[0:QN, par, 16:24]
            act(se, se, AF.Exp)
            for hl in range(8):
                act(pbf[par][0:QN, hl, 0:nk], sbb[0:QN, hl, 0:nk], AF.Exp, bias=smA[0:QN, par, 8 + hl:9 + hl],
                    accum=smA[0:QN, par, 24 + hl:25 + hl])

        def pe_phase(ui):
            qb, hg = units[ui]
            par = ui % 2
            blocks, kstart, nk = info[ui]
            ob = ps[6 + par]
            se = smA[0:QN, par, 16:24]
            tt_op("dve", se, se, smA[0:QN, par, 24:32], ALU.add)
            P.op("dve", lambda e, se=se: e.reciprocal(se, se), [se], [se])
            allfull = all(bk[1] == 128 for bk in blocks) and QN == 128
            def tphase(hl):
                pT_ = pT[hl % 2]
                tbb = nbank()[:, :].bitcast(BF16)
                for bi, (kcol, kn, vblk, dcol) in enumerate(blocks):
                    off = kcol - kstart
                    trb(tbb[0:kn, bi * 128:bi * 128 + QN], pbf[par][0:QN, hl, off:off + kn], QN)
                ev = "act"
                if allfull:
                    nb_ = len(blocks)
                    cp(ev, pT_[:, 0:nb_, :], tbb[:, 0:nb_ * 128].rearrange("p (b q) -> p b q", q=128))
                else:
                    for bi, (kcol, kn, vblk, dcol) in enumerate(blocks):
                        cp(ev, pT_[0:kn, bi, 0:QN], tbb[0:kn, bi * 128:bi * 128 + QN])

            def pvphase(hl):
                h = hg * 8 + hl
                kv = h // 4
                pT_ = pT[hl % 2]
                for bi, (kcol, kn, vblk, dcol) in enumerate(blocks):
                    mm(ob[0:QN, hl * 64:(hl + 1) * 64], pT_[0:kn, bi, 0:QN], vbuf[0:kn, vblk, kv * 64:kv * 64 + 64],
                       bi == 0, bi == len(blocks) - 1)

            tphase(0)
            for hl in range(1, 8):
                tphase(hl)
                pvphase(hl - 1)
            pvphase(7)
            ot = otok[qb % 2]
            rden = smA[0:QN, par, 16:24].unsqueeze(2).broadcast_to([QN, 8, 64])
            tt_op("dve", ot[0:QN, hg * 512:(hg + 1) * 512].rearrange("p (h d) -> p h d", d=64),
                  ob[0:QN, :].rearrange("p (h d) -> p h d", d=64), rden, ALU.mult)
            chk(3.5)
            if hg == 1:
                tbb = nbank()[:, :].bitcast(BF16)
                for oc in range(8):
                    trb(tbb[:, oc * 128:oc * 128 + QN], ot[0:QN, oc * 128:(oc + 1) * 128], QN)
                cp("act", oT[:, :, qb * 128:qb * 128 + QN], tbb[:, :].rearrange("p (c q) -> p c q", q=128)[:, :, 0:QN])

        s_phase(0)
        e_phase(0)
        for ui in range(1, len(units)):
            s_phase(ui)
            pe_phase(ui - 1)
            e_phase(ui)
        pe_phase(len(units) - 1)
        if not is_last:
            cp("pool", kT[:, :, 0:128], kT[:, :, N:N + 128])
            cp("pool", vbuf[:, 0, :], vbuf[:, NB, :])
        use_pieces(tile, PC_WOC, PC_WOC + 1)
        proj_ln(tile, N, 8, lambda k: oT[:, k, :],
                lambda k, half: rv_8x512[slot(tile, PC_WOC + half)][:, k, :], 2, None, tail_jobs=tail_jobs)

    epsb = sb("epsb", [128, 1], F32)[0]
    oneb = sb("oneb", [128, 1], F32)[0]
    sm2 = sb("sm2", [128, 8], F32)[0]
    assert cur[0] <= 229344, cur[0]
    P.op("pool", lambda e: e.memset(epsb[:, :], LN_EPS), [], [epsb[:, :]])
    P.op("pool", lambda e: e.memset(oneb[:, :], 1.0), [], [oneb[:, :]])

    def load_x_dma(src, N):
        PT = min(N, 128)
        NB = (N + 127) // 128
        if NB > 1:
            dma("act", "xin", xin[:, 0:NB, :], src.rearrange("(nb p) d -> p nb d", p=128))
        else:
            dma("act", "xin", xin[0:PT, 0, :], src)

    def load_x_tr(N):
        PT = min(N, 128)
        NB = (N + 127) // 128
        for nb in range(NB):
            xb_ = xnb[nb % 2]
            cp("act" if nb % 2 else "dve", xb_[0:PT, :], xin[0:PT, nb, :])
            tbb = nbank()[:, :].bitcast(BF16)
            for kc in range(8):
                trb(tbb[:, kc * 128:kc * 128 + PT], xb_[0:PT, kc * 128:(kc + 1) * 128], PT)
            cp("dve" if nb % 2 else "act", xTa[:, :, nb * 128:nb * 128 + PT],
               tbb[:, :].rearrange("p (c q) -> p c q", q=128)[:, :, 0:PT])

    P.op("pool", lambda e: e.memset(xr_buf[:, :, 0:3], 0.0), [], [xr_buf[:, :, 0:3]])
    P.op("pool", lambda e: e.memset(g_buf[:, :, 0:30], 0.0), [], [g_buf[:, :, 0:30]])
    P.op("pool", lambda e: e.memset(hstate[:, :], 0.0), [], [hstate[:, :]])
    P.op("pool", lambda e: e.memset(kT[:, :, 0:128], 0.0), [], [kT[:, :, 0:128]])
    P.op("pool", lambda e: e.memset(vbuf[:, 0, :], 0.0), [], [vbuf[:, 0, :]])

    def sample_init():
        dma("act", "c2", xr_buf[:, :, 0:3], st_rc_d.rearrange("p (c k) -> p c k", k=3))
        dma("act", "c3", hstate[:, :], st_h_d)
        dma("act", "c0", cc[:, 0, 0:120], st_cf_d)
        cp("dve", g_buf[:, :, 0:30], cc[:, 0, 0:120].rearrange("p (c k) -> p c k", k=30))
        dma("act", "c1", cc[:, 1, :], st_kT_d)
        cp("dve", kT[:, :, 0:128], cc[:, 1, :].rearrange("p (c k) -> p c k", k=128))
        dma("act", "c2", cc[:, 2, 0:256], st_v_d)
        cp("dve", vbuf[:, 0, :], cc[:, 2, 0:256])
        final_ops.append(dma("act", "so5", ok_d[1][0:64, :], ck_d[64:128, :]))
        final_ops.append(dma("act", "so6", ov_d[1][0:64, :], cv_d[64:128, :]))

    tiles = [(t, 512, 0) for t in range(n_tiles)] + ([(n_tiles, DEC, 1)] if with_sample else [])
    load_x_dma(x_d[0:512, :], 512)
    for job in stageA_jobs(0, 512, n_tiles == 1):
        job()
    for idx, (t, N, oi) in enumerate(tiles):
        is_sample = oi == 1
        last = (t == n_tiles - 1) or is_sample
        h0, hq_, h1 = {}, {}, {}
        split = N == 512
        projA(t, N, last, oi, tail_jobs=[lambda: ffn_partA(t, N, 0, h0)] if split else [])
        nxt = tiles[idx + 1] if idx + 1 < len(tiles) else None
        if nxt is not None:
            nt, nN, noi = nxt
            load_x_dma(xs_d if noi == 1 else x_d[nt * 512:(nt + 1) * 512, :], nN)
        ffn(t, N, 0, None, holder=h0, tail_jobs=[lambda: q_partA(t, N, hq_)] if split else [])
        side = []
        if nxt is not None:
            nt, nN, noi = nxt
            nlast = (nt == n_tiles - 1) or noi == 1
            side = stageA_jobs(nt, nN, nlast, pre=[sample_init] if noi == 1 else [])
        npre = min(len(side), 6 if (nxt is not None and nxt[2] == 1) else 5)
        tj = side[:npre] + ([lambda: ffn_partA(t, N, 1, h1)] if split else [])
        mixer_c(t, N, (t == 0 and not is_sample), last, oi, tail_jobs=tj, holder=hq_)
        ffn(t, N, 1, ys_d if is_sample else y_d[t * 512:(t + 1) * 512, :], side=side[npre:], holder=h1)

    P.wait_all("sp", final_ops + list(P.lastdma.values()))
    P.wait_all("act", final_ops)
    if order is not None:
        P.emit()
    return nc, seq_rec


def build_program(n_tiles=SEQ // 512, with_sample=True, seq=SEQ):
    _, order = _build(n_tiles, with_sample, seq, None)
    nc, order2 = _build(n_tiles, with_sample, seq, list(order))
    assert order2 == order
    return nc


def _pieces(inp):
    f = np.float32
    tape = np.zeros((NPIECE, 128, PIECE), f)

    def kmajor(w, c0, ncol):
        return w[:, c0:c0 + ncol].reshape(8, 128, ncol).transpose(1, 0, 2)

    w_in = inp["w_in_ab"][0]
    for g, c0 in enumerate((0, 1536, 1024, 512)):
        tape[PC_WIN + g] = kmajor(w_in, c0, 512).reshape(128, PIECE)
    cw = inp["cf_conv_w"][0]
    for c in range(4):
        pc = np.zeros((128, 32, 128), f)
        idx = np.arange(128)
        for k in range(31):
            pc[idx, k, idx] = cw[k, c * 128:(c + 1) * 128]
        tape[PC_CONV + c] = pc.reshape(128, PIECE)
    wo = inp["w_out_ab"][0]
    for h in range(2):
        tape[PC_WOUT + h] = kmajor(wo, h * 512, 512).reshape(128, PIECE)
    for layer, (pg, pd) in enumerate(((PC_GU0, PC_WD0), (PC_GU1, PC_WD1))):
        wg, wu, wd = inp["w_ff_gate"][layer], inp["w_ff_up"][layer], inp["w_ff_down"][layer]
        for jj in range(11):
            pc = np.stack([kmajor(wg, jj * 256, 256), kmajor(wu, jj * 256, 256)], axis=1)
            tape[pg + jj] = pc.reshape(128, PIECE)
        wdk = wd.reshape(NJ, 128, D).transpose(1, 0, 2)
        for q in range(6):
            pc = np.zeros((128, 4, D), f)
            n = min(4, NJ - q * 4)
            pc[:, 0:n] = wdk[:, q * 4:q * 4 + n]
            tape[pd + q] = pc.reshape(128, PIECE)
    wqkv = inp["w_qkv"][0]
    for h in range(2):
        tape[PC_Q + h] = kmajor(wqkv, h * 512, 512).reshape(128, PIECE)
    wk = kmajor(wqkv, 1024, 256).reshape(128, 8, 4, 1, 64)
    tape[PC_KDUP] = np.broadcast_to(wk, (128, 8, 4, 2, 64)).reshape(128, PIECE)
    tape[PC_KV] = kmajor(wqkv, 1024, 512).reshape(128, PIECE)
    woc = inp["w_out_c"][0]
    for h in range(2):
        tape[PC_WOC + h] = kmajor(woc, h * 512, 512).reshape(128, PIECE)
    return tape


def _col(v):
    return np.ascontiguousarray(v.reshape(-1, 128).T)


def _shared_inputs(inp):
    f = np.float32
    sh = {}
    sh["tape32"] = _pieces(inp)
    sh["ident"] = np.eye(128, dtype=f)
    sh["ones"] = np.ones((128, 128), f)
    i = np.arange(128)[:, None]
    s = np.arange(256)[None, :]
    dist = np.abs(128 + i - s).astype(f)
    qc = i // 64
    kc = s // 64 - 2
    valid = (kc <= qc) & (kc >= qc - 2)
    dist = np.where(valid, dist, f(1e10)).astype(f)
    sh["dist"] = dist
    pcol = np.zeros((128, NCOL), f)
    rcw = inp["rec_conv_w"][0]
    for c in range(4):
        for k in range(4):
            pcol[:, C_RCW + 4 * c + k] = rcw[k, c * 128:(c + 1) * 128]
    for name, col in (("rec_conv_b", C_RCB), ("rec_gate_a_b", C_GAB), ("rec_gate_x_b", C_GXB), ("rec_lambda", C_LAM),
                      ("cf_conv_b", C_CFB), ("cf_norm_g", C_CFG), ("cf_norm_b", C_CFBE)):
        pcol[:, col:col + 4] = _col(inp[name][0])
    lng = [inp["ln_mix_g"][0], inp["ln_ff_g"][0], inp["ln_mix_g"][1], inp["ln_ff_g"][1]]
    lnb = [inp["ln_mix_b"][0], inp["ln_ff_b"][0], inp["ln_mix_b"][1], inp["ln_ff_b"][1]]
    lntab = np.zeros((4, 128, 2 * D), f)
    for l in range(4):
        pcol[:, C_LNG + 8 * l:C_LNG + 8 * l + 8] = _col(lng[l])
        pcol[:, C_LNB + 8 * l:C_LNB + 8 * l + 8] = _col(lnb[l])
        lntab[l, :, 0:D] = lng[l][None, :]
        lntab[l, :, D:] = lnb[l][None, :]
    sh["pcol"] = pcol
    sh["lntab"] = lntab
    gates = np.zeros((128, 8, 128), f)
    for t_, nm in enumerate(("rec_gate_a_w", "rec_gate_x_w")):
        w = inp[nm][0]
        for c in range(4):
            gates[0:64, 4 * t_ + c, 0:64] = w[2 * c]
            gates[64:128, 4 * t_ + c, 64:128] = w[2 * c + 1]
    sh["gates"] = gates.reshape(128, 1024)
    sh["sinkb"] = np.broadcast_to(inp["attn_sinks"][0][None, :], (128, NHEAD)).astype(f).copy()
    return sh


def _core_inputs(inp, b, seq):
    f = np.float32
    m = {}
    m["x"] = np.ascontiguousarray(inp["x_prompt"][b, :seq])
    m["xs"] = np.ascontiguousarray(inp["x_sample"][b])
    rc = inp["state_rec_conv"][0, b]
    m["st_rc"] = np.ascontiguousarray(rc.reshape(3, 4, 128).transpose(2, 1, 0)).reshape(128, 12)
    cf = inp["state_cf_conv"][0, b]
    m["st_cf"] = np.ascontiguousarray(cf.reshape(30, 4, 128).transpose(2, 1, 0)).reshape(128, 120)
    m["st_h"] = _col(inp["state_rec_h"][0, b])
    ck = inp["cache_k"][0, b]
    kTt = ck.transpose(1, 2, 0)
    kd = np.stack([kTt, kTt], axis=1)
    m["st_kT"] = np.ascontiguousarray(kd.reshape(4, 128, 128).transpose(1, 0, 2)).reshape(128, 512)
    m["st_v"] = np.ascontiguousarray(inp["cache_v"][0, b].reshape(128, 256))
    m["ck"] = np.ascontiguousarray(ck.reshape(128, 256))
    m["cv"] = np.ascontiguousarray(inp["cache_v"][0, b].reshape(128, 256))
    return {k: np.asarray(v, f) for k, v in m.items()}


_PROG_CACHE = {}


def run(inp, ncores=NCORE, seq=SEQ, with_sample=True):
    inp = {k: np.asarray(v) for k, v in inp.items()}
    key = (seq, with_sample)
    if key not in _PROG_CACHE:
        _PROG_CACHE[key] = build_program(seq // 512, with_sample, seq)
    nc = _PROG_CACHE[key]
    sh = _shared_inputs(inp)
    in_maps = []
    for b in range(ncores):
        m = dict(sh)
        m.update(_core_inputs(inp, b, seq))
        in_maps.append(m)
    res = run_bass_kernel_spmd(nc, in_maps, core_ids=list(range(ncores)))
    R = res.results
    f = np.float32

    def st(name, shape):
        return np.stack([np.asarray(R[b][name], f).reshape(shape) for b in range(ncores)])

    y = st("y", (seq, D))
    ys = st("ys", (DEC, D))
    outs = (y, ys,
            st("o_h_p", (512,))[None], st("o_h_s", (512,))[None],
            st("o_rc_p", (3, 512))[None], st("o_rc_s", (3, 512))[None],
            st("o_cf_p", (30, 512))[None], st("o_cf_s", (30, 512))[None],
            st("o_k_p", (128, 4, 64))[None], st("o_k_s", (128, 4, 64))[None],
            st("o_v_p", (128, 4, 64))[None], st("o_v_s", (128, 4, 64))[None])
    return outs


def kernel(**inputs):
    return run(inputs)
```

```python
import contextlib
import numpy as np
import concourse.bass as bass
import concourse.mybir as mybir
from concourse.bass_utils import run_bass_kernel_spmd

F32 = mybir.dt.float32
BF16 = mybir.dt.bfloat16
AF = mybir.ActivationFunctionType
ALU = mybir.AluOpType
AX = mybir.AxisListType

D = 1024
SEQ = 8192
NCORE = 8
DEC = 64
D_FF = 2816
NJ = D_FF // 128
ALPHA = 4.0 ** 0.25
LN_EPS = 1e-5
NHEAD = 16
SLOPES = [2.0 ** (-8.0 * (h + 1) / NHEAD) for h in range(NHEAD)]
RING = 8
PIECE = 4096
SB_BASE = 16512

PC_WIN = 0
PC_CONV = 4
PC_WOUT = 8
PC_GU0 = 10
PC_WD0 = 21
PC_Q = 27
PC_KDUP = 29
PC_KV = 30
PC_WOC = 31
PC_GU1 = 33
PC_WD1 = 44
NPIECE = 50

C_RCW, C_RCB, C_GAB, C_GXB, C_LAM, C_CFB, C_CFG, C_CFBE, C_LNG, C_LNB = 0, 16, 20, 24, 28, 32, 36, 40, 44, 76
NCOL = 108


class Op:
    __slots__ = ("eng", "fn", "waits", "semkey", "seq", "needs_inc", "value", "is_dma")


class Prog:
    ENGS = ["pe", "act", "dve", "pool", "sp"]

    def __init__(self, nc):
        self.nc = nc
        self.ops = {e: [] for e in self.ENGS}
        self.recs = {}
        self.waited = {e: {} for e in self.ENGS}
        self.semseq = {}
        self.lastdma = {}
        self.sbuf_addr = {}
        self.pending = {}

    def region(self, ap):
        t = ap.tensor
        name = t.name
        pat = [(int(s), int(n)) for s, n in ap.ap]
        off = int(ap.offset)
        esz = mybir.dt.size(ap.dtype)
        cls = type(t).__name__
        if cls.startswith("DRam"):
            lo = off
            hi = off + sum((n - 1) * abs(s) for s, n in pat) + 1
            return ("d:" + name, 0, 1, lo * esz, hi * esz)
        pstep = pat[0][0]
        p0 = off // pstep if pstep else 0
        fo = off - p0 * pstep
        p1 = p0 + pat[0][1]
        ext = sum((n - 1) * abs(s) for s, n in pat[1:]) + 1
        if cls.startswith("PSum"):
            return ("p:" + name, 0, 128, 0, 2048)
        base = self.sbuf_addr[name]
        return ("sb", p0, p1, base + fo * esz, base + (fo + ext) * esz)

    def _psum_guard(self, op, reads, writes, start):
        for kind, aps in (("r", reads), ("w", writes)):
            for ap in aps:
                if not type(ap.tensor).__name__.startswith("PSum"):
                    continue
                pat = [(int(s_), int(n)) for s_, n in ap.ap]
                off = int(ap.offset)
                pstep = pat[0][0]
                fo = off - (off // pstep) * pstep if pstep else 0
                ext = sum((n - 1) * abs(s_) for s_, n in pat[1:]) + 1
                esz = mybir.dt.size(ap.dtype)
                lo, hi = fo * esz, (fo + ext) * esz
                pend = self.pending.setdefault(ap.tensor.name, [])
                if kind == "r" and op.eng != "pe":
                    pend[:] = [iv for iv in pend if not (iv[0] < hi and lo < iv[1])]
                elif kind == "w" and op.eng == "pe":
                    if start:
                        for iv in pend:
                            assert not (iv[0] < hi and lo < iv[1]), ("PSUM reuse before consumption", ap.tensor.name, lo, hi, iv)
                        pend.append((lo, hi))

    def _deps(self, op, reads, writes):
        deps = []
        for kind, aps in (("w", writes), ("r", reads)):
            for ap in aps:
                key, p0, p1, lo, hi = self.region(ap)
                lst = self.recs.setdefault(key, [])
                keep = []
                for r in lst:
                    rp0, rp1, rlo, rhi, rkind, rop = r
                    if rp0 < p1 and p0 < rp1 and rlo < hi and lo < rhi:
                        if kind == "r":
                            if rkind == "w":
                                deps.append((rop, "raw"))
                            elif key[0] == "p" and rop.eng != op.eng:
                                deps.append((rop, "rar"))
                            keep.append(r)
                        else:
                            deps.append((rop, "waw" if rkind == "w" else "war"))
                            if p0 <= rp0 and rp1 <= p1 and lo <= rlo and rhi <= hi:
                                continue
                            keep.append(r)
                    else:
                        keep.append(r)
                if kind == "r":
                    keep = [r for r in keep if not (r[4] == "r" and r[5].semkey == op.semkey and not r[5].is_dma
                                                    and not op.is_dma and p0 <= r[0] and r[1] <= p1 and lo <= r[2] and r[3] <= hi)]
                keep.append((p0, p1, lo, hi, kind, op))
                self.recs[key] = keep
        return deps

    def _add_waits(self, op, deps):
        w = self.waited[op.eng]
        for rop, kind in deps:
            if rop is op:
                continue
            if not rop.is_dma and not op.is_dma and rop.eng == op.eng:
                if op.eng == "pe":
                    continue
            if w.get(rop.semkey, -1) >= rop.seq:
                continue
            w[rop.semkey] = rop.seq
            rop.needs_inc = True
            op.waits.append(rop)

    def op(self, eng, fn, reads=(), writes=(), start=True):
        o = Op()
        o.eng, o.fn, o.waits, o.semkey, o.is_dma = eng, fn, [], eng, False
        o.seq = len(self.ops[eng])
        o.needs_inc, o.value = False, None
        self._psum_guard(o, reads, writes, start)
        self._add_waits(o, self._deps(o, reads, writes))
        self.ops[eng].append(o)
        return o

    def dma(self, q, sem, fn, reads=(), writes=()):
        o = Op()
        o.eng, o.fn, o.waits, o.semkey, o.is_dma = q, fn, [], "dma:" + sem, True
        o.seq = self.semseq.get(sem, 0)
        self.semseq[sem] = o.seq + 1
        o.needs_inc, o.value = True, 16 * (o.seq + 1)
        deps = self._deps(o, reads, writes)
        prev = self.lastdma.get(sem)
        if prev is not None:
            deps.append((prev, "raw"))
        self.lastdma[sem] = o
        self._add_waits(o, deps)
        self.ops[q].append(o)
        return o

    def wait_all(self, eng, oplist):
        o = Op()
        o.eng, o.fn, o.waits, o.semkey, o.is_dma = eng, None, [], eng, False
        o.seq = len(self.ops[eng])
        o.needs_inc, o.value = False, None
        self._add_waits(o, [(x, "raw") for x in oplist])
        self.ops[eng].append(o)

    def emit(self):
        nc = self.nc
        for e in self.ENGS:
            cnt = 0
            for o in self.ops[e]:
                if o.is_dma:
                    continue
                if o.needs_inc:
                    cnt += 1
                    o.value = cnt
        with contextlib.ExitStack() as es:
            sems = {}
            for e in self.ENGS:
                sems[e] = es.enter_context(nc.semaphore("s_" + e))
            for s in self.semseq:
                sems["dma:" + s] = es.enter_context(nc.semaphore("d_" + s))
            block = es.enter_context(nc.Block())

            def run(ename):
                def body(eng):
                    for o in self.ops[ename]:
                        for w in o.waits:
                            eng.wait_ge(sems[w.semkey], w.value)
                        if o.fn is None:
                            continue
                        ins = o.fn(eng)
                        if o.is_dma:
                            ins.then_inc(sems[o.semkey], 16)
                        elif o.needs_inc:
                            ins.then_inc(sems[o.semkey], 1)
                return body

            block.tensor(run("pe"))
            block.scalar(run("act"))
            block.vector(run("dve"))
            block.gpsimd(run("pool"))
            block.sync(run("sp"))


def _build(n_tiles, with_sample, seq, order):
    nc = bass.Bass("TRN2", target_bir_lowering=False)
    P = Prog(nc)

    def din(name, shape, dt=F32):
        return nc.dram_tensor(name, list(shape), dt, kind="ExternalInput").ap()

    def dout(name, shape, dt=F32):
        return nc.dram_tensor(name, list(shape), dt, kind="ExternalOutput").ap()

    x_d = din("x", [seq, D])
    xs_d = din("xs", [DEC, D])
    tape32 = din("tape32", [NPIECE, 128, PIECE])
    ident_d = din("ident", [128, 128])
    ones_d = din("ones", [128, 128])
    dist_d = din("dist", [128, 256])
    pcol_d = din("pcol", [128, NCOL])
    lntab_d = din("lntab", [4, 128, 2 * D])
    gates_d = din("gates", [128, 8 * 128])
    sinkb_d = din("sinkb", [128, NHEAD])
    st_rc_d = din("st_rc", [128, 12])
    st_cf_d = din("st_cf", [128, 120])
    st_h_d = din("st_h", [128, 4])
    st_kT_d = din("st_kT", [128, 512])
    st_v_d = din("st_v", [128, 256])
    ck_d = din("ck", [128, 256])
    cv_d = din("cv", [128, 256])

    y_d = dout("y", [seq, D])
    ys_d = dout("ys", [DEC, D])
    oh_d = [dout("o_h_p", [4, 128]), dout("o_h_s", [4, 128])]
    orc_d = [dout("o_rc_p", [3, 512]), dout("o_rc_s", [3, 512])]
    ocf_d = [dout("o_cf_p", [30, 512]), dout("o_cf_s", [30, 512])]
    ok_d = [dout("o_k_p", [128, 256]), dout("o_k_s", [128, 256])]
    ov_d = [dout("o_v_p", [128, 256]), dout("o_v_s", [128, 256])]

    tape16 = nc.dram_tensor("tape16", [NPIECE, 128, PIECE], BF16, kind="Internal").ap()

    cur = [SB_BASE]

    def sb(name, shape, dt, at=None):
        esz = mybir.dt.size(dt)
        n = 1
        for s in shape[1:]:
            n *= s
        nbytes = n * esz
        if at is None:
            off = (cur[0] + 63) // 64 * 64
            cur[0] = off + nbytes
        else:
            off = at
        t = nc.alloc_sbuf_tensor_at(name, list(shape), dt, offset=off)
        P.sbuf_addr[t.name] = off
        return t, off

    ring_off = []
    rv_flat, rv_8x512, rv_conv, rv_gu, rv_wd = [], [], [], [], []
    for s in range(RING):
        t, off = sb(f"ring{s}", [128, PIECE], BF16)
        ring_off.append(off)
        rv_flat.append(t)
        rv_8x512.append(t[:, :].rearrange("p (k n) -> p k n", n=512))
        rv_conv.append(t[:, :].rearrange("p (k n) -> p k n", n=128))
        rv_gu.append(t[:, :].rearrange("p (t k n) -> p t k n", t=2, k=8))
        rv_wd.append(t[:, :].rearrange("p (j n) -> p j n", n=1024))
    xt = sb("xt", [128, 4, D], F32)[0]
    xT = sb("xT", [128, 8, 512], BF16)[0]
    xin = sb("xin", [128, 4, D], F32)[0]
    lnt = sb("lnt", [128, 2 * D], F32)[0]
    ident = sb("identt", [128, 128], F32)[0]
    ones = sb("onest", [128, 128], F32)[0]
    dist = sb("distt", [128, 256], F32)[0]
    pcol = sb("pcolt", [128, NCOL], F32)[0]
    gates32 = None
    gatesb = sb("gatesb", [128, 8, 128], BF16)[0]
    sinkb = sb("sinkbt", [128, NHEAD], F32)[0]
    identb = sb("identb", [128, 128], BF16)[0]
    smA = sb("smA", [128, 2, 32], F32)[0]
    xnb1 = sb("xnb", [128, D], BF16)[0]
    xnb = [xnb1, xnb1]
    negsink = sb("negsink", [128, NHEAD], F32)[0]
    cpv = sb("cpv", [128, 8], F32)[0]
    sm = sb("sm", [128, 128], F32)[0]
    xr_buf = sb("xr_buf", [128, 4, 3 + 512], F32)[0]
    g_buf = sb("g_buf", [128, 4, 30 + 512], BF16)[0]
    g32 = sb("g32", [128, 4, 30], F32)[0]
    hstate = sb("hstate", [128, 4], F32)[0]
    kT = sb("kT", [128, 4, 128 + 512], BF16)[0]
    vbuf = sb("vbuf", [128, 5, 256], BF16)[0]
    XB = (cur[0] + 63) // 64 * 64
    cur[0] = XB
    gy = sb("gy", [128, 4, 512], F32)[0]
    sg = sb("sg", [128, 512], F32)[0]
    xc2 = [sb(f"xc{i}", [128, 512], F32)[0] for i in range(2)]
    rr2 = [sb(f"rr{i}", [128, 512], F32)[0] for i in range(2)]
    ii2 = [sb(f"ii{i}", [128, 512], F32)[0] for i in range(2)]
    a22 = [sb(f"a2{i}", [128, 512], F32)[0] for i in range(2)]
    xcb2 = [sb(f"xcb{i}", [128, 512], BF16)[0] for i in range(2)]
    ro = sb("ro", [128, 4, 512], BF16)[0]
    cc = sb("cc", [128, 4, 512], F32)[0]
    sq = sb("sq", [128, 512], F32)[0]
    sq2 = [sq, sg]
    mean, var, tt = xc2[0], rr2[0], ii2[0]
    cn = gy[:, :, :].bitcast(BF16).rearrange("p c n -> p (c n)")[:, 0:2048].rearrange("p (c n) -> p c n", n=512)
    XE = cur[0]
    cur[0] = XB
    qT = sb("qT", [128, 8, 512], BF16)[0]
    oT = sb("oT", [128, 8, 512], BF16)[0]
    sbb = sb("sbb", [128, 8, 256], F32)[0]
    pbf = [sb(f"pbf{i}", [128, 8, 256], BF16)[0] for i in range(2)]
    pT = [sb(f"pT{i}", [128, 2, 128], BF16)[0] for i in range(2)]
    otok1 = sb("otok", [128, D], BF16)[0]
    otok = [otok1, otok1]
    stg = sb("stg", [128, 512], F32)[0]
    sm_rc = sb("sm_rc", [128, 512], F32)[0]
    sm_cf = sm_rc
    kvo = sb("kvo", [128, 512], F32)[0]
    XE = max(cur[0], XE)
    cur[0] = XE
    hT = sb("hT", [128, NJ, 512], BF16)[0]
    xTa = hT[:, 14:22, :]
    sgt1 = sb("sgt", [128, 512], F32)[0]
    sgt = [sgt1, sgt1]
    assert cur[0] <= 229344, cur[0]

    ps = [nc.alloc_psum_tensor(f"ps{i}", [128, 512], F32) for i in range(8)]
    bank = [0]

    def nbank():
        b = bank[0]
        bank[0] = (b + 1) % 6
        return ps[b]

    def mm(out, lhsT, rhs, start, stop):
        P.op("pe", lambda e: e.matmul(out, lhsT, rhs, start=start, stop=stop), [lhsT, rhs], [out], start=start)

    def tr(out, in_, n):
        idn = ident[0:n, 0:n]
        P.op("pe", lambda e: e.transpose(out, in_, idn), [in_, idn], [out])

    def trb(out, in_, n):
        idn = identb[0:n, 0:n]
        P.op("pe", lambda e: e.transpose(out, in_, idn), [in_, idn], [out])

    def act(out, in_, func, bias=None, scale=None, accum=None):
        rd = [in_]
        kw = {}
        if bias is not None:
            kw["bias"] = bias
            if not isinstance(bias, float):
                rd.append(bias)
        if scale is not None:
            kw["scale"] = scale
            if not isinstance(scale, float):
                rd.append(scale)
        wr = [out]
        if accum is not None:
            kw["accum_out"] = accum
            wr.append(accum)
        P.op("act", lambda e: e.activation(out, in_, func, **kw), rd, wr)

    def tt_op(eng, out, a, b, op):
        P.op(eng, lambda e: e.tensor_tensor(out, a, b, op), [a, b], [out])

    def ts_op(eng, out, a, s1, s2, op0, op1=None):
        rd = [a] + [s for s in (s1, s2) if s is not None and not isinstance(s, float)]
        if op1 is None:
            P.op(eng, lambda e: e.tensor_scalar(out, a, s1, None, op0), rd, [out])
        else:
            P.op(eng, lambda e: e.tensor_scalar(out, a, s1, s2, op0, op1), rd, [out])

    def stt(out, a, s, b, op0, op1):
        rd = [a, b] + ([] if isinstance(s, float) else [s])
        P.op("dve", lambda e: e.scalar_tensor_tensor(out, a, s, b, op0, op1), rd, [out])

    def cp(eng, out, in_):
        if eng == "act":
            act(out, in_, AF.Copy)
        else:
            P.op(eng, lambda e: e.tensor_copy(out, in_), [in_], [out])

    def dma(q, sem, out, in_, **kw):
        return P.dma(q, sem, lambda e: e.dma_start(out=out, in_=in_, **kw), [in_], [out])

    for i in range(NPIECE):
        src = tape32[i].rearrange("p (a b) -> p a b", b=2048)
        dst = tape16[i].rearrange("p (a b) -> p a b", b=2048)
        dma("pool", f"cv{i % 4}", dst, src)

    seq_rec = []
    nload = [0]
    held = set()
    curpos = {}

    def use_pieces(tile, first, last):
        ids = list(range(first, last + 1))
        p0 = len(seq_rec)
        for i, pid in enumerate(ids):
            curpos[pid] = p0 + i
            seq_rec.append(pid)
        p1 = p0 + len(ids) - 1
        if order is not None:
            assert order[p0:p1 + 1] == ids, (order[p0:p1 + 1], ids)
            base = min([p0] + list(held))
            lim = min(max(p1, base + RING - 1), len(order) - 1)
        else:
            base = min([p0] + list(held))
            lim = p1
        assert p1 - base < RING, (p1, base)
        while nload[0] <= lim:
            g = nload[0]
            pid = order[g] if order is not None else seq_rec[g]
            dma("sp", f"ring{g % RING}", rv_flat[g % RING][:, :], tape16[pid])
            nload[0] += 1

    def slot(tile, piece):
        return curpos[piece] % RING

    dma("act", "c0", ident[:, :], ident_d)
    dma("act", "c1", ones[:, :], ones_d)
    dma("act", "c2", dist[:, :], dist_d)
    dma("act", "c3", pcol[:, :], pcol_d)
    dma("act", "c0", sinkb[:, :], sinkb_d)
    gst = cc
    dma("act", "c1", gst[:, 0:2, :].rearrange("p a b -> p (a b)"), gates_d)
    P.op("dve", lambda e: e.tensor_copy(gatesb[:, :, :].rearrange("p a b -> p (a b)"),
                                        gst[:, 0:2, :].rearrange("p a b -> p (a b)")),
         [gst[:, 0:2, :]], [gatesb[:, :, :]])
    ts_op("dve", negsink[:, :], sinkb[:, :], -1.0, None, ALU.mult)
    cp("dve", identb[:, :], ident[:, :])
    lam = pcol[:, C_LAM:C_LAM + 4]
    s_abs, s_y, s_z, s_z2, s_p, s_m = (sm[:, 4 * i:4 * i + 4] for i in range(6))
    ts_op("dve", s_m, lam, -1.0, None, ALU.mult)
    tt_op("dve", s_abs, lam, s_m, ALU.max)
    act(s_y, s_abs, AF.Exp, scale=-1.0)
    ts_op("dve", s_z, s_y, 2.0, None, ALU.add)
    P.op("dve", lambda e: e.reciprocal(s_z, s_z), [s_z], [s_z])
    tt_op("dve", s_z, s_z, s_y, ALU.mult)
    tt_op("dve", s_z2, s_z, s_z, ALU.mult)
    ts_op("dve", s_p, s_z2, 1.0 / 9.0, 1.0 / 7.0, ALU.mult, ALU.add)
    for cst in (1.0 / 5.0, 1.0 / 3.0, 1.0):
        tt_op("dve", s_p, s_p, s_z2, ALU.mult)
        ts_op("dve", s_p, s_p, cst, None, ALU.add)
    tt_op("dve", s_p, s_p, s_z, ALU.mult)
    ts_op("dve", s_m, s_m, 0.0, None, ALU.max)
    stt(s_p, s_p, 2.0, s_m, ALU.mult, ALU.add)
    ts_op("dve", cpv[:, 0:4], s_p, -8.0, None, ALU.mult)
    ts_op("dve", cpv[:, 4:8], s_p, -16.0, None, ALU.mult)

    final_ops = []

    def chk(k):
        pass

    def proj_ln(tile, N, nk, src, wrhs, ln_idx, ydst, res=None, tail_jobs=()):
        PT = min(N, 128)
        NB = (N + 127) // 128
        if res is None:
            res = xt
        dma("pool", "lnt", lnt[:, :], lntab_d[ln_idx])
        pbs = {}

        def mmphase(nb):
            pb = [nbank(), nbank()]
            for half in range(2):
                for k in range(nk):
                    mm(pb[half][0:PT, :], src(k)[:, nb * 128:nb * 128 + PT], wrhs(k, half), k == 0, k == nk - 1)
            pbs[nb] = pb

        def post_a(nb):
            pb = pbs[nb]
            for half in range(2):
                xs_ = xt[0:PT, nb, half * 512:(half + 1) * 512]
                rs_ = res[0:PT, nb, half * 512:(half + 1) * 512]
                stt(xs_, rs_, ALPHA, pb[half][0:PT, :], ALU.mult, ALU.add)

        def post(nb, do_a=True):
            if do_a:
                post_a(nb)
            so = 64 + 16 * (nb % 2)
            st = sm[0:PT, so:so + 12]
            mv = sm[0:PT, so + 12:so + 14]
            rs = sm[0:PT, so + 14:so + 15]
            nmr = sm[0:PT, so + 15:so + 16]
            for half in range(2):
                o_ = sm[0:PT, so + 6 * half:so + 6 + 6 * half]
                i_ = xt[0:PT, nb, half * 512:(half + 1) * 512]
                P.op("dve", lambda e, o_=o_, i_=i_: e.bn_stats(o_, i_), [i_], [o_])
            P.op("dve", lambda e: e.bn_aggr(mv, st), [st], [mv])
            act(rs, sm[0:PT, so + 13:so + 14], AF.Sqrt, bias=epsb[0:PT, :])
            P.op("dve", lambda e: e.reciprocal(rs, rs), [rs], [rs])
            stt(nmr, sm[0:PT, so + 12:so + 13], -1.0, rs, ALU.mult, ALU.mult)
            row = xt[0:PT, nb, :]
            if ydst is None:
                xb_ = xnb[nb % 2]
                act(xb_[0:PT, :], row, AF.Identity, bias=nmr, scale=rs)
                gcol = pcol[:, C_LNG + 8 * ln_idx:C_LNG + 8 * ln_idx + 8]
                bcol = pcol[:, C_LNB + 8 * ln_idx:C_LNB + 8 * ln_idx + 8]
                tbb = nbank()[:, :].bitcast(BF16)
                for kc in range(8):
                    trb(tbb[:, kc * 128:kc * 128 + PT], xb_[0:PT, kc * 128:(kc + 1) * 128], PT)
                for kc in range(8):
                    ts_op("dve", xT[:, kc, nb * 128:nb * 128 + PT], tbb[:, kc * 128:kc * 128 + PT],
                          gcol[:, kc:kc + 1], bcol[:, kc:kc + 1], ALU.mult, ALU.add)
            if ydst is not None:
                act(row, row, AF.Identity, bias=nmr, scale=rs)
                tt_op("dve", row, row, lnt[0:PT, 0:D], ALU.mult)
                tt_op("pool", row, row, lnt[0:PT, D:2 * D], ALU.add)
            else:
                ts_op("pool", row, row, rs, nmr, ALU.mult, ALU.add)
                tt_op("pool", row, row, lnt[0:PT, 0:D], ALU.mult)
                tt_op("pool", row, row, lnt[0:PT, D:2 * D], ALU.add)
            if ydst is not None:
                final_ops.append(dma("pool", "yout", ydst[nb * 128:nb * 128 + PT, :], row))

        for nb in range(NB):
            mmphase(nb)
            if nb > 0:
                post(nb - 1)
        post_a(NB - 1)
        for job in tail_jobs:
            job()
        post(NB - 1, do_a=False)

    def ffn_partA(tile, N, layer, holder):
        pg = PC_GU0 if layer == 0 else PC_GU1
        use_pieces(tile, pg, pg)
        held.add(curpos[pg])
        V = rv_gu[slot(tile, pg)]
        pre = {}
        for sub in range(2):
            pre[sub] = (nbank(), nbank())
        for sub in range(2):
            for t_ in range(2):
                for kc in range(8):
                    mm(pre[sub][t_][:, 0:384], V[:, t_, kc, sub * 128:(sub + 1) * 128], xT[:, kc, 0:384], kc == 0, kc == 7)
        holder["pre"] = pre

    def ffn(tile, N, layer, ydst, side=(), holder=None, tail_jobs=()):
        pg = PC_GU0 if layer == 0 else PC_GU1
        pd = PC_WD0 if layer == 0 else PC_WD1
        side = list(side)
        iters = 22
        for jj in range(11):
            pre = {}
            if jj == 0 and holder is not None and "pre" in holder:
                pre = holder["pre"]
                gpos = curpos[pg]
                held.discard(gpos)
                V = rv_gu[slot(tile, pg)]
                for sub in range(2):
                    for t_ in range(2):
                        for kc in range(8):
                            mm(pre[sub][t_][:, 384:512], V[:, t_, kc, sub * 128:(sub + 1) * 128], xT[:, kc, 384:512],
                               kc == 0, kc == 7)
            else:
                use_pieces(tile, pg + jj, pg + jj)
                gpos = curpos[pg + jj]
                V = rv_gu[slot(tile, pg + jj)]
            for sub in range(2):
                j = jj * 2 + sub
                if sub in pre:
                    bg, bu = pre[sub]
                else:
                    bg, bu = nbank(), nbank()
                    for kc in range(8):
                        mm(bg[:, 0:N], V[:, 0, kc, sub * 128:(sub + 1) * 128], xT[:, kc, 0:N], kc == 0, kc == 7)
                    for kc in range(8):
                        mm(bu[:, 0:N], V[:, 1, kc, sub * 128:(sub + 1) * 128], xT[:, kc, 0:N], kc == 0, kc == 7)
                s_ = sgt[j % 2]
                act(s_[:, 0:N], bg[:, 0:N], AF.Silu)
                tt_op("dve", hT[:, j, 0:N], s_[:, 0:N], bu[:, 0:N], ALU.mult)
                if side and not (pre and sub == 0):
                    held.add(gpos)
                    n = -(-len(side) // (iters - j))
                    for _ in range(n):
                        side.pop(0)()
                    held.discard(gpos)
        for job in side:
            job()
        use_pieces(tile, pd, pd + 5)
        proj_ln(tile, N, NJ, lambda j: hT[:, j, :],
                lambda j, half: rv_wd[slot(tile, pd + j // 4)][:, j % 4, half * 512:(half + 1) * 512],
                2 * layer + 1, ydst, tail_jobs=tail_jobs)

    def stageA_jobs(tile, N, is_last, pre=()):
        jobs = list(pre)
        jobs.append(lambda: load_x_tr(N))

        def j_xr(c):
            if c == 0:
                use_pieces(tile, PC_WIN, PC_WIN)
                held.add(curpos[PC_WIN])
            V = rv_8x512[slot(tile, PC_WIN)]
            b = nbank()
            for kc in range(8):
                mm(b[:, 0:N], V[:, kc, c * 128:(c + 1) * 128], xTa[:, kc, 0:N], kc == 0, kc == 7)
            act(xr_buf[:, c, 3:3 + N], b[:, 0:N], AF.Copy)
            if c == 3:
                held.discard(curpos[PC_WIN])

        def j_g(c):
            if c == 0:
                use_pieces(tile, PC_WIN + 1, PC_WIN + 2)
                held.add(curpos[PC_WIN + 1])
                held.add(curpos[PC_WIN + 2])
            Vg = rv_8x512[slot(tile, PC_WIN + 1)]
            Vv = rv_8x512[slot(tile, PC_WIN + 2)]
            b1, b2 = nbank(), nbank()
            for kc in range(8):
                mm(b1[:, 0:N], Vg[:, kc, c * 128:(c + 1) * 128], xTa[:, kc, 0:N], kc == 0, kc == 7)
            for kc in range(8):
                mm(b2[:, 0:N], Vv[:, kc, c * 128:(c + 1) * 128], xTa[:, kc, 0:N], kc == 0, kc == 7)
            act(sg[:, 0:N], b1[:, 0:N], AF.Sigmoid)
            tt_op("dve", g_buf[:, c, 30:30 + N], b2[:, 0:N], sg[:, 0:N], ALU.mult)
            if is_last:
                tt_op("dve", g32[:, c, :], b2[:, N - 30:N], sg[:, N - 30:N], ALU.mult)
            if c == 3:
                held.discard(curpos[PC_WIN + 1])
                held.discard(curpos[PC_WIN + 2])

        def j_yr(c):
            if c == 0:
                use_pieces(tile, PC_WIN + 3, PC_WIN + 3)
                held.add(curpos[PC_WIN + 3])
            V = rv_8x512[slot(tile, PC_WIN + 3)]
            b = nbank()
            for kc in range(8):
                mm(b[:, 0:N], V[:, kc, c * 128:(c + 1) * 128], xTa[:, kc, 0:N], kc == 0, kc == 7)
            act(gy[:, c, 0:N], b[:, 0:N], AF.Gelu_apprx_tanh)
            if c == 3:
                held.discard(curpos[PC_WIN + 3])

        for c in range(4):
            jobs.append(lambda c=c: j_xr(c))
        for c in range(4):
            jobs.append(lambda c=c: j_g(c))
        for c in range(4):
            jobs.append(lambda c=c: j_yr(c))

        def rec1(c):
            xc, rr, ii, a2, xcb = xc2[c % 2], rr2[c % 2], ii2[c % 2], a22[c % 2], xcb2[c % 2]
            w = lambda k: pcol[:, C_RCW + 4 * c + k:C_RCW + 4 * c + k + 1]
            ts_op("dve", xc[:, 0:N], xr_buf[:, c, 0:N], w(0), pcol[:, C_RCB + c:C_RCB + c + 1], ALU.mult, ALU.add)
            for k in range(1, 4):
                stt(xc[:, 0:N], xr_buf[:, c, k:k + N], w(k), xc[:, 0:N], ALU.mult, ALU.add)
            cp("pool", xr_buf[:, c, 0:3], xr_buf[:, c, N:N + 3])
            act(xcb[:, 0:N], xc[:, 0:N], AF.Copy)

        def rec1b(c):
            xc, rr, ii, a2, xcb = xc2[c % 2], rr2[c % 2], ii2[c % 2], a22[c % 2], xcb2[c % 2]
            b1, b2 = nbank(), nbank()
            mm(b1[:, 0:N], gatesb[:, c, :], xcb[:, 0:N], True, True)
            mm(b2[:, 0:N], gatesb[:, 4 + c, :], xcb[:, 0:N], True, True)
            act(rr[:, 0:N], b1[:, 0:N], AF.Sigmoid, bias=pcol[:, C_GAB + c:C_GAB + c + 1])
            act(ii[:, 0:N], b2[:, 0:N], AF.Sigmoid, bias=pcol[:, C_GXB + c:C_GXB + c + 1])
            act(a2[:, 0:N], rr[:, 0:N], AF.Exp, scale=cpv[:, 4 + c:5 + c])
            act(rr[:, 0:N], rr[:, 0:N], AF.Exp, scale=cpv[:, c:c + 1])
            ts_op("pool", a2[:, 0:N], a2[:, 0:N], 1.0, 0.0, ALU.min, ALU.max)
            act(a2[:, 0:N], a2[:, 0:N], AF.Sqrt, bias=oneb[:, :], scale=-1.0)

        def rec2(c):
            xc, rr, ii, a2 = xc2[c % 2], rr2[c % 2], ii2[c % 2], a22[c % 2]
            tt_op("dve", a2[:, 0:N], a2[:, 0:N], ii[:, 0:N], ALU.mult)
            tt_op("dve", a2[:, 0:N], a2[:, 0:N], xc[:, 0:N], ALU.mult)
            hi_, da_, du_, h0_ = ii[:, 0:N], rr[:, 0:N], a2[:, 0:N], hstate[:, c:c + 1]
            P.op("dve", lambda e, hi_=hi_, da_=da_, du_=du_, h0_=h0_: e.tensor_tensor_scan(hi_, da_, du_, h0_, ALU.mult, ALU.add),
                 [da_, du_, h0_], [hi_])
            cp("dve", hstate[:, c:c + 1], ii[:, N - 1:N])
            tt_op("pool", ro[:, c, 0:N], ii[:, 0:N], gy[:, c, 0:N], ALU.mult)

        s1, s2 = ps[6], ps[7]

        def convpe(c):
            use_pieces(tile, PC_CONV + c, PC_CONV + c)
            V = rv_conv[slot(tile, PC_CONV + c)]
            b = nbank()
            for k in range(31):
                mm(b[:, 0:N], V[:, k, :], g_buf[:, c, k:k + N], k == 0, k == 30)
            cp("pool", g_buf[:, c, 0:30], g_buf[:, c, N:N + 30])
            if c > 0:
                stats(c - 1)
            act(cc[:, c, 0:N], b[:, 0:N], AF.Identity, bias=pcol[:, C_CFB + c:C_CFB + c + 1])
            act(sq2[c % 2][:, 0:N], cc[:, c, 0:N], AF.Square)

        def stats(c):
            mm(s1[:, 0:N], ones[:, :], cc[:, c, 0:N], c == 0, c == 3)
            mm(s2[:, 0:N], ones[:, :], sq2[c % 2][:, 0:N], c == 0, c == 3)

        for f, c in ((rec1, 0), (convpe, 0), (rec1b, 0), (rec1, 1), (rec2, 0), (convpe, 1), (rec1b, 1), (rec1, 2),
                     (rec2, 1), (convpe, 2), (rec1b, 2), (rec1, 3), (rec2, 2), (convpe, 3), (rec1b, 3), (rec2, 3)):
            jobs.append(lambda f=f, c=c: f(c))

        def j_cln():
            stats(3)
            ts_op("dve", mean[:, 0:N], s1[:, 0:N], 1.0 / 512.0, None, ALU.mult)
            tt_op("dve", var[:, 0:N], mean[:, 0:N], mean[:, 0:N], ALU.mult)
            stt(var[:, 0:N], s2[:, 0:N], 1.0 / 512.0, var[:, 0:N], ALU.mult, ALU.subtract)
            act(var[:, 0:N], var[:, 0:N], AF.Sqrt, bias=epsb[:, :])
            P.op("dve", lambda e: e.reciprocal(var[:, 0:N], var[:, 0:N]), [var[:, 0:N]], [var[:, 0:N]])
            for c in range(4):
                tt_op("pool", tt[:, 0:N], cc[:, c, 0:N], mean[:, 0:N], ALU.subtract)
                tt_op("dve", tt[:, 0:N], tt[:, 0:N], var[:, 0:N], ALU.mult)
                act(cn[:, c, 0:N], tt[:, 0:N], AF.Silu, bias=pcol[:, C_CFBE + c:C_CFBE + c + 1],
                    scale=pcol[:, C_CFG + c:C_CFG + c + 1])
        jobs.append(j_cln)
        return jobs

    def projA(tile, N, is_last, oi, tail_jobs=()):
        if is_last:
            b = nbank()
            tr(b[0:4, 0:128], hstate[:, 0:4], 128)
            cp("dve", stg[0:4, 0:128], b[0:4, 0:128])
            final_ops.append(dma("act", "so0", oh_d[oi], stg[0:4, 0:128]))
            b = nbank()
            for c in range(4):
                tr(b[0:3, c * 128:(c + 1) * 128], xr_buf[:, c, 0:3], 128)
            cp("dve", sm_rc[0:3, :], b[0:3, :])
            final_ops.append(dma("act", "so1", orc_d[oi], sm_rc[0:3, :]))
            b = nbank()
            for c in range(4):
                tr(b[0:30, c * 128:(c + 1) * 128], g32[:, c, 0:30], 128)
            cp("dve", sm_cf[0:30, :], b[0:30, :])
            final_ops.append(dma("act", "so2", ocf_d[oi], sm_cf[0:30, :]))
        use_pieces(tile, PC_WOUT, PC_WOUT + 1)
        proj_ln(tile, N, 8, lambda k: (ro[:, k, :] if k < 4 else cn[:, k - 4, :]),
                lambda k, half: rv_8x512[slot(tile, PC_WOUT + half)][:, k, :], 0, None, res=xin, tail_jobs=tail_jobs)

    def q_partA(tile, N, holder):
        use_pieces(tile, PC_Q, PC_Q)
        held.add(curpos[PC_Q])
        V = rv_8x512[slot(tile, PC_Q)]
        pre = {}
        for c4 in range(4):
            pre[c4] = nbank()
        for c4 in range(4):
            for kc in range(8):
                mm(pre[c4][:, 0:384], V[:, kc, c4 * 128:(c4 + 1) * 128], xT[:, kc, 0:384], kc == 0, kc == 7)
        holder["pre"] = pre

    def mixer_c(tile, N, is_first, is_last, oi, tail_jobs=(), holder=None):
        PT = min(N, 128)
        NB = (N + 127) // 128
        for hq in range(2):
            pre = {}
            if hq == 0 and holder is not None and "pre" in holder:
                pre = holder["pre"]
                held.discard(curpos[PC_Q])
                V = rv_8x512[slot(tile, PC_Q)]
                for c4 in range(4):
                    for kc in range(8):
                        mm(pre[c4][:, 384:512], V[:, kc, c4 * 128:(c4 + 1) * 128], xT[:, kc, 384:512], kc == 0, kc == 7)
            else:
                use_pieces(tile, PC_Q + hq, PC_Q + hq)
                V = rv_8x512[slot(tile, PC_Q + hq)]
            for c4 in range(4):
                oc = hq * 4 + c4
                if c4 in pre:
                    b = pre[c4]
                else:
                    b = nbank()
                    for kc in range(8):
                        mm(b[:, 0:N], V[:, kc, c4 * 128:(c4 + 1) * 128], xT[:, kc, 0:N], kc == 0, kc == 7)
                act(qT[:, oc, 0:N], b[:, 0:N], AF.Identity, scale=0.125)
        chk(3.01)
        use_pieces(tile, PC_KDUP, PC_KDUP)
        V = rv_8x512[slot(tile, PC_KDUP)]
        for j in range(4):
            b = nbank()
            for kc in range(8):
                mm(b[:, 0:N], V[:, kc, j * 128:(j + 1) * 128], xT[:, kc, 0:N], kc == 0, kc == 7)
            cp("dve", kT[:, j, 128:128 + N], b[:, 0:N])
        chk(3.02)
        use_pieces(tile, PC_KV, PC_KV)
        V = rv_8x512[slot(tile, PC_KV)]
        for nb in range(NB):
            b = nbank()
            for kc in range(8):
                mm(b[0:PT, :], xT[:, kc, nb * 128:nb * 128 + PT], V[:, kc, :], kc == 0, kc == 7)
            act(vbuf[0:PT, 1 + nb, :], b[0:PT, 256:512], AF.Copy)
            if nb == 1:
                chk(3.03)
            if nb == NB - 1:
                chk(3.04)
            if is_last and nb == NB - 1:
                cp("dve", kvo[0:PT, :], b[0:PT, :])
                chk(3.05)
                r0 = 128 - PT
                final_ops.append(dma("act", "so3", ok_d[oi][r0:128, :], kvo[0:PT, 0:256]))
                final_ops.append(dma("act", "so4", ov_d[oi][r0:128, :], kvo[0:PT, 256:512]))
        chk(3.1)
        QN = PT
        units = [(qb, hg) for qb in range(NB) for hg in range(2)]
        info = {}

        def s_phase(ui):
            qb, hg = units[ui]
            par = ui % 2
            blocks = [(qb * 128, 128, qb, 0), (qb * 128 + 128, QN, qb + 1, 128)]
            if is_first and qb == 0:
                blocks = blocks[1:]
            kstart = blocks[0][0]
            d0 = blocks[0][3]
            nk = sum(bk[1] for bk in blocks)
            for hp in range(4):
                bb = [nbank(), nbank()]
                for hh in range(2):
                    h = hg * 8 + hp * 2 + hh
                    oc, half, kv = h // 2, h % 2, h // 4
                    pr = slice(half * 64, half * 64 + 64)
                    mm(bb[hh][0:QN, 0:nk], qT[pr, oc, qb * 128:qb * 128 + QN],
                       kT[pr, kv, kstart:kstart + nk], True, True)
                for hh in range(2):
                    h = hg * 8 + hp * 2 + hh
                    stt(sbb[0:QN, hp * 2 + hh, 0:nk], dist[0:QN, d0:d0 + nk], -SLOPES[h],
                        bb[hh][0:QN, 0:nk], ALU.mult, ALU.add)
            mx = smA[0:QN, par, 0:8]
            negm = smA[0:QN, par, 8:16]
            se = smA[0:QN, par, 16:24]
            rsum = smA[0:QN, par, 24:32]
            sin_ = sbb[0:QN, :, 0:nk]
            P.op("dve", lambda e, sin_=sin_, mx=mx: e.tensor_reduce(mx, sin_, AX.X, ALU.max), [sin_], [mx])
            stt(negm, mx, -1.0, negsink[0:QN, hg * 8:hg * 8 + 8], ALU.mult, ALU.min)
            tt_op("dve", se, negm, sinkb[0:QN, hg * 8:hg * 8 + 8], ALU.add)
            info[ui] = (blocks, kstart, nk)

        def e_phase(ui):
            qb, hg = units[ui]
            par = ui % 2
            blocks, kstart, nk = info[ui]
            se = smA[0:QN, par, 16:24]
            act(se, se, AF.Exp)
            for hl in range(8):
                act(pbf[par][0:QN, hl, 0:nk], sbb[0:QN, hl, 0:nk], AF.Exp, bias=smA[0:QN, par, 8 + hl:9 + hl],
                    accum=smA[0:QN, par, 24 + hl:25 + hl])

        def pe_phase(ui):
            qb, hg = units[ui]
            par = ui % 2
            blocks, kstart, nk = info[ui]
            ob = ps[6 + par]
            se = smA[0:QN, par, 16:24]
            tt_op("dve", se, se, smA[0:QN, par, 24:32], ALU.add)
            P.op("dve", lambda e, se=se: e.reciprocal(se, se), [se], [se])
            allfull = all(bk[1] == 128 for bk in blocks) and QN == 128
            def tphase(hl):
                pT_ = pT[hl % 2]
                tbb = nbank()[:, :].bitcast(BF16)
                for bi, (kcol, kn, vblk, dcol) in enumerate(blocks):
                    off = kcol - kstart
                    trb(tbb[0:kn, bi * 128:bi * 128 + QN], pbf[par][0:QN, hl, off:off + kn], QN)
                ev = "act"
                if allfull:
                    nb_ = len(blocks)
                    cp(ev, pT_[:, 0:nb_, :], tbb[:, 0:nb_ * 128].rearrange("p (b q) -> p b q", q=128))
                else:
                    for bi, (kcol, kn, vblk, dcol) in enumerate(blocks):
                        cp(ev, pT_[0:kn, bi, 0:QN], tbb[0:kn, bi * 128:bi * 128 + QN])

            def pvphase(hl):
                h = hg * 8 + hl
                kv = h // 4
                pT_ = pT[hl % 2]
                for bi, (kcol, kn, vblk, dcol) in enumerate(blocks):
                    mm(ob[0:QN, hl * 64:(hl + 1) * 64], pT_[0:kn, bi, 0:QN], vbuf[0:kn, vblk, kv * 64:kv * 64 + 64],
                       bi == 0, bi == len(blocks) - 1)

            tphase(0)
            for hl in range(1, 8):
                tphase(hl)
                pvphase(hl - 1)
            pvphase(7)
            ot = otok[qb % 2]
            rden = smA[0:QN, par, 16:24].unsqueeze(2).broadcast_to([QN, 8, 64])
            tt_op("dve", ot[0:QN, hg * 512:(hg + 1) * 512].rearrange("p (h d) -> p h d", d=64),
                  ob[0:QN, :].rearrange("p (h d) -> p h d", d=64), rden, ALU.mult)
            chk(3.5)
            if hg == 1:
                tbb = nbank()[:, :].bitcast(BF16)
                for oc in range(8):
                    trb(tbb[:, oc * 128:oc * 128 + QN], ot[0:QN, oc * 128:(oc + 1) * 128], QN)
                cp("act", oT[:, :, qb * 128:qb * 128 + QN], tbb[:, :].rearrange("p (c q) -> p c q", q=128)[:, :, 0:QN])

        s_phase(0)
        e_phase(0)
        for ui in range(1, len(units)):
            s_phase(ui)
            pe_phase(ui - 1)
            e_phase(ui)
        pe_phase(len(units) - 1)
        if not is_last:
            cp("pool", kT[:, :, 0:128], kT[:, :, N:N + 128])
            cp("pool", vbuf[:, 0, :], vbuf[:, NB, :])
        use_pieces(tile, PC_WOC, PC_WOC + 1)
        proj_ln(tile, N, 8, lambda k: oT[:, k, :],
                lambda k, half: rv_8x512[slot(tile, PC_WOC + half)][:, k, :], 2, None, tail_jobs=tail_jobs)

    epsb = sb("epsb", [128, 1], F32)[0]
    oneb = sb("oneb", [128, 1], F32)[0]
    sm2 = sb("sm2", [128, 8], F32)[0]
    assert cur[0] <= 229344, cur[0]
    P.op("pool", lambda e: e.memset(epsb[:, :], LN_EPS), [], [epsb[:, :]])
    P.op("pool", lambda e: e.memset(oneb[:, :], 1.0), [], [oneb[:, :]])

    def load_x_dma(src, N):
        PT = min(N, 128)
        NB = (N + 127) // 128
        if NB > 1:
            dma("act", "xin", xin[:, 0:NB, :], src.rearrange("(nb p) d -> p nb d", p=128))
        else:
            dma("act", "xin", xin[0:PT, 0, :], src)

    def load_x_tr(N):
        PT = min(N, 128)
        NB = (N + 127) // 128
        for nb in range(NB):
            xb_ = xnb[nb % 2]
            cp("act" if nb % 2 else "dve", xb_[0:PT, :], xin[0:PT, nb, :])
            tbb = nbank()[:, :].bitcast(BF16)
            for kc in range(8):
                trb(tbb[:, kc * 128:kc * 128 + PT], xb_[0:PT, kc * 128:(kc + 1) * 128], PT)
            cp("dve" if nb % 2 else "act", xTa[:, :, nb * 128:nb * 128 + PT],
               tbb[:, :].rearrange("p (c q) -> p c q", q=128)[:, :, 0:PT])

    P.op("pool", lambda e: e.memset(xr_buf[:, :, 0:3], 0.0), [], [xr_buf[:, :, 0:3]])
    P.op("pool", lambda e: e.memset(g_buf[:, :, 0:30], 0.0), [], [g_buf[:, :, 0:30]])
    P.op("pool", lambda e: e.memset(hstate[:, :], 0.0), [], [hstate[:, :]])
    P.op("pool", lambda e: e.memset(kT[:, :, 0:128], 0.0), [], [kT[:, :, 0:128]])
    P.op("pool", lambda e: e.memset(vbuf[:, 0, :], 0.0), [], [vbuf[:, 0, :]])

    def sample_init():
        dma("act", "c2", xr_buf[:, :, 0:3], st_rc_d.rearrange("p (c k) -> p c k", k=3))
        dma("act", "c3", hstate[:, :], st_h_d)
        dma("act", "c0", cc[:, 0, 0:120], st_cf_d)
        cp("dve", g_buf[:, :, 0:30], cc[:, 0, 0:120].rearrange("p (c k) -> p c k", k=30))
        dma("act", "c1", cc[:, 1, :], st_kT_d)
        cp("dve", kT[:, :, 0:128], cc[:, 1, :].rearrange("p (c k) -> p c k", k=128))
        dma("act", "c2", cc[:, 2, 0:256], st_v_d)
        cp("dve", vbuf[:, 0, :], cc[:, 2, 0:256])
        final_ops.append(dma("act", "so5", ok_d[1][0:64, :], ck_d[64:128, :]))
        final_ops.append(dma("act", "so6", ov_d[1][0:64, :], cv_d[64:128, :]))

    tiles = [(t, 512, 0) for t in range(n_tiles)] + ([(n_tiles, DEC, 1)] if with_sample else [])
    load_x_dma(x_d[0:512, :], 512)
    for job in stageA_jobs(0, 512, n_tiles == 1):
        job()
    for idx, (t, N, oi) in enumerate(tiles):
        is_sample = oi == 1
        last = (t == n_tiles - 1) or is_sample
        h0, hq_, h1 = {}, {}, {}
        split = N == 512
        projA(t, N, last, oi, tail_jobs=[lambda: ffn_partA(t, N, 0, h0)] if split else [])
        nxt = tiles[idx + 1] if idx + 1 < len(tiles) else None
        if nxt is not None:
            nt, nN, noi = nxt
            load_x_dma(xs_d if noi == 1 else x_d[nt * 512:(nt + 1) * 512, :], nN)
        ffn(t, N, 0, None, holder=h0, tail_jobs=[lambda: q_partA(t, N, hq_)] if split else [])
        side = []
        if nxt is not None:
            nt, nN, noi = nxt
            nlast = (nt == n_tiles - 1) or noi == 1
            side = stageA_jobs(nt, nN, nlast, pre=[sample_init] if noi == 1 else [])
        npre = min(len(side), 6 if (nxt is not None and nxt[2] == 1) else 5)
        tj = side[:npre] + ([lambda: ffn_partA(t, N, 1, h1)] if split else [])
        mixer_c(t, N, (t == 0 and not is_sample), last, oi, tail_jobs=tj, holder=hq_)
        ffn(t, N, 1, ys_d if is_sample else y_d[t * 512:(t + 1) * 512, :], side=side[npre:], holder=h1)

    P.wait_all("sp", final_ops + list(P.lastdma.values()))
    P.wait_all("act", final_ops)
    if order is not None:
        P.emit()
    return nc, seq_rec


def build_program(n_tiles=SEQ // 512, with_sample=True, seq=SEQ):
    _, order = _build(n_tiles, with_sample, seq, None)
    nc, order2 = _build(n_tiles, with_sample, seq, list(order))
    assert order2 == order
    return nc


def _pieces(inp):
    f = np.float32
    tape = np.zeros((NPIECE, 128, PIECE), f)

    def kmajor(w, c0, ncol):
        return w[:, c0:c0 + ncol].reshape(8, 128, ncol).transpose(1, 0, 2)

    w_in = inp["w_in_ab"][0]
    for g, c0 in enumerate((0, 1536, 1024, 512)):
        tape[PC_WIN + g] = kmajor(w_in, c0, 512).reshape(128, PIECE)
    cw = inp["cf_conv_w"][0]
    for c in range(4):
        pc = np.zeros((128, 32, 128), f)
        idx = np.arange(128)
        for k in range(31):
            pc[idx, k, idx] = cw[k, c * 128:(c + 1) * 128]
        tape[PC_CONV + c] = pc.reshape(128, PIECE)
    wo = inp["w_out_ab"][0]
    for h in range(2):
        tape[PC_WOUT + h] = kmajor(wo, h * 512, 512).reshape(128, PIECE)
    for layer, (pg, pd) in enumerate(((PC_GU0, PC_WD0), (PC_GU1, PC_WD1))):
        wg, wu, wd = inp["w_ff_gate"][layer], inp["w_ff_up"][layer], inp["w_ff_down"][layer]
        for jj in range(11):
            pc = np.stack([kmajor(wg, jj * 256, 256), kmajor(wu, jj * 256, 256)], axis=1)
            tape[pg + jj] = pc.reshape(128, PIECE)
        wdk = wd.reshape(NJ, 128, D).transpose(1, 0, 2)
        for q in range(6):
            pc = np.zeros((128, 4, D), f)
            n = min(4, NJ - q * 4)
            pc[:, 0:n] = wdk[:, q * 4:q * 4 + n]
            tape[pd + q] = pc.reshape(128, PIECE)
    wqkv = inp["w_qkv"][0]
    for h in range(2):
        tape[PC_Q + h] = kmajor(wqkv, h * 512, 512).reshape(128, PIECE)
    wk = kmajor(wqkv, 1024, 256).reshape(128, 8, 4, 1, 64)
    tape[PC_KDUP] = np.broadcast_to(wk, (128, 8, 4, 2, 64)).reshape(128, PIECE)
    tape[PC_KV] = kmajor(wqkv, 1024, 512).reshape(128, PIECE)
    woc = inp["w_out_c"][0]
    for h in range(2):
        tape[PC_WOC + h] = kmajor(woc, h * 512, 512).reshape(128, PIECE)
    return tape


def _col(v):
    return np.ascontiguousarray(v.reshape(-1, 128).T)


def _shared_inputs(inp):
    f = np.float32
    sh = {}
    sh["tape32"] = _pieces(inp)
    sh["ident"] = np.eye(128, dtype=f)
    sh["ones"] = np.ones((128, 128), f)
    i = np.arange(128)[:, None]
    s = np.arange(256)[None, :]
    dist = np.abs(128 + i - s).astype(f)
    qc = i // 64
    kc = s // 64 - 2
    valid = (kc <= qc) & (kc >= qc - 2)
    dist = np.where(valid, dist, f(1e10)).astype(f)
    sh["dist"] = dist
    pcol = np.zeros((128, NCOL), f)
    rcw = inp["rec_conv_w"][0]
    for c in range(4):
        for k in range(4):
            pcol[:, C_RCW + 4 * c + k] = rcw[k, c * 128:(c + 1) * 128]
    for name, col in (("rec_conv_b", C_RCB), ("rec_gate_a_b", C_GAB), ("rec_gate_x_b", C_GXB), ("rec_lambda", C_LAM),
                      ("cf_conv_b", C_CFB), ("cf_norm_g", C_CFG), ("cf_norm_b", C_CFBE)):
        pcol[:, col:col + 4] = _col(inp[name][0])
    lng = [inp["ln_mix_g"][0], inp["ln_ff_g"][0], inp["ln_mix_g"][1], inp["ln_ff_g"][1]]
    lnb = [inp["ln_mix_b"][0], inp["ln_ff_b"][0], inp["ln_mix_b"][1], inp["ln_ff_b"][1]]
    lntab = np.zeros((4, 128, 2 * D), f)
    for l in range(4):
        pcol[:, C_LNG + 8 * l:C_LNG + 8 * l + 8] = _col(lng[l])
        pcol[:, C_LNB + 8 * l:C_LNB + 8 * l + 8] = _col(lnb[l])
        lntab[l, :, 0:D] = lng[l][None, :]
        lntab[l, :, D:] = lnb[l][None, :]
    sh["pcol"] = pcol
    sh["lntab"] = lntab
    gates = np.zeros((128, 8, 128), f)
    for t_, nm in enumerate(("rec_gate_a_w", "rec_gate_x_w")):
        w = inp[nm][0]
        for c in range(4):
            gates[0:64, 4 * t_ + c, 0:64] = w[2 * c]
            gates[64:128, 4 * t_ + c, 64:128] = w[2 * c + 1]
    sh["gates"] = gates.reshape(128, 1024)
    sh["sinkb"] = np.broadcast_to(inp["attn_sinks"][0][None, :], (128, NHEAD)).astype(f).copy()
    return sh


def _core_inputs(inp, b, seq):
    f = np.float32
    m = {}
    m["x"] = np.ascontiguousarray(inp["x_prompt"][b, :seq])
    m["xs"] = np.ascontiguousarray(inp["x_sample"][b])
    rc = inp["state_rec_conv"][0, b]
    m["st_rc"] = np.ascontiguousarray(rc.reshape(3, 4, 128).transpose(2, 1, 0)).reshape(128, 12)
    cf = inp["state_cf_conv"][0, b]
    m["st_cf"] = np.ascontiguousarray(cf.reshape(30, 4, 128).transpose(2, 1, 0)).reshape(128, 120)
    m["st_h"] = _col(inp["state_rec_h"][0, b])
    ck = inp["cache_k"][0, b]
    kTt = ck.transpose(1, 2, 0)
    kd = np.stack([kTt, kTt], axis=1)
    m["st_kT"] = np.ascontiguousarray(kd.reshape(4, 128, 128).transpose(1, 0, 2)).reshape(128, 512)
    m["st_v"] = np.ascontiguousarray(inp["cache_v"][0, b].reshape(128, 256))
    m["ck"] = np.ascontiguousarray(ck.reshape(128, 256))
    m["cv"] = np.ascontiguousarray(inp["cache_v"][0, b].reshape(128, 256))
    return {k: np.asarray(v, f) for k, v in m.items()}


_PROG_CACHE = {}


def run(inp, ncores=NCORE, seq=SEQ, with_sample=True):
    inp = {k: np.asarray(v) for k, v in inp.items()}
    key = (seq, with_sample)
    if key not in _PROG_CACHE:
        _PROG_CACHE[key] = build_program(seq // 512, with_sample, seq)
    nc = _PROG_CACHE[key]
    sh = _shared_inputs(inp)
    in_maps = []
    for b in range(ncores):
        m = dict(sh)
        m.update(_core_inputs(inp, b, seq))
        in_maps.append(m)
    res = run_bass_kernel_spmd(nc, in_maps, core_ids=list(range(ncores)))
    R = res.results
    f = np.float32

    def st(name, shape):
        return np.stack([np.asarray(R[b][name], f).reshape(shape) for b in range(ncores)])

    y = st("y", (seq, D))
    ys = st("ys", (DEC, D))
    outs = (y, ys,
            st("o_h_p", (512,))[None], st("o_h_s", (512,))[None],
            st("o_rc_p", (3, 512))[None], st("o_rc_s", (3, 512))[None],
            st("o_cf_p", (30, 512))[None], st("o_cf_s", (30, 512))[None],
            st("o_k_p", (128, 4, 64))[None], st("o_k_s", (128, 4, 64))[None],
            st("o_v_p", (128, 4, 64))[None], st("o_v_s", (128, 4, 64))[None])
    return outs


def kernel(**inputs):
    return run(inputs)
```

```python
import contextlib
import numpy as np
import concourse.bass as bass
import concourse.mybir as mybir
from concourse.bass_utils import run_bass_kernel_spmd

F32 = mybir.dt.float32
BF16 = mybir.dt.bfloat16
AF = mybir.ActivationFunctionType
ALU = mybir.AluOpType
AX = mybir.AxisListType

D = 1024
SEQ = 8192
NCORE = 8
DEC = 64
D_FF = 2816
NJ = D_FF // 128
ALPHA = 4.0 ** 0.25
LN_EPS = 1e-5
NHEAD = 16
SLOPES = [2.0 ** (-8.0 * (h + 1) / NHEAD) for h in range(NHEAD)]
RING = 8
PIECE = 4096
SB_BASE = 16512

PC_WIN = 0
PC_CONV = 4
PC_WOUT = 8
PC_GU0 = 10
PC_WD0 = 21
PC_Q = 27
PC_KDUP = 29
PC_KV = 30
PC_WOC = 31
PC_GU1 = 33
PC_WD1 = 44
NPIECE = 50

C_RCW, C_RCB, C_GAB, C_GXB, C_LAM, C_CFB, C_CFG, C_CFBE, C_LNG, C_LNB = 0, 16, 20, 24, 28, 32, 36, 40, 44, 76
NCOL = 108


class Op:
    __slots__ = ("eng", "fn", "waits", "semkey", "seq", "needs_inc", "value", "is_dma")


class Prog:
    ENGS = ["pe", "act", "dve", "pool", "sp"]

    def __init__(self, nc):
        self.nc = nc
        self.ops = {e: [] for e in self.ENGS}
        self.recs = {}
        self.waited = {e: {} for e in self.ENGS}
        self.semseq = {}
        self.lastdma = {}
        self.sbuf_addr = {}
        self.pending = {}

    def region(self, ap):
        t = ap.tensor
        name = t.name
        pat = [(int(s), int(n)) for s, n in ap.ap]
        off = int(ap.offset)
        esz = mybir.dt.size(ap.dtype)
        cls = type(t).__name__
        if cls.startswith("DRam"):
            lo = off
            hi = off + sum((n - 1) * abs(s) for s, n in pat) + 1
            return ("d:" + name, 0, 1, lo * esz, hi * esz)
        pstep = pat[0][0]
        p0 = off // pstep if pstep else 0
        fo = off - p0 * pstep
        p1 = p0 + pat[0][1]
        ext = sum((n - 1) * abs(s) for s, n in pat[1:]) + 1
        if cls.startswith("PSum"):
            return ("p:" + name, 0, 128, 0, 2048)
        base = self.sbuf_addr[name]
        return ("sb", p0, p1, base + fo * esz, base + (fo + ext) * esz)

    def _psum_guard(self, op, reads, writes, start):
        for kind, aps in (("r", reads), ("w", writes)):
            for ap in aps:
                if not type(ap.tensor).__name__.startswith("PSum"):
                    continue
                pat = [(int(s_), int(n)) for s_, n in ap.ap]
                off = int(ap.offset)
                pstep = pat[0][0]
                fo = off - (off // pstep) * pstep if pstep else 0
                ext = sum((n - 1) * abs(s_) for s_, n in pat[1:]) + 1
                esz = mybir.dt.size(ap.dtype)
                lo, hi = fo * esz, (fo + ext) * esz
                pend = self.pending.setdefault(ap.tensor.name, [])
                if kind == "r" and op.eng != "pe":
                    pend[:] = [iv for iv in pend if not (iv[0] < hi and lo < iv[1])]
                elif kind == "w" and op.eng == "pe":
                    if start:
                        for iv in pend:
                            assert not (iv[0] < hi and lo < iv[1]), ("PSUM reuse before consumption", ap.tensor.name, lo, hi, iv)
                        pend.append((lo, hi))

    def _deps(self, op, reads, writes):
        deps = []
        for kind, aps in (("w", writes), ("r", reads)):
            for ap in aps:
                key, p0, p1, lo, hi = self.region(ap)
                lst = self.recs.setdefault(key, [])
                keep = []
                for r in lst:
                    rp0, rp1, rlo, rhi, rkind, rop = r
                    if rp0 < p1 and p0 < rp1 and rlo < hi and lo < rhi:
                        if kind == "r":
                            if rkind == "w":
                                deps.append((rop, "raw"))
                            elif key[0] == "p" and rop.eng != op.eng:
                                deps.append((rop, "rar"))
                            keep.append(r)
                        else:
                            deps.append((rop, "waw" if rkind == "w" else "war"))
                            if p0 <= rp0 and rp1 <= p1 and lo <= rlo and rhi <= hi:
                                continue
                            keep.append(r)
                    else:
                        keep.append(r)
                if kind == "r":
                    keep = [r for r in keep if not (r[4] == "r" and r[5].semkey == op.semkey and not r[5].is_dma
                                                    and not op.is_dma and p0 <= r[0] and r[1] <= p1 and lo <= r[2] and r[3] <= hi)]
                keep.append((p0, p1, lo, hi, kind, op))
                self.recs[key] = keep
        return deps

    def _add_waits(self, op, deps):
        w = self.waited[op.eng]
        for rop, kind in deps:
            if rop is op:
                continue
            if not rop.is_dma and not op.is_dma and rop.eng == op.eng:
                if op.eng == "pe":
                    continue
            if w.get(rop.semkey, -1) >= rop.seq:
                continue
            w[rop.semkey] = rop.seq
            rop.needs_inc = True
            op.waits.append(rop)

    def op(self, eng, fn, reads=(), writes=(), start=True):
        o = Op()
        o.eng, o.fn, o.waits, o.semkey, o.is_dma = eng, fn, [], eng, False
        o.seq = len(self.ops[eng])
        o.needs_inc, o.value = False, None
        self._psum_guard(o, reads, writes, start)
        self._add_waits(o, self._deps(o, reads, writes))
        self.ops[eng].append(o)
        return o

    def dma(self, q, sem, fn, reads=(), writes=()):
        o = Op()
        o.eng, o.fn, o.waits, o.semkey, o.is_dma = q, fn, [], "dma:" + sem, True
        o.seq = self.semseq.get(sem, 0)
        self.semseq[sem] = o.seq + 1
        o.needs_inc, o.value = True, 16 * (o.seq + 1)
        deps = self._deps(o, reads, writes)
        prev = self.lastdma.get(sem)
        if prev is not None:
            deps.append((prev, "raw"))
        self.lastdma[sem] = o
        self._add_waits(o, deps)
        self.ops[q].append(o)
        return o

    def wait_all(self, eng, oplist):
        o = Op()
        o.eng, o.fn, o.waits, o.semkey, o.is_dma = eng, None, [], eng, False
        o.seq = len(self.ops[eng])
        o.needs_inc, o.value = False, None
        self._add_waits(o, [(x, "raw") for x in oplist])
        self.ops[eng].append(o)

    def emit(self):
        nc = self.nc
        for e in self.ENGS:
            cnt = 0
            for o in self.ops[e]:
                if o.is_dma:
                    continue
                if o.needs_inc:
                    cnt += 1
                    o.value = cnt
        with contextlib.ExitStack() as es:
            sems = {}
            for e in self.ENGS:
                sems[e] = es.enter_context(nc.semaphore("s_" + e))
            for s in self.semseq:
                sems["dma:" + s] = es.enter_context(nc.semaphore("d_" + s))
            block = es.enter_context(nc.Block())

            def run(ename):
                def body(eng):
                    for o in self.ops[ename]:
                        for w in o.waits:
                            eng.wait_ge(sems[w.semkey], w.value)
                        if o.fn is None:
                            continue
                        ins = o.fn(eng)
                        if o.is_dma:
                            ins.then_inc(sems[o.semkey], 16)
                        elif o.needs_inc:
                            ins.then_inc(sems[o.semkey], 1)
                return body

            block.tensor(run("pe"))
            block.scalar(run("act"))
            block.vector(run("dve"))
            block.gpsimd(run("pool"))
            block.sync(run("sp"))


def _build(n_tiles, with_sample, seq, order):
    nc = bass.Bass("TRN2", target_bir_lowering=False)
    P = Prog(nc)

    def din(name, shape, dt=F32):
        return nc.dram_tensor(name, list(shape), dt, kind="ExternalInput").ap()

    def dout(name, shape, dt=F32):
        return nc.dram_tensor(name, list(shape), dt, kind="ExternalOutput").ap()

    x_d = din("x", [seq, D])
    xs_d = din("xs", [DEC, D])
    tape32 = din("tape32", [NPIECE, 128, PIECE])
    ident_d = din("ident", [128, 128])
    ones_d = din("ones", [128, 128])
    dist_d = din("dist", [128, 256])
    pcol_d = din("pcol", [128, NCOL])
    lntab_d = din("lntab", [4, 128, 2 * D])
    gates_d = din("gates", [128, 8 * 128])
    sinkb_d = din("sinkb", [128, NHEAD])
    st_rc_d = din("st_rc", [128, 12])
    st_cf_d = din("st_cf", [128, 120])
    st_h_d = din("st_h", [128, 4])
    st_kT_d = din("st_kT", [128, 512])
    st_v_d = din("st_v", [128, 256])
    ck_d = din("ck", [128, 256])
    cv_d = din("cv", [128, 256])

    y_d = dout("y", [seq, D])
    ys_d = dout("ys", [DEC, D])
    oh_d = [dout("o_h_p", [4, 128]), dout("o_h_s", [4, 128])]
    orc_d = [dout("o_rc_p", [3, 512]), dout("o_rc_s", [3, 512])]
    ocf_d = [dout("o_cf_p", [30, 512]), dout("o_cf_s", [30, 512])]
    ok_d = [dout("o_k_p", [128, 256]), dout("o_k_s", [128, 256])]
    ov_d = [dout("o_v_p", [128, 256]), dout("o_v_s", [128, 256])]

    tape16 = nc.dram_tensor("tape16", [NPIECE, 128, PIECE], BF16, kind="Internal").ap()

    cur = [SB_BASE]

    def sb(name, shape, dt, at=None):
        esz = mybir.dt.size(dt)
        n = 1
        for s in shape[1:]:
            n *= s
        nbytes = n * esz
        if at is None:
            off = (cur[0] + 63) // 64 * 64
            cur[0] = off + nbytes
        else:
            off = at
        t = nc.alloc_sbuf_tensor_at(name, list(shape), dt, offset=off)
        P.sbuf_addr[t.name] = off
        return t, off

    ring_off = []
    rv_flat, rv_8x512, rv_conv, rv_gu, rv_wd = [], [], [], [], []
    for s in range(RING):
        t, off = sb(f"ring{s}", [128, PIECE], BF16)
        ring_off.append(off)
        rv_flat.append(t)
        rv_8x512.append(t[:, :].rearrange("p (k n) -> p k n", n=512))
        rv_conv.append(t[:, :].rearrange("p (k n) -> p k n", n=128))
        rv_gu.append(t[:, :].rearrange("p (t k n) -> p t k n", t=2, k=8))
        rv_wd.append(t[:, :].rearrange("p (j n) -> p j n", n=1024))
    xt = sb("xt", [128, 4, D], F32)[0]
    xT = sb("xT", [128, 8, 512], BF16)[0]
    xin = sb("xin", [128, 4, D], F32)[0]
    lnt = sb("lnt", [128, 2 * D], F32)[0]
    ident = sb("identt", [128, 128], F32)[0]
    ones = sb("onest", [128, 128], F32)[0]
    dist = sb("distt", [128, 256], F32)[0]
    pcol = sb("pcolt", [128, NCOL], F32)[0]
    gates32 = None
    gatesb = sb("gatesb", [128, 8, 128], BF16)[0]
    sinkb = sb("sinkbt", [128, NHEAD], F32)[0]
    identb = sb("identb", [128, 128], BF16)[0]
    smA = sb("smA", [128, 2, 32], F32)[0]
    xnb1 = sb("xnb", [128, D], BF16)[0]
    xnb = [xnb1, xnb1]
    negsink = sb("negsink", [128, NHEAD], F32)[0]
    cpv = sb("cpv", [128, 8], F32)[0]
    sm = sb("sm", [128, 128], F32)[0]
    xr_buf = sb("xr_buf", [128, 4, 3 + 512], F32)[0]
    g_buf = sb("g_buf", [128, 4, 30 + 512], BF16)[0]
    g32 = sb("g32", [128, 4, 30], F32)[0]
    hstate = sb("hstate", [128, 4], F32)[0]
    kT = sb("kT", [128, 4, 128 + 512], BF16)[0]
    vbuf = sb("vbuf", [128, 5, 256], BF16)[0]
    XB = (cur[0] + 63) // 64 * 64
    cur[0] = XB
    gy = sb("gy", [128, 4, 512], F32)[0]
    sg = sb("sg", [128, 512], F32)[0]
    xc2 = [sb(f"xc{i}", [128, 512], F32)[0] for i in range(2)]
    rr2 = [sb(f"rr{i}", [128, 512], F32)[0] for i in range(2)]
    ii2 = [sb(f"ii{i}", [128, 512], F32)[0] for i in range(2)]
    a22 = [sb(f"a2{i}", [128, 512], F32)[0] for i in range(2)]
    xcb2 = [sb(f"xcb{i}", [128, 512], BF16)[0] for i in range(2)]
    ro = sb("ro", [128, 4, 512], BF16)[0]
    cc = sb("cc", [128, 4, 512], F32)[0]
    sq = sb("sq", [128, 512], F32)[0]
    sq2 = [sq, sg]
    mean, var, tt = xc2[0], rr2[0], ii2[0]
    cn = gy[:, :, :].bitcast(BF16).rearrange("p c n -> p (c n)")[:, 0:2048].rearrange("p (c n) -> p c n", n=512)
    XE = cur[0]
    cur[0] = XB
    qT = sb("qT", [128, 8, 512], BF16)[0]
    oT = sb("oT", [128, 8, 512], BF16)[0]
    sbb, sbb_off = sb("sbb", [128, 8, 256], F32)
    xnb_b = sb("xnb_b", [128, D], BF16, at=sbb_off)[0]
    xnb[1] = xnb_b
    pbf = [sb(f"pbf{i}", [128, 8, 256], BF16)[0] for i in range(2)]
    pT = [sb(f"pT{i}", [128, 2, 128], BF16)[0] for i in range(2)]
    otok1 = sb("otok", [128, D], BF16)[0]
    otok = [otok1, otok1]
    stg = sb("stg", [128, 512], F32)[0]
    sm_rc = sb("sm_rc", [128, 512], F32)[0]
    sm_cf = sm_rc
    kvo = sb("kvo", [128, 512], F32)[0]
    XE = max(cur[0], XE)
    cur[0] = XE
    hT = sb("hT", [128, NJ, 512], BF16)[0]
    xTa = hT[:, 14:22, :]
    sgt1 = sb("sgt", [128, 512], F32)[0]
    sgt = [sgt1, sgt1]
    assert cur[0] <= 229344, cur[0]

    ps = [nc.alloc_psum_tensor(f"ps{i}", [128, 512], F32) for i in range(8)]
    bank = [0]

    def nbank():
        b = bank[0]
        bank[0] = (b + 1) % 6
        return ps[b]

    def mm(out, lhsT, rhs, start, stop):
        P.op("pe", lambda e: e.matmul(out, lhsT, rhs, start=start, stop=stop), [lhsT, rhs], [out], start=start)

    def tr(out, in_, n):
        idn = ident[0:n, 0:n]
        P.op("pe", lambda e: e.transpose(out, in_, idn), [in_, idn], [out])

    def trb(out, in_, n):
        idn = identb[0:n, 0:n]
        P.op("pe", lambda e: e.transpose(out, in_, idn), [in_, idn], [out])

    def act(out, in_, func, bias=None, scale=None, accum=None):
        rd = [in_]
        kw = {}
        if bias is not None:
            kw["bias"] = bias
            if not isinstance(bias, float):
                rd.append(bias)
        if scale is not None:
            kw["scale"] = scale
            if not isinstance(scale, float):
                rd.append(scale)
        wr = [out]
        if accum is not None:
            kw["accum_out"] = accum
            wr.append(accum)
        P.op("act", lambda e: e.activation(out, in_, func, **kw), rd, wr)

    def tt_op(eng, out, a, b, op):
        P.op(eng, lambda e: e.tensor_tensor(out, a, b, op), [a, b], [out])

    def ts_op(eng, out, a, s1, s2, op0, op1=None):
        rd = [a] + [s for s in (s1, s2) if s is not None and not isinstance(s, float)]
        if op1 is None:
            P.op(eng, lambda e: e.tensor_scalar(out, a, s1, None, op0), rd, [out])
        else:
            P.op(eng, lambda e: e.tensor_scalar(out, a, s1, s2, op0, op1), rd, [out])

    def stt(out, a, s, b, op0, op1):
        rd = [a, b] + ([] if isinstance(s, float) else [s])
        P.op("dve", lambda e: e.scalar_tensor_tensor(out, a, s, b, op0, op1), rd, [out])

    def cp(eng, out, in_):
        if eng == "act":
            act(out, in_, AF.Copy)
        else:
            P.op(eng, lambda e: e.tensor_copy(out, in_), [in_], [out])

    def dma(q, sem, out, in_, **kw):
        return P.dma(q, sem, lambda e: e.dma_start(out=out, in_=in_, **kw), [in_], [out])

    for i in range(NPIECE):
        src = tape32[i].rearrange("p (a b) -> p a b", b=2048)
        dst = tape16[i].rearrange("p (a b) -> p a b", b=2048)
        dma("pool", f"cv{i % 4}", dst, src)

    seq_rec = []
    nload = [0]
    held = set()
    curpos = {}

    def use_pieces(tile, first, last):
        ids = list(range(first, last + 1))
        p0 = len(seq_rec)
        for i, pid in enumerate(ids):
            curpos[pid] = p0 + i
            seq_rec.append(pid)
        p1 = p0 + len(ids) - 1
        if order is not None:
            assert order[p0:p1 + 1] == ids, (order[p0:p1 + 1], ids)
            base = min([p0] + list(held))
            lim = min(max(p1, base + RING - 1), len(order) - 1)
        else:
            base = min([p0] + list(held))
            lim = p1
        assert p1 - base < RING, (p1, base)
        while nload[0] <= lim:
            g = nload[0]
            pid = order[g] if order is not None else seq_rec[g]
            dma("sp", f"ring{g % RING}", rv_flat[g % RING][:, :], tape16[pid])
            nload[0] += 1

    def slot(tile, piece):
        return curpos[piece] % RING

    dma("act", "c0", ident[:, :], ident_d)
    dma("act", "c1", ones[:, :], ones_d)
    dma("act", "c2", dist[:, :], dist_d)
    dma("act", "c3", pcol[:, :], pcol_d)
    dma("act", "c0", sinkb[:, :], sinkb_d)
    gst = cc
    dma("act", "c1", gst[:, 0:2, :].rearrange("p a b -> p (a b)"), gates_d)
    P.op("dve", lambda e: e.tensor_copy(gatesb[:, :, :].rearrange("p a b -> p (a b)"),
                                        gst[:, 0:2, :].rearrange("p a b -> p (a b)")),
         [gst[:, 0:2, :]], [gatesb[:, :, :]])
    ts_op("dve", negsink[:, :], sinkb[:, :], -1.0, None, ALU.mult)
    cp("dve", identb[:, :], ident[:, :])
    lam = pcol[:, C_LAM:C_LAM + 4]
    s_abs, s_y, s_z, s_z2, s_p, s_m = (sm[:, 4 * i:4 * i + 4] for i in range(6))
    ts_op("dve", s_m, lam, -1.0, None, ALU.mult)
    tt_op("dve", s_abs, lam, s_m, ALU.max)
    act(s_y, s_abs, AF.Exp, scale=-1.0)
    ts_op("dve", s_z, s_y, 2.0, None, ALU.add)
    P.op("dve", lambda e: e.reciprocal(s_z, s_z), [s_z], [s_z])
    tt_op("dve", s_z, s_z, s_y, ALU.mult)
    tt_op("dve", s_z2, s_z, s_z, ALU.mult)
    ts_op("dve", s_p, s_z2, 1.0 / 9.0, 1.0 / 7.0, ALU.mult, ALU.add)
    for cst in (1.0 / 5.0, 1.0 / 3.0, 1.0):
        tt_op("dve", s_p, s_p, s_z2, ALU.mult)
        ts_op("dve", s_p, s_p, cst, None, ALU.add)
    tt_op("dve", s_p, s_p, s_z, ALU.mult)
    ts_op("dve", s_m, s_m, 0.0, None, ALU.max)
    stt(s_p, s_p, 2.0, s_m, ALU.mult, ALU.add)
    ts_op("dve", cpv[:, 0:4], s_p, -8.0, None, ALU.mult)
    ts_op("dve", cpv[:, 4:8], s_p, -16.0, None, ALU.mult)

    final_ops = []

    def chk(k):
        pass

    def proj_ln(tile, N, nk, src, wrhs, ln_idx, ydst, res=None, tail_jobs=()):
        PT = min(N, 128)
        NB = (N + 127) // 128
        if res is None:
            res = xt
        dma("pool", "lnt", lnt[:, :], lntab_d[ln_idx])
        pbs = {}

        def mmphase(nb):
            pb = [nbank(), nbank()]
            for half in range(2):
                for k in range(nk):
                    mm(pb[half][0:PT, :], src(k)[:, nb * 128:nb * 128 + PT], wrhs(k, half), k == 0, k == nk - 1)
            pbs[nb] = pb

        def post_a(nb):
            pb = pbs[nb]
            for half in range(2):
                xs_ = xt[0:PT, nb, half * 512:(half + 1) * 512]
                rs_ = res[0:PT, nb, half * 512:(half + 1) * 512]
                stt(xs_, rs_, ALPHA, pb[half][0:PT, :], ALU.mult, ALU.add)

        def post_s(nb, do_a=True):
            if do_a:
                post_a(nb)
            so = 64 + 16 * (nb % 2)
            st = sm[0:PT, so:so + 12]
            mv = sm[0:PT, so + 12:so + 14]
            rs = sm[0:PT, so + 14:so + 15]
            nmr = sm[0:PT, so + 15:so + 16]
            for half in range(2):
                o_ = sm[0:PT, so + 6 * half:so + 6 + 6 * half]
                i_ = xt[0:PT, nb, half * 512:(half + 1) * 512]
                P.op("dve", lambda e, o_=o_, i_=i_: e.bn_stats(o_, i_), [i_], [o_])
            P.op("dve", lambda e: e.bn_aggr(mv, st), [st], [mv])
            act(rs, sm[0:PT, so + 13:so + 14], AF.Sqrt, bias=epsb[0:PT, :])
            P.op("dve", lambda e: e.reciprocal(rs, rs), [rs], [rs])
            stt(nmr, sm[0:PT, so + 12:so + 13], -1.0, rs, ALU.mult, ALU.mult)
            row = xt[0:PT, nb, :]
            if ydst is None:
                xb_ = xnb[nb % 2]
                act(xb_[0:PT, :], row, AF.Identity, bias=nmr, scale=rs)
                ts_op("pool", row, row, rs, nmr, ALU.mult, ALU.add)
                tt_op("pool", row, row, lnt[0:PT, 0:D], ALU.mult)
                tt_op("pool", row, row, lnt[0:PT, D:2 * D], ALU.add)
            else:
                act(row, row, AF.Identity, bias=nmr, scale=rs)
                tt_op("dve", row, row, lnt[0:PT, 0:D], ALU.mult)
                tt_op("pool", row, row, lnt[0:PT, D:2 * D], ALU.add)
                final_ops.append(dma("pool", "yout", ydst[nb * 128:nb * 128 + PT, :], row))

        def post_t(nb):
            if ydst is not None:
                return
            xb_ = xnb[nb % 2]
            gcol = pcol[:, C_LNG + 8 * ln_idx:C_LNG + 8 * ln_idx + 8]
            bcol = pcol[:, C_LNB + 8 * ln_idx:C_LNB + 8 * ln_idx + 8]
            tbb = nbank()[:, :].bitcast(BF16)
            for kc in range(8):
                trb(tbb[:, kc * 128:kc * 128 + PT], xb_[0:PT, kc * 128:(kc + 1) * 128], PT)
            for kc in range(8):
                ts_op("dve", xT[:, kc, nb * 128:nb * 128 + PT], tbb[:, kc * 128:kc * 128 + PT],
                      gcol[:, kc:kc + 1], bcol[:, kc:kc + 1], ALU.mult, ALU.add)

        for nb in range(NB):
            mmphase(nb)
            if nb > 0:
                post_s(nb - 1)
            if nb > 1:
                post_t(nb - 2)
        post_a(NB - 1)
        if NB > 1:
            post_t(NB - 2)
        for job in tail_jobs:
            job()
        post_s(NB - 1, do_a=False)
        post_t(NB - 1)

    def ffn_partA(tile, N, layer, holder):
        pg = PC_GU0 if layer == 0 else PC_GU1
        use_pieces(tile, pg, pg)
        held.add(curpos[pg])
        V = rv_gu[slot(tile, pg)]
        pre = {}
        for sub in range(2):
            pre[sub] = (nbank(), nbank())
        for sub in range(2):
            for t_ in range(2):
                for kc in range(8):
                    mm(pre[sub][t_][:, 0:384], V[:, t_, kc, sub * 128:(sub + 1) * 128], xT[:, kc, 0:384], kc == 0, kc == 7)
        holder["pre"] = pre

    def ffn(tile, N, layer, ydst, side=(), holder=None, tail_jobs=()):
        pg = PC_GU0 if layer == 0 else PC_GU1
        pd = PC_WD0 if layer == 0 else PC_WD1
        side = list(side)
        iters = 22
        for jj in range(11):
            pre = {}
            if jj == 0 and holder is not None and "pre" in holder:
                pre = holder["pre"]
                gpos = curpos[pg]
                held.discard(gpos)
                V = rv_gu[slot(tile, pg)]
                for sub in range(2):
                    for t_ in range(2):
                        for kc in range(8):
                            mm(pre[sub][t_][:, 384:512], V[:, t_, kc, sub * 128:(sub + 1) * 128], xT[:, kc, 384:512],
                               kc == 0, kc == 7)
            else:
                use_pieces(tile, pg + jj, pg + jj)
                gpos = curpos[pg + jj]
                V = rv_gu[slot(tile, pg + jj)]
            for sub in range(2):
                j = jj * 2 + sub
                if sub in pre:
                    bg, bu = pre[sub]
                else:
                    bg, bu = nbank(), nbank()
                    for kc in range(8):
                        mm(bg[:, 0:N], V[:, 0, kc, sub * 128:(sub + 1) * 128], xT[:, kc, 0:N], kc == 0, kc == 7)
                    for kc in range(8):
                        mm(bu[:, 0:N], V[:, 1, kc, sub * 128:(sub + 1) * 128], xT[:, kc, 0:N], kc == 0, kc == 7)
                s_ = sgt[j % 2]
                act(s_[:, 0:N], bg[:, 0:N], AF.Silu)
                tt_op("dve", hT[:, j, 0:N], s_[:, 0:N], bu[:, 0:N], ALU.mult)
                if side and not (pre and sub == 0):
                    held.add(gpos)
                    n = -(-len(side) // (iters - j))
                    for _ in range(n):
                        side.pop(0)()
                    held.discard(gpos)
        for job in side:
            job()
        use_pieces(tile, pd, pd + 5)
        proj_ln(tile, N, NJ, lambda j: hT[:, j, :],
                lambda j, half: rv_wd[slot(tile, pd + j // 4)][:, j % 4, half * 512:(half + 1) * 512],
                2 * layer + 1, ydst, tail_jobs=tail_jobs)

    def stageA_jobs(tile, N, is_last, pre=()):
        jobs = list(pre)
        jobs.append(lambda: load_x_tr(N))

        def j_xr(c):
            if c == 0:
                use_pieces(tile, PC_WIN, PC_WIN)
                held.add(curpos[PC_WIN])
            V = rv_8x512[slot(tile, PC_WIN)]
            b = nbank()
            for kc in range(8):
                mm(b[:, 0:N], V[:, kc, c * 128:(c + 1) * 128], xTa[:, kc, 0:N], kc == 0, kc == 7)
            act(xr_buf[:, c, 3:3 + N], b[:, 0:N], AF.Copy)
            if c == 3:
                held.discard(curpos[PC_WIN])

        def j_g(c):
            if c == 0:
                use_pieces(tile, PC_WIN + 1, PC_WIN + 2)
                held.add(curpos[PC_WIN + 1])
                held.add(curpos[PC_WIN + 2])
            Vg = rv_8x512[slot(tile, PC_WIN + 1)]
            Vv = rv_8x512[slot(tile, PC_WIN + 2)]
            b1, b2 = nbank(), nbank()
            for kc in range(8):
                mm(b1[:, 0:N], Vg[:, kc, c * 128:(c + 1) * 128], xTa[:, kc, 0:N], kc == 0, kc == 7)
            for kc in range(8):
                mm(b2[:, 0:N], Vv[:, kc, c * 128:(c + 1) * 128], xTa[:, kc, 0:N], kc == 0, kc == 7)
            act(sg[:, 0:N], b1[:, 0:N], AF.Sigmoid)
            tt_op("dve", g_buf[:, c, 30:30 + N], b2[:, 0:N], sg[:, 0:N], ALU.mult)
            if is_last:
                tt_op("dve", g32[:, c, :], b2[:, N - 30:N], sg[:, N - 30:N], ALU.mult)
            if c == 3:
                held.discard(curpos[PC_WIN + 1])
                held.discard(curpos[PC_WIN + 2])

        def j_yr(c):
            if c == 0:
                use_pieces(tile, PC_WIN + 3, PC_WIN + 3)
                held.add(curpos[PC_WIN + 3])
            V = rv_8x512[slot(tile, PC_WIN + 3)]
            b = nbank()
            for kc in range(8):
                mm(b[:, 0:N], V[:, kc, c * 128:(c + 1) * 128], xTa[:, kc, 0:N], kc == 0, kc == 7)
            act(gy[:, c, 0:N], b[:, 0:N], AF.Gelu_apprx_tanh)
            if c == 3:
                held.discard(curpos[PC_WIN + 3])

        for c in range(4):
            jobs.append(lambda c=c: j_xr(c))
        for c in range(4):
            jobs.append(lambda c=c: j_g(c))
        for c in range(4):
            jobs.append(lambda c=c: j_yr(c))

        def rec1(c):
            xc, rr, ii, a2, xcb = xc2[c % 2], rr2[c % 2], ii2[c % 2], a22[c % 2], xcb2[c % 2]
            w = lambda k: pcol[:, C_RCW + 4 * c + k:C_RCW + 4 * c + k + 1]
            ts_op("dve", xc[:, 0:N], xr_buf[:, c, 0:N], w(0), pcol[:, C_RCB + c:C_RCB + c + 1], ALU.mult, ALU.add)
            for k in range(1, 4):
                stt(xc[:, 0:N], xr_buf[:, c, k:k + N], w(k), xc[:, 0:N], ALU.mult, ALU.add)
            cp("pool", xr_buf[:, c, 0:3], xr_buf[:, c, N:N + 3])
            act(xcb[:, 0:N], xc[:, 0:N], AF.Copy)

        def rec1b(c):
            xc, rr, ii, a2, xcb = xc2[c % 2], rr2[c % 2], ii2[c % 2], a22[c % 2], xcb2[c % 2]
            b1, b2 = nbank(), nbank()
            mm(b1[:, 0:N], gatesb[:, c, :], xcb[:, 0:N], True, True)
            mm(b2[:, 0:N], gatesb[:, 4 + c, :], xcb[:, 0:N], True, True)
            act(rr[:, 0:N], b1[:, 0:N], AF.Sigmoid, bias=pcol[:, C_GAB + c:C_GAB + c + 1])
            act(ii[:, 0:N], b2[:, 0:N], AF.Sigmoid, bias=pcol[:, C_GXB + c:C_GXB + c + 1])
            act(a2[:, 0:N], rr[:, 0:N], AF.Exp, scale=cpv[:, 4 + c:5 + c])
            act(rr[:, 0:N], rr[:, 0:N], AF.Exp, scale=cpv[:, c:c + 1])
            ts_op("pool", a2[:, 0:N], a2[:, 0:N], 1.0, 0.0, ALU.min, ALU.max)
            act(a2[:, 0:N], a2[:, 0:N], AF.Sqrt, bias=oneb[:, :], scale=-1.0)

        def rec2(c):
            xc, rr, ii, a2 = xc2[c % 2], rr2[c % 2], ii2[c % 2], a22[c % 2]
            tt_op("dve", a2[:, 0:N], a2[:, 0:N], ii[:, 0:N], ALU.mult)
            tt_op("dve", a2[:, 0:N], a2[:, 0:N], xc[:, 0:N], ALU.mult)
            hi_, da_, du_, h0_ = ii[:, 0:N], rr[:, 0:N], a2[:, 0:N], hstate[:, c:c + 1]
            P.op("dve", lambda e, hi_=hi_, da_=da_, du_=du_, h0_=h0_: e.tensor_tensor_scan(hi_, da_, du_, h0_, ALU.mult, ALU.add),
                 [da_, du_, h0_], [hi_])
            cp("dve", hstate[:, c:c + 1], ii[:, N - 1:N])
            tt_op("pool", ro[:, c, 0:N], ii[:, 0:N], gy[:, c, 0:N], ALU.mult)

        s1, s2 = ps[6], ps[7]

        def convpe(c):
            use_pieces(tile, PC_CONV + c, PC_CONV + c)
            V = rv_conv[slot(tile, PC_CONV + c)]
            b = nbank()
            for k in range(31):
                mm(b[:, 0:N], V[:, k, :], g_buf[:, c, k:k + N], k == 0, k == 30)
            cp("pool", g_buf[:, c, 0:30], g_buf[:, c, N:N + 30])
            if c > 0:
                stats(c - 1)
            act(cc[:, c, 0:N], b[:, 0:N], AF.Identity, bias=pcol[:, C_CFB + c:C_CFB + c + 1])
            act(sq2[c % 2][:, 0:N], cc[:, c, 0:N], AF.Square)

        def stats(c):
            mm(s1[:, 0:N], ones[:, :], cc[:, c, 0:N], c == 0, c == 3)
            mm(s2[:, 0:N], ones[:, :], sq2[c % 2][:, 0:N], c == 0, c == 3)

        for f, c in ((rec1, 0), (convpe, 0), (rec1b, 0), (rec1, 1), (rec2, 0), (convpe, 1), (rec1b, 1), (rec1, 2),
                     (rec2, 1), (convpe, 2), (rec1b, 2), (rec1, 3), (rec2, 2), (convpe, 3), (rec1b, 3), (rec2, 3)):
            jobs.append(lambda f=f, c=c: f(c))

        def j_cln():
            stats(3)
            ts_op("dve", mean[:, 0:N], s1[:, 0:N], 1.0 / 512.0, None, ALU.mult)
            tt_op("dve", var[:, 0:N], mean[:, 0:N], mean[:, 0:N], ALU.mult)
            stt(var[:, 0:N], s2[:, 0:N], 1.0 / 512.0, var[:, 0:N], ALU.mult, ALU.subtract)
            act(var[:, 0:N], var[:, 0:N], AF.Sqrt, bias=epsb[:, :])
            P.op("dve", lambda e: e.reciprocal(var[:, 0:N], var[:, 0:N]), [var[:, 0:N]], [var[:, 0:N]])
            for c in range(4):
                tt_op("pool", tt[:, 0:N], cc[:, c, 0:N], mean[:, 0:N], ALU.subtract)
                tt_op("dve", tt[:, 0:N], tt[:, 0:N], var[:, 0:N], ALU.mult)
                act(cn[:, c, 0:N], tt[:, 0:N], AF.Silu, bias=pcol[:, C_CFBE + c:C_CFBE + c + 1],
                    scale=pcol[:, C_CFG + c:C_CFG + c + 1])
        jobs.append(j_cln)
        return jobs

    def projA(tile, N, is_last, oi, tail_jobs=()):
        if is_last:
            b = nbank()
            tr(b[0:4, 0:128], hstate[:, 0:4], 128)
            cp("dve", stg[0:4, 0:128], b[0:4, 0:128])
            final_ops.append(dma("act", "so0", oh_d[oi], stg[0:4, 0:128]))
            b = nbank()
            for c in range(4):
                tr(b[0:3, c * 128:(c + 1) * 128], xr_buf[:, c, 0:3], 128)
            cp("dve", sm_rc[0:3, :], b[0:3, :])
            final_ops.append(dma("act", "so1", orc_d[oi], sm_rc[0:3, :]))
            b = nbank()
            for c in range(4):
                tr(b[0:30, c * 128:(c + 1) * 128], g32[:, c, 0:30], 128)
            cp("dve", sm_cf[0:30, :], b[0:30, :])
            final_ops.append(dma("act", "so2", ocf_d[oi], sm_cf[0:30, :]))
        use_pieces(tile, PC_WOUT, PC_WOUT + 1)
        proj_ln(tile, N, 8, lambda k: (ro[:, k, :] if k < 4 else cn[:, k - 4, :]),
                lambda k, half: rv_8x512[slot(tile, PC_WOUT + half)][:, k, :], 0, None, res=xin, tail_jobs=tail_jobs)

    def q_partA(tile, N, holder):
        use_pieces(tile, PC_Q, PC_Q)
        held.add(curpos[PC_Q])
        V = rv_8x512[slot(tile, PC_Q)]
        pre = {}
        for c4 in range(4):
            pre[c4] = nbank()
        for c4 in range(4):
            for kc in range(8):
                mm(pre[c4][:, 0:384], V[:, kc, c4 * 128:(c4 + 1) * 128], xT[:, kc, 0:384], kc == 0, kc == 7)
        holder["pre"] = pre

    def mixer_c(tile, N, is_first, is_last, oi, tail_jobs=(), holder=None):
        PT = min(N, 128)
        NB = (N + 127) // 128
        for hq in range(2):
            pre = {}
            if hq == 0 and holder is not None and "pre" in holder:
                pre = holder["pre"]
                held.discard(curpos[PC_Q])
                V = rv_8x512[slot(tile, PC_Q)]
                for c4 in range(4):
                    for kc in range(8):
                        mm(pre[c4][:, 384:512], V[:, kc, c4 * 128:(c4 + 1) * 128], xT[:, kc, 384:512], kc == 0, kc == 7)
            else:
                use_pieces(tile, PC_Q + hq, PC_Q + hq)
                V = rv_8x512[slot(tile, PC_Q + hq)]
            for c4 in range(4):
                oc = hq * 4 + c4
                if c4 in pre:
                    b = pre[c4]
                else:
                    b = nbank()
                    for kc in range(8):
                        mm(b[:, 0:N], V[:, kc, c4 * 128:(c4 + 1) * 128], xT[:, kc, 0:N], kc == 0, kc == 7)
                act(qT[:, oc, 0:N], b[:, 0:N], AF.Identity, scale=0.125)
        chk(3.01)
        use_pieces(tile, PC_KDUP, PC_KDUP)
        V = rv_8x512[slot(tile, PC_KDUP)]
        for j in range(4):
            b = nbank()
            for kc in range(8):
                mm(b[:, 0:N], V[:, kc, j * 128:(j + 1) * 128], xT[:, kc, 0:N], kc == 0, kc == 7)
            cp("dve", kT[:, j, 128:128 + N], b[:, 0:N])
        chk(3.02)
        use_pieces(tile, PC_KV, PC_KV)
        V = rv_8x512[slot(tile, PC_KV)]
        for nb in range(NB):
            b = nbank()
            need_k = is_last and nb == NB - 1
            c0 = 0 if need_k else 256
            for kc in range(8):
                mm(b[0:PT, c0:512], xT[:, kc, nb * 128:nb * 128 + PT], V[:, kc, c0:512], kc == 0, kc == 7)
            act(vbuf[0:PT, 1 + nb, :], b[0:PT, 256:512], AF.Copy)
            if nb == 1:
                chk(3.03)
            if nb == NB - 1:
                chk(3.04)
            if is_last and nb == NB - 1:
                cp("dve", kvo[0:PT, :], b[0:PT, :])
                chk(3.05)
                r0 = 128 - PT
                final_ops.append(dma("act", "so3", ok_d[oi][r0:128, :], kvo[0:PT, 0:256]))
                final_ops.append(dma("act", "so4", ov_d[oi][r0:128, :], kvo[0:PT, 256:512]))
        chk(3.1)
        QN = PT
        units = [(qb, hg) for qb in range(NB) for hg in range(2)]
        info = {}

        def s_phase(ui):
            qb, hg = units[ui]
            par = ui % 2
            blocks = [(qb * 128, 128, qb, 0), (qb * 128 + 128, QN, qb + 1, 128)]
            if is_first and qb == 0:
                blocks = blocks[1:]
            kstart = blocks[0][0]
            d0 = blocks[0][3]
            nk = sum(bk[1] for bk in blocks)
            for hp in range(4):
                bb = [nbank(), nbank()]
                for hh in range(2):
                    h = hg * 8 + hp * 2 + hh
                    oc, half, kv = h // 2, h % 2, h // 4
                    pr = slice(half * 64, half * 64 + 64)
                    mm(bb[hh][0:QN, 0:nk], qT[pr, oc, qb * 128:qb * 128 + QN],
                       kT[pr, kv, kstart:kstart + nk], True, True)
                for hh in range(2):
                    h = hg * 8 + hp * 2 + hh
                    stt(sbb[0:QN, hp * 2 + hh, 0:nk], dist[0:QN, d0:d0 + nk], -SLOPES[h],
                        bb[hh][0:QN, 0:nk], ALU.mult, ALU.add)
            mx = smA[0:QN, par, 0:8]
            negm = smA[0:QN, par, 8:16]
            se = smA[0:QN, par, 16:24]
            rsum = smA[0:QN, par, 24:32]
            sin_ = sbb[0:QN, :, 0:nk]
            P.op("dve", lambda e, sin_=sin_, mx=mx: e.tensor_reduce(mx, sin_, AX.X, ALU.max), [sin_], [mx])
            stt(negm, mx, -1.0, negsink[0:QN, hg * 8:hg * 8 + 8], ALU.mult, ALU.min)
            tt_op("dve", se, negm, sinkb[0:QN, hg * 8:hg * 8 + 8], ALU.add)
            info[ui] = (blocks, kstart, nk)

        def e_phase(ui):
            qb, hg = units[ui]
            par = ui % 2
            blocks, kstart, nk = info[ui]
            se = smA[0:QN, par, 16:24]
            act(se, se, AF.Exp)
            for hl in range(8):
                act(pbf[par][0:QN, hl, 0:nk], sbb[0:QN, hl, 0:nk], AF.Exp, bias=smA[0:QN, par, 8 + hl:9 + hl],
                    accum=smA[0:QN, par, 24 + hl:25 + hl])

        def pe_phase(ui):
            qb, hg = units[ui]
            par = ui % 2
            blocks, kstart, nk = info[ui]
            ob = ps[6 + par]
            se = smA[0:QN, par, 16:24]
            tt_op("dve", se, se, smA[0:QN, par, 24:32], ALU.add)
            P.op("dve", lambda e, se=se: e.reciprocal(se, se), [se], [se])
            allfull = all(bk[1] == 128 for bk in blocks) and QN == 128
            def tphase(hl):
                pT_ = pT[hl % 2]
                tbb = nbank()[:, :].bitcast(BF16)
                for bi, (kcol, kn, vblk, dcol) in enumerate(blocks):
                    off = kcol - kstart
                    trb(tbb[0:kn, bi * 128:bi * 128 + QN], pbf[par][0:QN, hl, off:off + kn], QN)
                ev = "act"
                if allfull:
                    nb_ = len(blocks)
                    cp(ev, pT_[:, 0:nb_, :], tbb[:, 0:nb_ * 128].rearrange("p (b q) -> p b q", q=128))
                else:
                    for bi, (kcol, kn, vblk, dcol) in enumerate(blocks):
                        cp(ev, pT_[0:kn, bi, 0:QN], tbb[0:kn, bi * 128:bi * 128 + QN])

            def pvphase(hl):
                h = hg * 8 + hl
                kv = h // 4
                pT_ = pT[hl % 2]
                for bi, (kcol, kn, vblk, dcol) in enumerate(blocks):
                    mm(ob[0:QN, hl * 64:(hl + 1) * 64], pT_[0:kn, bi, 0:QN], vbuf[0:kn, vblk, kv * 64:kv * 64 + 64],
                       bi == 0, bi == len(blocks) - 1)

            tphase(0)
            for hl in range(1, 8):
                tphase(hl)
                pvphase(hl - 1)
            pvphase(7)
            ot = otok[qb % 2]
            rden = smA[0:QN, par, 16:24].unsqueeze(2).broadcast_to([QN, 8, 64])
            tt_op("dve", ot[0:QN, hg * 512:(hg + 1) * 512].rearrange("p (h d) -> p h d", d=64),
                  ob[0:QN, :].rearrange("p (h d) -> p h d", d=64), rden, ALU.mult)
            chk(3.5)
            if hg == 1:
                tbb = nbank()[:, :].bitcast(BF16)
                for oc in range(8):
                    trb(tbb[:, oc * 128:oc * 128 + QN], ot[0:QN, oc * 128:(oc + 1) * 128], QN)
                cp("dve", oT[:, :, qb * 128:qb * 128 + QN], tbb[:, :].rearrange("p (c q) -> p c q", q=128)[:, :, 0:QN])

        s_phase(0)
        e_phase(0)
        for ui in range(1, len(units)):
            s_phase(ui)
            pe_phase(ui - 1)
            e_phase(ui)
        pe_phase(len(units) - 1)
        if not is_last:
            cp("pool", kT[:, :, 0:128], kT[:, :, N:N + 128])
            cp("pool", vbuf[:, 0, :], vbuf[:, NB, :])
        use_pieces(tile, PC_WOC, PC_WOC + 1)
        proj_ln(tile, N, 8, lambda k: oT[:, k, :],
                lambda k, half: rv_8x512[slot(tile, PC_WOC + half)][:, k, :], 2, None, tail_jobs=tail_jobs)

    epsb = sb("epsb", [128, 1], F32)[0]
    oneb = sb("oneb", [128, 1], F32)[0]
    sm2 = sb("sm2", [128, 8], F32)[0]
    assert cur[0] <= 229344, cur[0]
    P.op("pool", lambda e: e.memset(epsb[:, :], LN_EPS), [], [epsb[:, :]])
    P.op("pool", lambda e: e.memset(oneb[:, :], 1.0), [], [oneb[:, :]])

    def load_x_dma(src, N):
        PT = min(N, 128)
        NB = (N + 127) // 128
        if NB > 1:
            dma("act", "xin", xin[:, 0:NB, :], src.rearrange("(nb p) d -> p nb d", p=128))
        else:
            dma("act", "xin", xin[0:PT, 0, :], src)

    def load_x_tr(N):
        PT = min(N, 128)
        NB = (N + 127) // 128
        for nb in range(NB):
            xb_ = xnb[nb % 2]
            cp("act" if nb % 2 else "dve", xb_[0:PT, :], xin[0:PT, nb, :])
            tbb = nbank()[:, :].bitcast(BF16)
            for kc in range(8):
                trb(tbb[:, kc * 128:kc * 128 + PT], xb_[0:PT, kc * 128:(kc + 1) * 128], PT)
            cp("dve" if nb % 2 else "act", xTa[:, :, nb * 128:nb * 128 + PT],
               tbb[:, :].rearrange("p (c q) -> p c q", q=128)[:, :, 0:PT])

    P.op("pool", lambda e: e.memset(xr_buf[:, :, 0:3], 0.0), [], [xr_buf[:, :, 0:3]])
    P.op("pool", lambda e: e.memset(g_buf[:, :, 0:30], 0.0), [], [g_buf[:, :, 0:30]])
    P.op("pool", lambda e: e.memset(hstate[:, :], 0.0), [], [hstate[:, :]])
    P.op("pool", lambda e: e.memset(kT[:, :, 0:128], 0.0), [], [kT[:, :, 0:128]])
    P.op("pool", lambda e: e.memset(vbuf[:, 0, :], 0.0), [], [vbuf[:, 0, :]])

    def sample_init():
        dma("act", "c2", xr_buf[:, :, 0:3], st_rc_d.rearrange("p (c k) -> p c k", k=3))
        dma("act", "c3", hstate[:, :], st_h_d)
        dma("act", "c0", cc[:, 0, 0:120], st_cf_d)
        cp("dve", g_buf[:, :, 0:30], cc[:, 0, 0:120].rearrange("p (c k) -> p c k", k=30))
        dma("act", "c1", cc[:, 1, :], st_kT_d)
        cp("dve", kT[:, :, 0:128], cc[:, 1, :].rearrange("p (c k) -> p c k", k=128))
        dma("act", "c2", cc[:, 2, 0:256], st_v_d)
        cp("dve", vbuf[:, 0, :], cc[:, 2, 0:256])
        final_ops.append(dma("act", "so5", ok_d[1][0:64, :], ck_d[64:128, :]))
        final_ops.append(dma("act", "so6", ov_d[1][0:64, :], cv_d[64:128, :]))

    tiles = [(t, 512, 0) for t in range(n_tiles)] + ([(n_tiles, DEC, 1)] if with_sample else [])
    load_x_dma(x_d[0:512, :], 512)
    for job in stageA_jobs(0, 512, n_tiles == 1):
        job()
    for idx, (t, N, oi) in enumerate(tiles):
        is_sample = oi == 1
        last = (t == n_tiles - 1) or is_sample
        h0, hq_, h1 = {}, {}, {}
        split = N == 512
        projA(t, N, last, oi, tail_jobs=[lambda: ffn_partA(t, N, 0, h0)] if split else [])
        nxt = tiles[idx + 1] if idx + 1 < len(tiles) else None
        if nxt is not None:
            nt, nN, noi = nxt
            load_x_dma(xs_d if noi == 1 else x_d[nt * 512:(nt + 1) * 512, :], nN)
        ffn(t, N, 0, None, holder=h0, tail_jobs=[lambda: q_partA(t, N, hq_)] if split else [])
        side = []
        if nxt is not None:
            nt, nN, noi = nxt
            nlast = (nt == n_tiles - 1) or noi == 1
            side = stageA_jobs(nt, nN, nlast, pre=[sample_init] if noi == 1 else [])
        npre = min(len(side), 6 if (nxt is not None and nxt[2] == 1) else 5)
        tj = side[:npre] + ([lambda: ffn_partA(t, N, 1, h1)] if split else [])
        mixer_c(t, N, (t == 0 and not is_sample), last, oi, tail_jobs=tj, holder=hq_)
        ffn(t, N, 1, ys_d if is_sample else y_d[t * 512:(t + 1) * 512, :], side=side[npre:], holder=h1)

    P.wait_all("sp", final_ops + list(P.lastdma.values()))
    P.wait_all("act", final_ops)
    if order is not None:
        P.emit()
    return nc, seq_rec


def build_program(n_tiles=SEQ // 512, with_sample=True, seq=SEQ):
    _, order = _build(n_tiles, with_sample, seq, None)
    nc, order2 = _build(n_tiles, with_sample, seq, list(order))
    assert order2 == order
    return nc


def _pieces(inp):
    f = np.float32
    tape = np.zeros((NPIECE, 128, PIECE), f)

    def kmajor(w, c0, ncol):
        return w[:, c0:c0 + ncol].reshape(8, 128, ncol).transpose(1, 0, 2)

    w_in = inp["w_in_ab"][0]
    for g, c0 in enumerate((0, 1536, 1024, 512)):
        tape[PC_WIN + g] = kmajor(w_in, c0, 512).reshape(128, PIECE)
    cw = inp["cf_conv_w"][0]
    for c in range(4):
        pc = np.zeros((128, 32, 128), f)
        idx = np.arange(128)
        for k in range(31):
            pc[idx, k, idx] = cw[k, c * 128:(c + 1) * 128]
        tape[PC_CONV + c] = pc.reshape(128, PIECE)
    wo = inp["w_out_ab"][0]
    for h in range(2):
        tape[PC_WOUT + h] = kmajor(wo, h * 512, 512).reshape(128, PIECE)
    for layer, (pg, pd) in enumerate(((PC_GU0, PC_WD0), (PC_GU1, PC_WD1))):
        wg, wu, wd = inp["w_ff_gate"][layer], inp["w_ff_up"][layer], inp["w_ff_down"][layer]
        for jj in range(11):
            pc = np.stack([kmajor(wg, jj * 256, 256), kmajor(wu, jj * 256, 256)], axis=1)
            tape[pg + jj] = pc.reshape(128, PIECE)
        wdk = wd.reshape(NJ, 128, D).transpose(1, 0, 2)
        for q in range(6):
            pc = np.zeros((128, 4, D), f)
            n = min(4, NJ - q * 4)
            pc[:, 0:n] = wdk[:, q * 4:q * 4 + n]
            tape[pd + q] = pc.reshape(128, PIECE)
    wqkv = inp["w_qkv"][0]
    for h in range(2):
        tape[PC_Q + h] = kmajor(wqkv, h * 512, 512).reshape(128, PIECE)
    wk = kmajor(wqkv, 1024, 256).reshape(128, 8, 4, 1, 64)
    tape[PC_KDUP] = np.broadcast_to(wk, (128, 8, 4, 2, 64)).reshape(128, PIECE)
    tape[PC_KV] = kmajor(wqkv, 1024, 512).reshape(128, PIECE)
    woc = inp["w_out_c"][0]
    for h in range(2):
        tape[PC_WOC + h] = kmajor(woc, h * 512, 512).reshape(128, PIECE)
    return tape


def _col(v):
    return np.ascontiguousarray(v.reshape(-1, 128).T)


def _shared_inputs(inp):
    f = np.float32
    sh = {}
    sh["tape32"] = _pieces(inp)
    sh["ident"] = np.eye(128, dtype=f)
    sh["ones"] = np.ones((128, 128), f)
    i = np.arange(128)[:, None]
    s = np.arange(256)[None, :]
    dist = np.abs(128 + i - s).astype(f)
    qc = i // 64
    kc = s // 64 - 2
    valid = (kc <= qc) & (kc >= qc - 2)
    dist = np.where(valid, dist, f(1e10)).astype(f)
    sh["dist"] = dist
    pcol = np.zeros((128, NCOL), f)
    rcw = inp["rec_conv_w"][0]
    for c in range(4):
        for k in range(4):
            pcol[:, C_RCW + 4 * c + k] = rcw[k, c * 128:(c + 1) * 128]
    for name, col in (("rec_conv_b", C_RCB), ("rec_gate_a_b", C_GAB), ("rec_gate_x_b", C_GXB), ("rec_lambda", C_LAM),
                      ("cf_conv_b", C_CFB), ("cf_norm_g", C_CFG), ("cf_norm_b", C_CFBE)):
        pcol[:, col:col + 4] = _col(inp[name][0])
    lng = [inp["ln_mix_g"][0], inp["ln_ff_g"][0], inp["ln_mix_g"][1], inp["ln_ff_g"][1]]
    lnb = [inp["ln_mix_b"][0], inp["ln_ff_b"][0], inp["ln_mix_b"][1], inp["ln_ff_b"][1]]
    lntab = np.zeros((4, 128, 2 * D), f)
    for l in range(4):
        pcol[:, C_LNG + 8 * l:C_LNG + 8 * l + 8] = _col(lng[l])
        pcol[:, C_LNB + 8 * l:C_LNB + 8 * l + 8] = _col(lnb[l])
        lntab[l, :, 0:D] = lng[l][None, :]
        lntab[l, :, D:] = lnb[l][None, :]
    sh["pcol"] = pcol
    sh["lntab"] = lntab
    gates = np.zeros((128, 8, 128), f)
    for t_, nm in enumerate(("rec_gate_a_w", "rec_gate_x_w")):
        w = inp[nm][0]
        for c in range(4):
            gates[0:64, 4 * t_ + c, 0:64] = w[2 * c]
            gates[64:128, 4 * t_ + c, 64:128] = w[2 * c + 1]
    sh["gates"] = gates.reshape(128, 1024)
    sh["sinkb"] = np.broadcast_to(inp["attn_sinks"][0][None, :], (128, NHEAD)).astype(f).copy()
    return sh


def _core_inputs(inp, b, seq):
    f = np.float32
    m = {}
    m["x"] = np.ascontiguousarray(inp["x_prompt"][b, :seq])
    m["xs"] = np.ascontiguousarray(inp["x_sample"][b])
    rc = inp["state_rec_conv"][0, b]
    m["st_rc"] = np.ascontiguousarray(rc.reshape(3, 4, 128).transpose(2, 1, 0)).reshape(128, 12)
    cf = inp["state_cf_conv"][0, b]
    m["st_cf"] = np.ascontiguousarray(cf.reshape(30, 4, 128).transpose(2, 1, 0)).reshape(128, 120)
    m["st_h"] = _col(inp["state_rec_h"][0, b])
    ck = inp["cache_k"][0, b]
    kTt = ck.transpose(1, 2, 0)
    kd = np.stack([kTt, kTt], axis=1)
    m["st_kT"] = np.ascontiguousarray(kd.reshape(4, 128, 128).transpose(1, 0, 2)).reshape(128, 512)
    m["st_v"] = np.ascontiguousarray(inp["cache_v"][0, b].reshape(128, 256))
    m["ck"] = np.ascontiguousarray(ck.reshape(128, 256))
    m["cv"] = np.ascontiguousarray(inp["cache_v"][0, b].reshape(128, 256))
    return {k: np.asarray(v, f) for k, v in m.items()}


_PROG_CACHE = {}


def run(inp, ncores=NCORE, seq=SEQ, with_sample=True):
    inp = {k: np.asarray(v) for k, v in inp.items()}
    key = (seq, with_sample)
    if key not in _PROG_CACHE:
        _PROG_CACHE[key] = build_program(seq // 512, with_sample, seq)
    nc = _PROG_CACHE[key]
    sh = _shared_inputs(inp)
    in_maps = []
    for b in range(ncores):
        m = dict(sh)
        m.update(_core_inputs(inp, b, seq))
        in_maps.append(m)
    res = run_bass_kernel_spmd(nc, in_maps, core_ids=list(range(ncores)))
    R = res.results
    f = np.float32

    def st(name, shape):
        return np.stack([np.asarray(R[b][name], f).reshape(shape) for b in range(ncores)])

    y = st("y", (seq, D))
    ys = st("ys", (DEC, D))
    outs = (y, ys,
            st("o_h_p", (512,))[None], st("o_h_s", (512,))[None],
            st("o_rc_p", (3, 512))[None], st("o_rc_s", (3, 512))[None],
            st("o_cf_p", (30, 512))[None], st("o_cf_s", (30, 512))[None],
            st("o_k_p", (128, 4, 64))[None], st("o_k_s", (128, 4, 64))[None],
            st("o_v_p", (128, 4, 64))[None], st("o_v_s", (128, 4, 64))[None])
    return outs


def kernel(**inputs):
    return run(inputs)
```
